# Optimizing a Trainium2 kernel written in Bass

```python
import jax, jax.numpy as jnp
from jax import lax
import numpy as np

D_MODEL = 1024
BATCH = 4
SEQ = 4096
DEPTH = 1

MEM_LEN = 256
D_MIX = D_MODEL
POOL_WINDOWS = (2, 4, 8, 16)
N_POOL_GROUPS = len(POOL_WINDOWS)
D_POOL = D_MIX // 4
POOL_GROUP_DIM = D_POOL // N_POOL_GROUPS
D_FOX = D_MIX - D_POOL
FOX_HEAD_DIM = 64
FOX_HEADS = D_FOX // FOX_HEAD_DIM
Q_BLOCK = 128
XA_HEADS = 4
XA_HEAD_DIM = D_MODEL // XA_HEADS
D_FF = 4 * D_MODEL
CONV_WIDTH = 3
NORM_EPS = 1e-6
D_IN = D_POOL + 3 * D_FOX + FOX_HEADS

kernel_name = 'hybrid_pool_fox_memxattn_convffn'


def rms_norm(x, g):
    xf = x.astype(jnp.float32)
    y = xf * lax.rsqrt(jnp.mean(xf * xf, axis=-1, keepdims=True) + NORM_EPS)
    return (y * g.astype(jnp.float32)).astype(x.dtype)


def pool_mixer(u, w_pool, pool_scale):
    b, s, _ = u.shape
    ug = u.astype(jnp.float32).reshape(b, s, N_POOL_GROUPS, POOL_GROUP_DIM)
    csum = jnp.pad(jnp.cumsum(ug, axis=1), ((0, 0), (1, 0), (0, 0), (0, 0)))
    t1 = jnp.arange(1, s + 1, dtype=jnp.float32)
    pooled = []
    for g, w in enumerate(POOL_WINDOWS):
        c = csum[:, :, g]
        lower = jnp.concatenate([jnp.zeros((b, w - 1, POOL_GROUP_DIM), c.dtype), c[:, :s + 1 - w]], axis=1)
        win_sum = c[:, 1:] - lower
        count = jnp.minimum(t1, float(w))[None, :, None]
        pooled.append(win_sum / count)
    pooled = jnp.stack(pooled, axis=2)
    diff = (pooled - ug).astype(u.dtype)
    mixed = jnp.einsum('bsgc,gcd->bsgd', diff, w_pool)
    return (mixed * pool_scale).reshape(b, s, D_POOL)


def forgetting_attention(q, k, v, log_f):
    b, s, h, dh = q.shape
    cum = jnp.cumsum(log_f, axis=1).transpose(0, 2, 1)
    scale = dh ** -0.5
    outs = []
    for i in range(s // Q_BLOCK):
        q0, q1 = i * Q_BLOCK, (i + 1) * Q_BLOCK
        qb, kb, vb = q[:, q0:q1], k[:, :q1], v[:, :q1]
        logits = jnp.einsum('bqhd,bkhd->bhqk', qb, kb).astype(jnp.float32) * scale
        decay = cum[:, :, q0:q1, None] - cum[:, :, None, :q1]
        mask = (q0 + jnp.arange(Q_BLOCK))[:, None] >= jnp.arange(q1)[None, :]
        logits = jnp.where(mask, logits + decay, -jnp.inf)
        probs = jax.nn.softmax(logits, axis=-1)
        outs.append(jnp.einsum('bhqk,bkhd->bqhd', probs.astype(v.dtype), vb))
    return jnp.concatenate(outs, axis=1)


def memory_cross_attention(h, mem_n, w_xq, w_xkv, w_xo):
    b, s, _ = h.shape
    m = mem_n.shape[1]
    q = (h @ w_xq).reshape(b, s, XA_HEADS, XA_HEAD_DIM)
    kv = mem_n @ w_xkv
    k = kv[..., :D_MODEL].reshape(b, m, XA_HEADS, XA_HEAD_DIM)
    v = kv[..., D_MODEL:].reshape(b, m, XA_HEADS, XA_HEAD_DIM)
    logits = jnp.einsum('bshd,bmhd->bhsm', q, k).astype(jnp.float32) * (XA_HEAD_DIM ** -0.5)
    probs = jax.nn.softmax(logits, axis=-1)
    out = jnp.einsum('bhsm,bmhd->bshd', probs.astype(v.dtype), v).reshape(b, s, D_MODEL)
    return out @ w_xo


def conv_gated_mlp(h, w_up, conv_w, conv_b, w_down):
    hid = h @ w_up
    hid = lax.conv_general_dilated(
        hid, conv_w, window_strides=(1,), padding=[(CONV_WIDTH - 1, 0)],
        dimension_numbers=('NWC', 'WIO', 'NWC'), feature_group_count=2 * D_FF) + conv_b
    gate, up = hid[..., :D_FF], hid[..., D_FF:]
    return (jax.nn.gelu(gate, approximate=True) * up) @ w_down


def setup_inputs(seed: int = 0) -> dict:
    key = jax.random.key(seed)
    ks = jax.random.split(key, 22)
    nrm = lambda k, shape, s: jax.random.normal(k, shape, jnp.float32) * s
    gain = lambda k: 1.0 + 0.1 * jax.random.normal(k, (DEPTH, D_MODEL), jnp.float32)
    return {
        'x': jax.random.normal(ks[0], (BATCH, SEQ, D_MODEL), jnp.float32),
        'mem': jax.random.normal(ks[1], (BATCH, MEM_LEN, D_MODEL), jnp.float32),
        'norm_mix_pre': gain(ks[2]),
        'norm_mix_post': gain(ks[3]),
        'w_in': nrm(ks[4], (DEPTH, D_MODEL, D_IN), D_MODEL ** -0.5),
        'b_forget': jax.random.uniform(ks[5], (DEPTH, FOX_HEADS), jnp.float32, 1.0, 6.0),
        'w_pool': nrm(ks[6], (DEPTH, N_POOL_GROUPS, POOL_GROUP_DIM, POOL_GROUP_DIM), POOL_GROUP_DIM ** -0.5),
        'pool_scale': 1.0 + 0.1 * jax.random.normal(ks[7], (DEPTH, N_POOL_GROUPS, POOL_GROUP_DIM), jnp.float32),
        'w_mix_out': nrm(ks[8], (DEPTH, D_MIX, D_MODEL), D_MIX ** -0.5),
        'norm_mem': gain(ks[9]),
        'norm_xa_pre': gain(ks[10]),
        'norm_xa_post': gain(ks[11]),
        'w_xq': nrm(ks[12], (DEPTH, D_MODEL, D_MODEL), D_MODEL ** -0.5),
        'w_xkv': nrm(ks[13], (DEPTH, D_MODEL, 2 * D_MODEL), D_MODEL ** -0.5),
        'w_xo': nrm(ks[14], (DEPTH, D_MODEL, D_MODEL), D_MODEL ** -0.5),
        'norm_ffn_pre': gain(ks[15]),
        'norm_ffn_post': gain(ks[16]),
        'w_up': nrm(ks[17], (DEPTH, D_MODEL, 2 * D_FF), D_MODEL ** -0.5),
        'conv_w': nrm(ks[18], (DEPTH, CONV_WIDTH, 1, 2 * D_FF), CONV_WIDTH ** -0.5),
        'conv_b': nrm(ks[19], (DEPTH, 2 * D_FF), 0.02),
        'w_down': nrm(ks[20], (DEPTH, D_FF, D_MODEL), D_FF ** -0.5),
    }


def reference(x, mem, norm_mix_pre, norm_mix_post, w_in, b_forget, w_pool, pool_scale, w_mix_out,
              norm_mem, norm_xa_pre, norm_xa_post, w_xq, w_xkv, w_xo,
              norm_ffn_pre, norm_ffn_post, w_up, conv_w, conv_b, w_down):
    b, s, _ = x.shape
    for l in range(DEPTH):
        h = rms_norm(x, norm_mix_pre[l])
        proj = h @ w_in[l]
        u_pool = proj[..., :D_POOL]
        o = D_POOL
        q = proj[..., o:o + D_FOX].reshape(b, s, FOX_HEADS, FOX_HEAD_DIM)
        k = proj[..., o + D_FOX:o + 2 * D_FOX].reshape(b, s, FOX_HEADS, FOX_HEAD_DIM)
        v = proj[..., o + 2 * D_FOX:o + 3 * D_FOX].reshape(b, s, FOX_HEADS, FOX_HEAD_DIM)
        f_logit = proj[..., o + 3 * D_FOX:].astype(jnp.float32) + b_forget[l].astype(jnp.float32)
        log_f = jax.nn.log_sigmoid(f_logit)
        y_pool = pool_mixer(u_pool, w_pool[l], pool_scale[l])
        y_fox = forgetting_attention(q, k, v, log_f).reshape(b, s, D_FOX)
        y = jnp.concatenate([y_pool, y_fox], axis=-1) @ w_mix_out[l]
        x = x + rms_norm(y, norm_mix_post[l])
        h = rms_norm(x, norm_xa_pre[l])
        mem_n = rms_norm(mem, norm_mem[l])
        y = memory_cross_attention(h, mem_n, w_xq[l], w_xkv[l], w_xo[l])
        x = x + rms_norm(y, norm_xa_post[l])
        h = rms_norm(x, norm_ffn_pre[l])
        y = conv_gated_mlp(h, w_up[l], conv_w[l], conv_b[l], w_down[l])
        x = x + rms_norm(y, norm_ffn_post[l])
    return x
```

```python
import contextlib
import numpy as np
import ml_dtypes
import concourse.bass as bass
import concourse.mybir as mybir
from concourse.bass_utils import run_bass_kernel_spmd

F32 = mybir.dt.float32
BF16 = mybir.dt.bfloat16
AF = mybir.ActivationFunctionType
ALU = mybir.AluOpType

COMPUTE = ("pe", "act", "dve", "pool")
D = 1024
DIN = 2572
NT = 4096
NQ = 2176
QOFF = 1920
NQB = 17
MASKV = -30000.0


class Sched:
    def __init__(self, nc, es, n_dma_sems=48, same_engine_sync=True):
        self.nc = nc
        self.sem = {e: es.enter_context(nc.semaphore("s_" + e)) for e in COMPUTE}
        self.cnt = {e: 0 for e in COMPUTE}
        self.dsems = [es.enter_context(nc.semaphore("d%d" % i)) for i in range(n_dma_sems)]
        self.dcnt = [0] * n_dma_sems
        self.dnext = 0
        self.streams = {e: [] for e in COMPUTE + ("sp",)}
        self.waited = {e: {} for e in COMPUTE + ("sp",)}
        self.lastw = {}
        self.readers = {}
        self.same = same_engine_sync

    def _need(self, eng, token, waits, allow_same=True):
        if token is None:
            return
        key, val = token
        if key == eng and not (self.same and allow_same):
            return
        if self.waited[eng].get(key, 0) >= val:
            return
        self.waited[eng][key] = val
        waits.append(token)

    def _deps(self, eng, reads, writes, waits, accum=False):
        for r in reads:
            self._need(eng, self.lastw.get(r), waits)
        for w in writes:
            self._need(eng, self.lastw.get(w), waits, allow_same=not accum)
            for k, v in self.readers.get(w, {}).items():
                if k != eng:
                    self._need(eng, (k, v), waits)

    def _commit(self, token, reads, writes):
        k, v = token
        for r in reads:
            d = self.readers.setdefault(r, {})
            if d.get(k, 0) < v:
                d[k] = v
        for w in writes:
            self.lastw[w] = token
            self.readers[w] = {}

    def op(self, eng, fn, reads=(), writes=(), accum=False):
        waits = []
        self._deps(eng, reads, writes, waits, accum)
        self.cnt[eng] += 1
        token = (eng, self.cnt[eng])
        self.streams[eng].append((fn, waits, None))
        self._commit(token, reads, writes)
        return token

    def dma(self, fn, reads=(), writes=(), queue="sp"):
        waits = []
        self._deps(queue, reads, writes, waits)
        i = self.dnext
        self.dnext = (self.dnext + 1) % len(self.dsems)
        key = ("d", i)
        if self.dcnt[i] > 0:
            self._need(queue, (key, 16 * self.dcnt[i]), waits)
        self.dcnt[i] += 1
        token = (key, 16 * self.dcnt[i])
        self.streams[queue].append((fn, waits, i))
        self._commit(token, reads, writes)
        return token

    def alias(self, new_keys, old_keys):
        toks = {}
        for ok in old_keys:
            lw = self.lastw.get(ok)
            if lw is not None:
                toks[lw[0]] = max(toks.get(lw[0], 0), lw[1])
            for k, v in self.readers.get(ok, {}).items():
                toks[k] = max(toks.get(k, 0), v)
        for nk in new_keys:
            d = self.readers.setdefault(nk, {})
            for k, v in toks.items():
                d[k] = max(d.get(k, 0), v)

    def barrier(self):
        toks = [(e, self.cnt[e]) for e in COMPUTE if self.cnt[e] > 0]
        toks += [(("d", i), 16 * c) for i, c in enumerate(self.dcnt) if c > 0]
        for eng in COMPUTE + ("sp",):
            waits = []
            for t in toks:
                self._need(eng, t, waits)
            if waits:
                self.streams[eng].append((None, waits, None))

    def semh(self, key):
        if isinstance(key, tuple):
            return self.dsems[key[1]]
        return self.sem[key]

    def replay(self, eng, e):
        for fn, waits, di in self.streams[eng]:
            for key, val in waits:
                e.wait_ge(self.semh(key), val)
            if fn is None:
                continue
            inst = fn(e)
            if di is not None:
                inst.then_inc(self.dsems[di], 16)
            else:
                inst.then_inc(self.sem[eng], 1)

    def emit(self):
        with self.nc.Block() as block:
            @block.tensor
            def _(e):
                self.replay("pe", e)

            @block.scalar
            def _(e):
                self.replay("act", e)

            @block.vector
            def _(e):
                self.replay("dve", e)

            @block.gpsimd
            def _(e):
                self.replay("pool", e)

            @block.sync
            def _(e):
                self.replay("sp", e)


def build(stage=4):
    nc = bass.Bass("TRN2", target_bir_lowering=False)

    def dt(name, shape, dty=F32, kind="ExternalInput"):
        return nc.dram_tensor(name, shape, dty, kind=kind).ap()

    xk = dt("xk", [NT, D])
    memd = dt("mem", [256, D])
    gains = dt("gains", [7, D])
    w_in = dt("w_in", [D, DIN])
    b_f = dt("b_forget", [12, 1])
    w_pool = dt("w_pool", [4, 64, 64])
    pscale = dt("pool_scale", [256, 1])
    w_mix = dt("w_mix_out", [D, D])
    w_xq = dt("w_xq", [D, D])
    w_xkv = dt("w_xkv", [D, 2 * D])
    w_xo = dt("w_xo", [D, D])
    w_up = dt("w_up", [D, 8192])
    conv_w = dt("conv_w", [3, 8192])
    conv_b = dt("conv_b", [1, 8192])
    w_down = dt("w_down", [4096, D])
    kconst = dt("kconst", [4, NT], BF16)
    qconst = dt("qconst", [4, NQ], BF16)
    invc = dt("invc", [2, 128, 16])
    flags = dt("flags", [128, 2])
    out = dt("out", [2048, D], kind="ExternalOutput")
    gsc = dt("gsc", [12, 3, NT], BF16, kind="Internal")

    with contextlib.ExitStack() as es:
        S = Sched(nc, es)

        def sb(name, shape, dty):
            return es.enter_context(nc.sbuf_tensor(name, shape, dty))

        A1 = sb("A1", [128, 32768], BF16)
        A2 = sb("A2", [128, 34816], BF16)
        A3 = sb("A3", [128, 15360], BF16)
        xl = [sb("xl%d" % i, [128, D], F32) for i in range(2)]
        hb = [sb("hb%d" % i, [128, D], BF16) for i in range(3)]
        junk = sb("junk", [128, D], BF16)
        gt = [sb("gt%d" % i, [128, D], F32) for i in range(3)]
        xhalo = sb("xhalo", [128, D], F32)
        ident = sb("ident", [128, 128], BF16)
        identf = sb("identf", [128, 128], F32)
        cbias = sb("cbias", [128, 128], BF16)
        cbf = sb("cbf", [128, 128], F32)
        rs = sb("rs", [128, 64], F32)
        rl = sb("rl", [128, 16], F32)
        flagt = sb("flagt", [128, 2], F32)
        pscol = sb("pscol", [128, 2], F32)
        negb = sb("negb", [12, 1], F32)
        epst = sb("epst", [128, 1], F32)
        mhalf = sb("mhalf", [128, 1], F32)
        cw = sb("cw", [128, 4, 64], F32)
        memkv = sb("memkv", [128, 4104], BF16)
        invt = sb("invt", [128, 2, 16], F32)
        h3h = sb("h3h", [128, 8, 2], BF16)
        psum = es.enter_context(nc.psum_tensor("psum", [128, 8, 512], F32))
        pst = psum[:, 7, :].bitcast(BF16)
        PB = lambda b: ("ps", b)

        hT = A1[:].rearrange("p (c t) -> p c t", c=8)
        xres = A1[:].bitcast(F32).rearrange("p (b d) -> p b d", b=16)
        actT = A1[:].rearrange("p (c t) -> p c t", c=32)

        KP = A2[:, 0:8192].rearrange("p (s t) -> p s t", s=2)
        QP = A2[:, 8192:12544].rearrange("p (s t) -> p s t", s=2)
        VP = A2[:, 12544:16704].rearrange("p (b s d) -> p b s d", b=32, s=2)
        ymix = A2[:, 16704:16704 + 17 * 768].rearrange("p (b d) -> p b d", b=17)

        def xr(qb):
            return xhalo[:] if qb == 0 else xres[:, qb - 1, :]

        rs_i = [0]

        def rs_col():
            rs_i[0] = (rs_i[0] + 1) % 64
            return rs[:, rs_i[0]:rs_i[0] + 1], ("rs", rs_i[0])

        rl_i = [0]

        def rl_col():
            rl_i[0] = (rl_i[0] + 1) % 16
            return rl[:, rl_i[0]:rl_i[0] + 1], ("rl", rl_i[0])

        rstd_mode = ["sqrt"]

        def finish_rstd(col, key):
            if rstd_mode[0] == "pow":
                S.op("dve", lambda e: e.tensor_scalar(out=col, in0=col, scalar1=1.0 / D, scalar2=1e-6,
                                                      op0=ALU.mult, op1=ALU.add), reads=[key], writes=[key])
                S.op("pool", lambda e: e.tensor_tensor(out=col, in0=col, in1=mhalf[:, 0:1], op=ALU.pow),
                     reads=[key, "mhalf"], writes=[key])
                return
            S.op("act", lambda e: e.activation(out=col, in_=col, func=AF.Sqrt, scale=1.0 / D, bias=epst[:, 0:1]),
                 reads=[key, "epst"], writes=[key])
            S.op("dve", lambda e: e.reciprocal(out=col, in_=col), reads=[key], writes=[key])

        def rstd_of(src, src_reads):
            col, key = rs_col()
            S.op("act", lambda e: e.activation(out=junk[:], in_=src, func=AF.Square, accum_out=col),
                 reads=src_reads, writes=[key])
            finish_rstd(col, key)
            return col, key

        def load_gain(i, gi):
            S.dma(lambda e: e.dma_start(out=gt[i][:], in_=gains[gi:gi + 1, :].partition_broadcast(128)),
                  writes=[("gt", i)])

        hb_i = [0]
        sb_i = [0]

        ev_i = [0]

        def norm_p1(src, src_reads, gi):
            col, key = rstd_of(src, src_reads)
            hb_i[0] = (hb_i[0] + 1) % 3
            i = hb_i[0]
            S.op("dve", lambda e: e.scalar_tensor_tensor(out=hb[i][:], in0=src, scalar=col, in1=gt[gi][:],
                                                         op0=ALU.mult, op1=ALU.mult),
                 reads=list(src_reads) + [key, ("gt", gi)], writes=[("hb", i)])
            return i

        def norm_p2(i, dst_fn, dst_key, sel=None):
            for c in range(8):
                S.op("pe", lambda e, c=c: e.transpose(out=pst[:, c * 128:(c + 1) * 128],
                                                      in_=hb[i][:, c * 128:(c + 1) * 128], identity=ident[:]),
                     reads=[("hb", i), "ident"], writes=[PB(7)], accum=True)
            src_v = pst.rearrange("p (c t) -> p c t", c=8)
            if sel is not None:
                src_v = src_v[:, :, sel[0]:sel[1]]
            ev_i[0] ^= 1
            if ev_i[0]:
                S.op("act", lambda e: e.copy(out=dst_fn(), in_=src_v), reads=[PB(7)], writes=[dst_key])
            else:
                S.op("dve", lambda e: e.tensor_copy(out=dst_fn(), in_=src_v), reads=[PB(7)], writes=[dst_key])

        def norm_seq(items):
            pend = []
            for (pre, src, src_reads, gi, dst_fn, dst_key, sel) in items:
                if pre is not None:
                    pre()
                i = norm_p1(src, src_reads, gi)
                pend.append((i, dst_fn, dst_key, sel))
                if len(pend) > 2:
                    norm_p2(*pend.pop(0))
            for p_ in pend:
                norm_p2(*p_)

        def load_w(dram, r0, nk, c0, ncols, dst, dst_key, eng=None, after=()):
            S.dma(lambda e: e.dma_start(out=dst, in_=dram[r0:r0 + nk * 128, c0:c0 + ncols]
                                        .rearrange("(k p) n -> p k n", p=128)), reads=list(after), writes=[dst_key],
                  queue="pool")

        def mm(out_ap, lhsT, rhs, start, stop, reads, wkey):
            S.op("pe", lambda e: e.matmul(out=out_ap, lhsT=lhsT, rhs=rhs, start=start, stop=stop),
                 reads=reads, writes=[wkey], accum=True)

        S.op("pool", lambda e: e.memset(epst[:], 1e-6), writes=["epst"])
        S.op("pool", lambda e: e.memset(mhalf[:], -0.5), writes=["mhalf"])
        S.op("pool", lambda e: e.memset(identf[:], 0.0), writes=["identf"])
        S.op("pool", lambda e: e.affine_select(out=identf[:], in_=identf[:], pattern=[[-1, 128]],
                                               compare_op=ALU.not_equal, fill=1.0, base=0, channel_multiplier=1),
             reads=["identf"], writes=["identf"])
        S.op("pool", lambda e: e.tensor_copy(out=ident[:], in_=identf[:]), reads=["identf"], writes=["ident"])
        S.op("pool", lambda e: e.memset(cbf[:], 0.0), writes=["cbf"])
        S.op("pool", lambda e: e.affine_select(out=cbf[:], in_=cbf[:], pattern=[[1, 128]],
                                               compare_op=ALU.is_ge, fill=MASKV, base=0, channel_multiplier=-1),
             reads=["cbf"], writes=["cbf"])
        S.op("pool", lambda e: e.tensor_copy(out=cbias[:], in_=cbf[:]), reads=["cbf"], writes=["cbias"])
        S.dma(lambda e: e.dma_start(out=flagt[:], in_=flags[:, :]), writes=["flagt"])
        S.dma(lambda e: e.dma_start(out=negb[:], in_=b_f[:, :]), writes=["negb"])
        S.op("act", lambda e: e.mul(out=negb[:], in_=negb[:], mul=-1.0), reads=["negb"], writes=["negb"])
        for cc in range(2):
            S.dma(lambda e, cc=cc: e.dma_start(out=pscol[:, cc:cc + 1], in_=pscale[cc * 128:(cc + 1) * 128, :]),
                  writes=["pscol"])

        o = 0
        wf = A3[:, o:o + 96].rearrange("p (k n) -> p k n", k=8); o += 96
        wpair = A3[:, o:o + 3072].rearrange("p (k w n) -> p k w n", k=8, w=3); o += 3072
        wu = A3[:, o:o + 2048].rearrange("p (k n) -> p k n", k=8); o += 2048
        wpbd = A3[:, o:o + 256].rearrange("p (c n) -> p c n", c=2); o += 256
        PT = [A3[:, o + i * 512:o + (i + 1) * 512] for i in range(3)]; o += 1536
        vtmp = A3[:, o:o + 512]; o += 512
        ypT = A3[:, o:o + 4352].rearrange("p (c t) -> p c t", c=2); o += 4352
        go = 16704
        g_et = A2[0:12, go:go + 1024].bitcast(F32); go += 1024
        g_sps = [A2[0:12, go + i * 1024:go + (i + 1) * 1024].bitcast(F32) for i in range(8)]; go += 8192
        g_G = [A2[0:12, go + i * 1024:go + (i + 1) * 1024].bitcast(F32) for i in range(2)]; go += 2048
        g_r1 = A2[0:12, go:go + 1024].bitcast(F32); go += 1024
        g_r2 = A2[0:12, go:go + 1024].bitcast(F32); go += 1024
        g_ones = A2[0:12, go:go + 512]; go += 512
        g_parts = [A2[0:12, go + i * 1536:go + (i + 1) * 1536].rearrange("p (a t) -> p a t", a=3) for i in range(2)]
        go += 3072
        wpst = A3[:, o:o + 512].bitcast(F32).rearrange("p (c n) -> p c n", c=2); o += 512
        yT = A3[:, o:o + 768].rearrange("p (c t) -> p c t", c=6); o += 768
        tmpf = A3[:, o:o + 2048].bitcast(F32); o += 2048
        assert o <= 15360, o

        G_KEYS = ["g_et"] + [("g_sp", i) for i in range(8)] + [("gG", 0), ("gG", 1), "g_r1", "g_r2", "g_ones", ("gp", 0), ("gp", 1)]
        hTk = lambda t0, n: [("hT", b) for b in range(t0 // 128, (t0 + n + 127) // 128)]
        load_gain(0, 0)
        load_gain(2, 2)
        wxk = A2[:, 16704:24896].rearrange("p (k n) -> p k n", k=8)
        wxv = A2[:, 24896:33088].rearrange("p (k n) -> p k n", k=8)
        memT = xhalo[:].bitcast(BF16).rearrange("p (k t) -> p k t", k=8)
        kTm = memkv[:, 0:2048].rearrange("p (k t) -> p k t", k=8)
        Vm = memkv[:, 2048:4104].rearrange("p (m h d) -> p m h d", m=2, h=4)

        def pair_weights(pair):
            for which in range(3):
                load_w(w_in, 0, 8, 256 + which * 768 + pair * 128, 128, wpair[:, :, which, :], ("wpair", which))

        S.op("dve", lambda e: e.memset(Vm[:, :, :, 256:257], 1.0), writes=["Vmones"])
        items = []
        for mb in range(2):
            i = mb % 2
            pre = (lambda mb=mb, i=i: S.dma(lambda e: e.dma_start(out=xl[i][:], in_=memd[mb * 128:(mb + 1) * 128, :]),
                                            writes=[("xl", i)]))
            items.append((pre, xl[i][:], [("xl", i)], 2, (lambda mb=mb: memT[:, :, mb * 128:(mb + 1) * 128]),
                          ("memT", mb), None))
        norm_seq(items)
        load_w(w_in, 0, 8, 2560, 12, wf, "wf", after=[("memT", 1)])
        load_w(w_in, 0, 8, 0, 128, wu[:, :, 0:128], "wu")
        load_w(w_in, 0, 8, 128, 128, wu[:, :, 128:256], "wu")
        S.op("dve", lambda e: e.memset(wpst[:], 0.0), writes=["wpst"])
        for g in range(4):
            cc, hf = g // 2, g % 2
            S.dma(lambda e, g=g, cc=cc, hf=hf: e.dma_start(out=wpst[hf * 64:(hf + 1) * 64, cc, hf * 64:(hf + 1) * 64],
                                                           in_=w_pool[g, :, :]), reads=["wpst"], writes=["wpst"])
        for cc in range(2):
            S.dma(lambda e, cc=cc: e.dma_start(out=invt[:, cc, :], in_=invc[cc, :, :]), writes=["invt"])
        pair_weights(0)
        for k2 in range(8):
            load_w(w_xkv, k2 * 128, 1, 0, 1024, wxk[:, k2:k2 + 1, :], ("wxk", k2 // 2))
            load_w(w_xkv, k2 * 128, 1, 1024, 1024, wxv[:, k2:k2 + 1, :], ("wxv", k2 // 2))
        S.op("pool", lambda e: e.tensor_copy(out=wpbd, in_=wpst), reads=["wpst"], writes=["wpbd"])
        items = []
        xs = A2[:, 0:16384].bitcast(F32).rearrange("p (r d) -> p r d", r=8)
        for kb in range(32):
            i = kb % 8
            pre = (lambda kb=kb, i=i: S.dma(lambda e: e.dma_start(out=xs[:, i, :], in_=xk[kb * 128:(kb + 1) * 128, :]),
                                            reads=([("xs", (kb - 3) % 8)] if kb >= 3 else []), writes=[("xs", i)]))
            items.append((pre, xs[:, i, :], [("xs", i)], 0, (lambda kb=kb: hT[:, :, kb * 128:(kb + 1) * 128]),
                          ("hT", kb), None))
        norm_seq(items[:16])
        mTk = [("memT", 0), ("memT", 1)]
        for fc in range(8):
            b = sb_i[0] % 3; sb_i[0] += 1
            for kc in range(8):
                mm(psum[:, b, 0:256], wxk[:, kc, fc * 128:(fc + 1) * 128], memT[:, kc, :], kc == 0, kc == 7,
                   mTk + [("wxk", kc // 2)], PB(b))
            S.op("act", lambda e, b=b, fc=fc: e.copy(out=kTm[:, fc, :], in_=psum[:, b, 0:256]),
                 reads=[PB(b)], writes=["kTm"])
        for mb in range(2):
            for n in range(2):
                b = sb_i[0] % 3; sb_i[0] += 1
                for kc in range(8):
                    mm(psum[:, b, :], memT[:, kc, mb * 128:(mb + 1) * 128], wxv[:, kc, n * 512:(n + 1) * 512],
                       kc == 0, kc == 7, mTk + [("wxv", kc // 2)], PB(b))
                S.op("act", lambda e, b=b, mb=mb, n=n: e.copy(out=Vm[:, mb, 2 * n:2 * n + 2, 0:256],
                                                              in_=psum[:, b, :].rearrange("p (h d) -> p h d", h=2)),
                     reads=[PB(b)], writes=["Vm"])
        norm_seq(items[16:])
        S.barrier()
        S.op("dve", lambda e: e.memset(KP[0:64, 1, :], 0.0), writes=[("KPaug", 1)])
        S.op("dve", lambda e: e.memset(QP[0:64, 1, :], 0.0), writes=[("QPaug", 1)])
        S.op("pool", lambda e: e.memset(KP[64:128, 0, :], 0.0), writes=[("KPaug", 0)])
        S.op("pool", lambda e: e.memset(QP[64:128, 0, :], 0.0), writes=[("QPaug", 0)])
        S.op("dve", lambda e: e.memset(VP[:, :, :, 64:65], 1.0), writes=["VPones"])
        for s in range(2):
            base = 64 if s == 0 else 0
            S.dma(lambda e, s=s, base=base: e.dma_start(out=KP[base + 3:base + 7, s, :], in_=kconst[:, :]),
                  writes=[("KPaug", s)])
            S.dma(lambda e, s=s, base=base: e.dma_start(out=QP[base:base + 3, s, :], in_=qconst[0:3, :]),
                  writes=[("QPaug", s)])
            S.dma(lambda e, s=s, base=base: e.dma_start(out=QP[base + 6:base + 7, s, :], in_=qconst[3:4, :]),
                  writes=[("QPaug", s)])

        qchunks = [(QOFF, 128, 0)] + [(2048 + 512 * i, 512, 128 + 512 * i) for i in range(4)]
        pt_i = [0]
        npairs = 6 if stage >= 1 else 0
        def pair_aug(pair):
            for s in range(2):
                h = 2 * pair + s
                base = 64 if s == 0 else 0
                S.dma(lambda e, s=s, h=h, base=base: e.dma_start(out=KP[base:base + 3, s, :], in_=gsc[h, :, :]),
                      reads=["gsc"], writes=[("KPaug", s)])
                S.dma(lambda e, s=s, h=h, base=base: e.dma_start(out=QP[base + 3:base + 6, s, :],
                                                                 in_=gsc[h, :, QOFF:NT]),
                      reads=["gsc"], writes=[("QPaug", s)])

        def g_chain():
            S.alias(G_KEYS, [("wxk", k) for k in range(4)] + [("wxv", k) for k in range(4)] + [("memT", 0), ("memT", 1)])
            S.op("dve", lambda e: e.memset(g_ones, 1.0), writes=["g_ones"])
            for tc in range(8):
                b = tc % 3
                for kc in range(8):
                    mm(psum[0:12, b, :], wf[:, kc, :], hT[:, kc, tc * 512:(tc + 1) * 512], kc == 0, kc == 7,
                       ["wf"] + hTk(tc * 512, 512), PB(b))
                S.op("act", lambda e, b=b: e.activation(out=g_et, in_=psum[0:12, b, :], func=AF.Exp,
                                                        bias=negb[:, 0:1], scale=-1.0),
                     reads=[PB(b), "negb"], writes=["g_et"])
                S.op("act", lambda e, tc=tc: e.activation(out=g_sps[tc], in_=g_et, func=AF.Ln, bias=1.0, scale=1.0),
                     reads=["g_et"], writes=[("g_sp", tc)])
            for tc in range(8):
                gi = tc % 2
                init = 0.0 if tc == 0 else g_G[1 - gi][:, 511:512]
                S.op("dve", lambda e, gi=gi, init=init, tc=tc: e.tensor_tensor_scan(out=g_G[gi], data0=g_ones, data1=g_sps[tc],
                                                                                    initial=init, op0=ALU.mult, op1=ALU.add),
                     reads=[("g_sp", tc), "g_ones", ("gG", 1 - gi)], writes=[("gG", gi)])
                pp = g_parts[gi]
                S.op("dve", lambda e, gi=gi, pp=pp: e.tensor_copy(out=pp[:, 0, :], in_=g_G[gi]),
                     reads=[("gG", gi)], writes=[("gp", gi)])
                S.op("dve", lambda e, gi=gi, pp=pp: e.tensor_tensor(out=g_r1, in0=g_G[gi], in1=pp[:, 0, :], op=ALU.subtract),
                     reads=[("gG", gi), ("gp", gi)], writes=["g_r1"])
                S.op("dve", lambda e, pp=pp: e.tensor_copy(out=pp[:, 1, :], in_=g_r1), reads=["g_r1"], writes=[("gp", gi)])
                S.op("dve", lambda e, pp=pp: e.tensor_tensor(out=g_r2, in0=g_r1, in1=pp[:, 1, :], op=ALU.subtract),
                     reads=["g_r1", ("gp", gi)], writes=["g_r2"])
                S.op("dve", lambda e, pp=pp: e.tensor_copy(out=pp[:, 2, :], in_=g_r2), reads=["g_r2"], writes=[("gp", gi)])
                S.dma(lambda e, tc=tc, pp=pp: e.dma_start(out=gsc[:, :, tc * 512:(tc + 1) * 512], in_=pp),
                      reads=[("gp", gi)], writes=["gsc"])
            S.alias([("ymix", qb) for qb in range(NQB)], G_KEYS)

        def pair_proj(pair):
            for ci, (t0, n, q0) in enumerate(qchunks):
                b = sb_i[0] % 3; sb_i[0] += 1
                for kc in range(8):
                    mm(psum[:, b, 0:n], wpair[:, kc, 0, :], hT[:, kc, t0:t0 + n], kc == 0, kc == 7,
                       [("wpair", 0)] + hTk(t0, n), PB(b))
                S.op("act", lambda e, b=b, n=n, q0=q0: e.activation(out=QP[0:64, 0, q0:q0 + n], in_=psum[0:64, b, 0:n],
                                                                    func=AF.Copy, scale=0.125),
                     reads=[PB(b)], writes=[("QP", 0, ci)])
                if pair == 0:
                    S.op("act", lambda e, b=b, n=n, q0=q0: e.activation(out=QP[64:128, 1, q0:q0 + n],
                                                                        in_=psum[64:128, b, 0:n], func=AF.Copy, scale=0.125),
                         reads=[PB(b)], writes=[("QP", 1, ci)])
                else:
                    S.op("dve", lambda e, b=b, n=n, q0=q0: e.tensor_scalar(out=QP[64:128, 1, q0:q0 + n],
                                                                           in0=psum[64:128, b, 0:n], scalar1=0.125,
                                                                           scalar2=None, op0=ALU.mult),
                         reads=[PB(b)], writes=[("QP", 1, ci)])
            for tc in range(8):
                b = sb_i[0] % 3; sb_i[0] += 1
                for kc in range(8):
                    mm(psum[:, b, :], wpair[:, kc, 1, :], hT[:, kc, tc * 512:(tc + 1) * 512], kc == 0, kc == 7,
                       [("wpair", 1)] + hTk(tc * 512, 512), PB(b))
                S.op("act", lambda e, b=b, tc=tc: e.copy(out=KP[0:64, 0, tc * 512:(tc + 1) * 512], in_=psum[0:64, b, :]),
                     reads=[PB(b)], writes=[("KP", 0, tc)])
                if pair == 0:
                    S.op("act", lambda e, b=b, tc=tc: e.copy(out=KP[64:128, 1, tc * 512:(tc + 1) * 512],
                                                             in_=psum[64:128, b, :]),
                         reads=[PB(b)], writes=[("KP", 1, tc)])
                else:
                    S.op("dve", lambda e, b=b, tc=tc: e.tensor_copy(out=KP[64:128, 1, tc * 512:(tc + 1) * 512],
                                                                    in_=psum[64:128, b, :]),
                         reads=[PB(b)], writes=[("KP", 1, tc)])
            for tc in range(8):
                b = sb_i[0] % 3; sb_i[0] += 1
                for kc in range(8):
                    mm(psum[:, b, :], wpair[:, kc, 2, :], hT[:, kc, tc * 512:(tc + 1) * 512], kc == 0, kc == 7,
                       [("wpair", 2)] + hTk(tc * 512, 512), PB(b))
                S.op("act", lambda e, b=b: e.copy(out=vtmp, in_=psum[:, b, :]), reads=[PB(b)], writes=["vtmp"])
                for i in range(4):
                    S.op("pe", lambda e, i=i: e.transpose(out=pst[:, i * 128:(i + 1) * 128],
                                                          in_=vtmp[:, i * 128:(i + 1) * 128], identity=ident[:]),
                         reads=["vtmp", "ident"], writes=[PB(7)], accum=True)
                if pair == 0:
                    S.op("act", lambda e, tc=tc: e.copy(
                        out=VP[:, 4 * tc:4 * tc + 4, :, 0:64],
                        in_=pst[:, 0:512].rearrange("p (b s d) -> p b s d", b=4, s=2)),
                         reads=[PB(7)], writes=[("VP", tc)])
                else:
                    S.op("dve", lambda e, tc=tc: e.tensor_copy(
                        out=VP[:, 4 * tc:4 * tc + 4, :, 0:64],
                        in_=pst[:, 0:512].rearrange("p (b s d) -> p b s d", b=4, s=2)),
                         reads=[PB(7)], writes=[("VP", tc)])

        def pair_attn(pair):
            steps = []
            for s in range(2):
                h = 2 * pair + s
                rows = slice(0, 128)
                for ci, (qb0, nqb) in enumerate([(0, 1), (1, 4), (5, 4), (9, 4), (13, 4)]):
                    q0 = qb0 * 128
                    klast = 15 + qb0 + nqb - 1
                    if ci == 0:
                        for g in range(4):
                            b = sb_i[0] % 3; sb_i[0] += 1
                            pi = pt_i[0] % 3; pt_i[0] += 1

                            def fS0(s=s, g=g, b=b):
                                for t in range(4):
                                    j = 4 * g + t
                                    dg = (j == 15)
                                    mm(psum[:, b, t * 128:(t + 1) * 128], KP[:, s, j * 128:(j + 1) * 128], QP[:, s, 0:128],
                                       True, not dg, [("KP", s, g), ("KPaug", s), ("QP", s, 0), ("QPaug", s)], PB(b))
                                    if dg:
                                        mm(psum[:, b, t * 128:(t + 1) * 128], ident[:], cbias[:], False, True,
                                           ["ident", "cbias"], PB(b))

                            def fE0(b=b, pi=pi):
                                S.op("act", lambda e: e.activation(out=PT[pi][:, 0:512], in_=psum[:, b, 0:512], func=AF.Exp),
                                     reads=[PB(b)], writes=[("PT", pi)])

                            def fPV0(s=s, g=g, pi=pi, h=h):
                                for t in range(4):
                                    j = 4 * g + t
                                    mm(psum[:, 3, 0:65], PT[pi][:, t * 128:(t + 1) * 128], VP[:, j, s, :],
                                       j == 0, j == 15, [("PT", pi), ("VP", g), "VPones"], PB(3))
                                if g == 3:
                                    col, key = rl_col()
                                    S.op("dve", lambda e: e.reciprocal(out=col, in_=psum[:, 3, 64:65]),
                                         reads=[PB(3)], writes=[key])
                                    S.op("dve", lambda e: e.tensor_scalar(
                                        out=ymix[:, 0, h * 64:(h + 1) * 64], in0=psum[:, 3, 0:64], scalar1=col,
                                        scalar2=None, op0=ALU.mult), reads=[PB(3), key], writes=[("ymix", 0)])

                            steps.append((fS0, fE0, fPV0))
                        continue
                    for j in range(klast + 1):
                        m = max(0, j - (15 + qb0))
                        c0, c1 = m * 128, nqb * 128
                        b = sb_i[0] % 3; sb_i[0] += 1
                        pi = pt_i[0] % 3; pt_i[0] += 1
                        diag = j >= 15 + qb0

                        def fS(s=s, rows=rows, ci=ci, q0=q0, j=j, c0=c0, c1=c1, b=b, diag=diag):
                            mm(psum[:, b, c0:c1], KP[rows, s, j * 128:(j + 1) * 128], QP[rows, s, q0 + c0:q0 + c1],
                               True, not diag, [("KP", s, j // 4), ("KPaug", s), ("QP", s, ci), ("QPaug", s)], PB(b))
                            if diag:
                                mm(psum[:, b, c0:c0 + 128], ident[:], cbias[:], False, True, ["ident", "cbias"], PB(b))

                        def fE(b=b, c0=c0, c1=c1, pi=pi):
                            S.op("act", lambda e: e.activation(out=PT[pi][:, c0:c1], in_=psum[:, b, c0:c1], func=AF.Exp),
                                 reads=[PB(b)], writes=[("PT", pi)])

                        def fPV(s=s, j=j, m=m, nqb=nqb, qb0=qb0, pi=pi, h=h, last=(j == klast)):
                            for lq in range(m, nqb):
                                mm(psum[:, 3 + lq, 0:65], PT[pi][:, lq * 128:(lq + 1) * 128], VP[:, j, s, :],
                                   j == 0, j == 15 + qb0 + lq, [("PT", pi), ("VP", j // 4), "VPones"], PB(3 + lq))
                            if last:
                                for lq in range(nqb):
                                    col, key = rl_col()
                                    S.op("dve", lambda e, lq=lq, col=col: e.reciprocal(out=col, in_=psum[:, 3 + lq, 64:65]),
                                         reads=[PB(3 + lq)], writes=[key])
                                    S.op("dve", lambda e, lq=lq, col=col, qb=qb0 + lq: e.tensor_scalar(
                                        out=ymix[:, qb, h * 64:(h + 1) * 64], in0=psum[:, 3 + lq, 0:64], scalar1=col,
                                        scalar2=None, op0=ALU.mult),
                                         reads=[PB(3 + lq), key], writes=[("ymix", qb0 + lq)])

                        steps.append((fS, fE, fPV))
            LA = 2
            for idx in range(len(steps) + LA):
                if idx < len(steps):
                    steps[idx][0]()
                    steps[idx][1]()
                if idx - LA >= 0:
                    steps[idx - LA][2]()

        wmixA = A2[:, 29760:33856].rearrange("p (k n) -> p k n", k=4)
        wmixC = A3[:, 96:3168].rearrange("p (k n) -> p k n", k=3)
        wmixD = A3[:, 5472:6496]
        for pair in range(npairs):
            if pair == 2:
                S.alias([("wmix", 0), ("wmix", 1)], G_KEYS)
                for k2 in range(4):
                    load_w(w_mix, k2 * 128, 1, 0, 1024, wmixA[:, k2:k2 + 1, :], ("wmix", k2 // 2))
            if pair == 0:
                g_chain()
            pair_proj(pair)
            if pair + 1 < npairs:
                pair_weights(pair + 1)
            if pair == npairs - 1:
                S.alias([("wmix", 2), ("wmix", 3)], [("wpair", w_) for w_ in range(3)] + ["wf"])
                for k2 in range(4, 7):
                    load_w(w_mix, k2 * 128, 1, 0, 1024, wmixC[:, k2 - 4:k2 - 3, :], ("wmix", k2 // 2))
            pair_aug(pair)
            pair_attn(pair)
            if pair == npairs - 1:
                S.alias([("wmix", 3)], [("PT", i_) for i_ in range(3)] + ["vtmp"])
                load_w(w_mix, 7 * 128, 1, 0, 1024, wmixD.rearrange("p (k n) -> p k n", k=1), ("wmix", 3))

        S.barrier()
        NP_ = 2304
        P0 = 1792
        uT = A2[:, 0:4608].bitcast(F32)
        sA = A2[:, 4608:9216].bitcast(F32)
        sB = A2[:, 9216:13824].bitcast(F32)
        dT = A2[:, 13824:13824 + 2304]
        t16 = A2[:, 16128:16160].bitcast(F32)
        for cc in range(2):
            for (a, n) in [(0, 512), (512, 512), (1024, 512), (1536, 512), (2048, 256)]:
                b = sb_i[0] % 3; sb_i[0] += 1
                for kc in range(8):
                    mm(psum[:, b, 0:n], wu[:, kc, cc * 128:(cc + 1) * 128], hT[:, kc, P0 + a:P0 + a + n],
                       kc == 0, kc == 7, ["wu"], PB(b))
                S.op("act", lambda e, b=b, a=a, n=n: e.copy(out=uT[:, a:a + n], in_=psum[:, b, 0:n]),
                     reads=[PB(b)], writes=["uT"])
            N = NP_

            def tt(out_ap, a_ap, b_ap, rd, wr):
                S.op("dve", lambda e: e.tensor_tensor(out=out_ap, in0=a_ap, in1=b_ap, op=ALU.add), reads=rd, writes=wr)

            def dif(pr, src, inv_w, srckey, cc=cc, N=NP_):
                S.op("dve", lambda e: e.scalar_tensor_tensor(out=dT[pr, 128:N], in0=src[pr, 128:N], scalar=inv_w,
                                                             in1=uT[pr, 128:N], op0=ALU.mult, op1=ALU.subtract),
                     reads=[srckey, "uT"], writes=["dT"])
                S.op("dve", lambda e: e.tensor_tensor(out=t16[pr, :], in0=src[pr, 256:272], in1=invt[pr, cc, :],
                                                      op=ALU.mult), reads=[srckey, "invt"], writes=["t16"])
                S.op("dve", lambda e: e.tensor_tensor(out=dT[pr, 256:272], in0=t16[pr, :], in1=uT[pr, 256:272],
                                                      op=ALU.subtract), reads=["t16", "uT", "dT"], writes=["dT"])

            lo, hi = slice(0, 64), slice(64, 128)
            tt(sA[:, 1:N], uT[:, 1:N], uT[:, 0:N - 1], ["uT"], ["sA"])
            if cc == 0:
                tt(sB[hi, 3:N], sA[hi, 3:N], sA[hi, 1:N - 2], ["sA"], ["sB"])
                dif(lo, sA, 0.5, "sA")
                dif(hi, sB, 0.25, "sB")
            else:
                tt(sB[:, 3:N], sA[:, 3:N], sA[:, 1:N - 2], ["sA"], ["sB"])
                tt(sA[:, 7:N], sB[:, 7:N], sB[:, 3:N - 4], ["sB"], ["sA"])
                tt(sB[hi, 15:N], sA[hi, 15:N], sA[hi, 7:N - 8], ["sA"], ["sB"])
                dif(lo, sA, 0.125, "sA")
                dif(hi, sB, 0.0625, "sB")
            for (a, n) in [(0, 512), (512, 512), (1024, 512), (1536, 512), (2048, 128)]:
                b = sb_i[0] % 3; sb_i[0] += 1
                mm(psum[:, b, 0:n], wpbd[:, cc, :], dT[:, 128 + a:128 + a + n], True, True, ["wpbd", "dT"], PB(b))
                S.op("act", lambda e, b=b, a=a, n=n, cc=cc: e.activation(out=ypT[:, cc, a:a + n], in_=psum[:, b, 0:n],
                                                                         func=AF.Identity, scale=pscol[:, cc:cc + 1]),
                     reads=[PB(b), "pscol"], writes=["ypT"])

        S.barrier()
        wxq = A2[:, 0:8192].rearrange("p (k n) -> p k n", k=8)
        wxo = A2[:, 8208:16400].rearrange("p (k n) -> p k n", k=8)
        for k2 in range(8):
            pass
        load_gain(1, 1)
        for k2 in range(8):
            load_w(w_xq, k2 * 128, 1, 0, 1024, wxq[:, k2:k2 + 1, :], ("wxq", k2 // 2))
        wmk = [("wmix", k2) for k2 in range(4)]

        def post_norm_residual(ybanks, gi, xsrc, xsrc_reads, dst, dst_key):
            yv = psum[:, ybanks[0]:ybanks[0] + 2, :].rearrange("p a b -> p (a b)")
            yk = [PB(ybanks[0]), PB(ybanks[1])]
            col, key = rstd_of(yv, yk)
            S.op("dve", lambda e: e.scalar_tensor_tensor(out=tmpf, in0=yv, scalar=col, in1=gt[gi][:],
                                                         op0=ALU.mult, op1=ALU.mult),
                 reads=yk + [key, ("gt", gi)], writes=["tmpf"])
            S.op("dve", lambda e: e.tensor_tensor(out=dst, in0=tmpf, in1=xsrc, op=ALU.add),
                 reads=["tmpf"] + list(xsrc_reads), writes=[dst_key])

        nB = NQB if stage >= 1 else 0
        yT2 = A2[:, 33856:34624].rearrange("p (c t) -> p c t", c=6)
        yTb = [yT, yT2]

        def b_T(qb):
            for c in range(6):
                S.op("pe", lambda e, c=c: e.transpose(out=pst[:, c * 128:(c + 1) * 128],
                                                      in_=ymix[:, qb, c * 128:(c + 1) * 128], identity=ident[:]),
                     reads=[("ymix", qb), "ident"], writes=[PB(7)], accum=True)
            S.op("act", lambda e: e.copy(out=yTb[qb % 2], in_=pst[:, 0:768].rearrange("p (c t) -> p c t", c=6)),
                 reads=[PB(7)], writes=[("yT", qb % 2)])

        def b_mm(qb):
            yb = (0, 1) if qb % 2 == 0 else (2, 3)
            for n in range(2):
                for kc in range(8):
                    lhsT = ypT[:, kc, qb * 128:(qb + 1) * 128] if kc < 2 else yTb[qb % 2][:, kc - 2, :]
                    wsl = (wmixA[:, kc, n * 512:(n + 1) * 512] if kc < 4 else
                           wmixC[:, kc - 4, n * 512:(n + 1) * 512] if kc < 7 else wmixD[:, n * 512:(n + 1) * 512])
                    mm(psum[:, yb[n], :], lhsT, wsl, kc == 0, kc == 7,
                       ["ypT", ("yT", qb % 2), ("wmix", kc // 2)], PB(yb[n]))
            i = qb % 2
            S.dma(lambda e: e.dma_start(out=xl[i][:], in_=xk[QOFF + qb * 128:QOFF + (qb + 1) * 128, :]),
                  writes=[("xl", i)])
            post_norm_residual(yb, 1, xl[i][:], [("xl", i)], xr(qb), ("xres", qb))

        if nB:
            b_T(0)
        for qb in range(nB):
            if qb + 1 < nB:
                b_T(qb + 1)
            b_mm(qb)

        if stage == 1:
            import os
            if os.environ.get("BAR"):
                S.barrier()
            toks = []
            for qb in range(1, NQB):
                toks.append(S.dma(lambda e, qb=qb: e.dma_start(out=out[(qb - 1) * 128:qb * 128, :], in_=xr(qb)),
                                  reads=[("xres", qb)], writes=[("outd", qb)]))
            S.barrier()
            S.emit()
            return nc

        rstd_mode[0] = "pow"
        S.alias([("wxo", k) for k in range(4)], [("wmix", k) for k in range(4)])
        S.alias([("h2T", i) for i in range(4)] + [("q2T", i) for i in range(8)] + [("oc", i) for i in range(4)],
                [("ymix", qb) for qb in range(NQB)])
        S.alias([("PTx", i) for i in range(4)] + [("ocT", 0), ("ocT", 1)], ["wf", "wu", ("wmix", 2), ("wmix", 3)] + [("wpair", i) for i in range(3)])
        o = 0
        PTx = [A3[:, o + i * 512:o + (i + 1) * 512] for i in range(4)]; o += 2048
        ocT = A3[:, o:o + 1024].rearrange("p (c t) -> p c t", c=8); o += 1024
        ocT2 = A3[:, o:o + 1024].rearrange("p (c t) -> p c t", c=8); o += 1024
        for k2 in range(8):
            load_w(w_xo, k2 * 128, 1, 0, 1024, wxo[:, k2:k2 + 1, :], ("wxo", k2 // 2))
        load_gain(0, 3)
        load_gain(1, 4)
        h2T = A2[:, 16400:20496].rearrange("p (k t) -> p k t", k=8)
        q2T = A2[:, 20496:24592].rearrange("p (k t) -> p k t", k=8)
        oc = A2[:, 24592:28688].rearrange("p (b d) -> p b d", b=4)
        wq_k = [("wxq", k2) for k2 in range(4)]
        wo_k = [("wxo", k2) for k2 in range(4)]
        px_i = [0]
        ob_i = [0]
        out_tokens = []
        chunks = [(0, 1), (1, 4), (5, 4), (9, 4), (13, 4)]
        ocTb = [ocT, ocT2]
        oct_i = [0]
        yb_i = [0]

        def c_norms(ci):
            qb0, nqb = chunks[ci]
            norm_seq([(None, xr(qb0 + lb), [("xres", qb0 + lb)], 0,
                       (lambda lb=lb: h2T[:, :, lb * 128:(lb + 1) * 128]), ("h2T", lb), None) for lb in range(nqb)])

        def c_qproj(ci, fc):
            qb0, nqb = chunks[ci]
            N = nqb * 128
            h2k = [("h2T", lb) for lb in range(nqb)]
            b = sb_i[0] % 3; sb_i[0] += 1
            for kc in range(8):
                mm(psum[:, b, 0:N], wxq[:, kc, fc * 128:(fc + 1) * 128], h2T[:, kc, 0:N], kc == 0, kc == 7,
                   h2k + [("wxq", kc // 2)], PB(b))
            S.op("act", lambda e: e.activation(out=q2T[:, fc, 0:N], in_=psum[:, b, 0:N], func=AF.Copy, scale=0.0625),
                 reads=[PB(b)], writes=[("q2T", fc)])

        def c_attn(ci):
            qb0, nqb = chunks[ci]
            N = nqb * 128

            def s_stage(hh):
                pis = []
                for mb in range(2):
                    b = sb_i[0] % 3; sb_i[0] += 1
                    for d in range(2):
                        mm(psum[:, b, 0:N], kTm[:, 2 * hh + d, mb * 128:(mb + 1) * 128], q2T[:, 2 * hh + d, 0:N],
                           d == 0, d == 1, ["kTm", ("q2T", 2 * hh + d)], PB(b))
                    pi = px_i[0] % 4; px_i[0] += 1
                    pis.append(pi)
                    S.op("act", lambda e, b=b, pi=pi: e.activation(out=PTx[pi][:, 0:N], in_=psum[:, b, 0:N], func=AF.Exp),
                         reads=[PB(b)], writes=[("PTx", pi)])
                return pis

            def pv_stage(hh, pis):
                for lb in range(nqb):
                    ob = 3 + (ob_i[0] % 4); ob_i[0] += 1
                    for mb in range(2):
                        mm(psum[:, ob, 0:257], PTx[pis[mb]][:, lb * 128:(lb + 1) * 128], Vm[:, mb, hh, :],
                           mb == 0, mb == 1, [("PTx", pis[mb]), "Vm", "Vmones"], PB(ob))
                    col, key = rl_col()
                    S.op("dve", lambda e, ob=ob, col=col: e.reciprocal(out=col, in_=psum[:, ob, 256:257]),
                         reads=[PB(ob)], writes=[key])
                    S.op("dve", lambda e, ob=ob, col=col, lb=lb: e.tensor_scalar(
                        out=oc[:, lb, hh * 256:(hh + 1) * 256], in0=psum[:, ob, 0:256], scalar1=col, scalar2=None,
                        op0=ALU.mult), reads=[PB(ob), key], writes=[("oc", lb)])

            prev = None
            for hh in range(4):
                pis = s_stage(hh)
                if prev is not None:
                    pv_stage(*prev)
                prev = (hh, pis)
            pv_stage(*prev)

        def c_out_T(ci, lb):
            k = oct_i[0] % 2; oct_i[0] += 1
            for c in range(8):
                S.op("pe", lambda e, c=c: e.transpose(out=pst[:, c * 128:(c + 1) * 128],
                                                      in_=oc[:, lb, c * 128:(c + 1) * 128], identity=ident[:]),
                     reads=[("oc", lb), "ident"], writes=[PB(7)], accum=True)
            S.op("act", lambda e: e.copy(out=ocTb[k], in_=pst.rearrange("p (c t) -> p c t", c=8)),
                 reads=[PB(7)], writes=[("ocT", k)])
            return k

        def c_out_mm(ci, lb, k):
            qb0, nqb = chunks[ci]
            qb = qb0 + lb
            yb = (5, 6) if yb_i[0] % 2 == 0 else (3, 4)
            yb_i[0] += 1
            for n in range(2):
                for kc in range(8):
                    mm(psum[:, yb[n], :], ocTb[k][:, kc, :], wxo[:, kc, n * 512:(n + 1) * 512], kc == 0, kc == 7,
                       [("ocT", k), ("wxo", kc // 2)], PB(yb[n]))
            post_norm_residual(yb, 1, xr(qb), [("xres", qb)], xr(qb), ("xres", qb))
            if qb >= 1:
                out_tokens.append(S.dma(lambda e: e.dma_start(out=out[(qb - 1) * 128:qb * 128, :], in_=xr(qb)),
                                        reads=[("xres", qb)], writes=[("outd", qb - 1)]))

        c_norms(0)
        for fc in range(8):
            c_qproj(0, fc)
        c_attn(0)
        h3T = A2[:, 0:8208].rearrange("p (k t) -> p k t", k=8)
        load_gain(2, 5)
        h3_items = [(None, xhalo[:], [("xres", 0)], 2, (lambda: h3T[:, :, 0:2]), ("h3T", -1), (126, 128))]
        for lb in range(8):
            h3_items.append((None, xr(1 + lb), [("xres", 1 + lb)], 2,
                             (lambda lb=lb: h3T[:, :, 2 + lb * 128:2 + (lb + 1) * 128]), ("h3T", lb), None))
        for ci in range(len(chunks)):
            nqb = chunks[ci][1]
            nxt = ci + 1 < len(chunks)
            if nxt:
                c_norms(ci + 1)
            else:
                S.alias([("h3T", lb) for lb in range(-1, 8)], [("wxq", k_) for k_ in range(4)])
            fcs = list(range(8))
            per = (8 + nqb - 1) // nqb
            for lb in range(nqb):
                k = c_out_T(ci, lb)
                if nxt:
                    for fc in fcs[lb * per:(lb + 1) * per]:
                        c_qproj(ci + 1, fc)
                elif stage >= 4:
                    norm_seq(h3_items[lb * 3:(lb + 1) * 3])
                c_out_mm(ci, lb, k)
            if nxt:
                c_attn(ci + 1)
        if stage >= 4:
            S.op("pool", lambda e: e.tensor_copy(out=h3h[:, :, :], in_=h3T[:, :, 1024:1026]),
                 reads=[("h3T", 7)], writes=["h3h"])
        if stage == 2:
            S.barrier()
            S.emit()
            return nc

        S.barrier()
        o = 0
        o += 8208
        wupb = [A2[:, o + i * 2048:o + (i + 1) * 2048].rearrange("p (g k n) -> p g k n", g=2, k=8) for i in range(2)]
        o += 4096
        wdnb = [A2[:, o + i * 512:o + (i + 1) * 512] for i in range(8)]
        o += 4096
        ylo = A3[:, 0:8192].bitcast(F32).rearrange("p (b n) -> p b n", b=8)
        yhb = [A3[:, 8192 + i * 1024:8192 + (i + 1) * 1024].bitcast(F32) for i in range(4)]
        yh_i = [0]
        hgb = [A2[:, o + i * 2052:o + (i + 1) * 2052].bitcast(F32) for i in range(2)]; o += 4104
        hub = [A2[:, o + i * 2052:o + (i + 1) * 2052].bitcast(F32) for i in range(2)]; o += 4104
        cvg = [A2[:, o + i * 1024:o + (i + 1) * 1024].bitcast(F32) for i in range(2)]; o += 2048
        cvu = [A2[:, o + i * 1024:o + (i + 1) * 1024].bitcast(F32) for i in range(2)]; o += 2048
        glb = [A2[:, o + i * 1024:o + (i + 1) * 1024].bitcast(F32) for i in range(2)]; o += 2048
        cwr = A2[0:64, o:o + 1024].bitcast(F32).rearrange("p (k n) -> p k n", k=4); o += 1024
        assert o <= 34816
        load_gain(0, 6)
        xpre = [(A3[:, i * 2048:(i + 1) * 2048].bitcast(F32), ("xpre", i)) for i in range(6)] + \
               [(xl[0][:], ("xl", 0)), (xl[1][:], ("xl", 1))]
        S.dma(lambda e: e.dma_start(out=cwr[:, 0:3, :], in_=conv_w.rearrange("k (c p) -> c k p", p=128)), writes=["cwr"])
        S.dma(lambda e: e.dma_start(out=cwr[:, 3, :], in_=conv_b.rearrange("k (c p) -> (k c) p", p=128)),
              writes=["cwr"])
        for k in range(4):
            S.op("pe", lambda e, k=k: e.transpose(out=psum[:, 0, k * 64:(k + 1) * 64], in_=cwr[:, k, :],
                                                  identity=identf[0:64, 0:64]),
                 reads=["cwr", "identf"], writes=[PB(0)], accum=True)
        S.op("act", lambda e: e.copy(out=cw[:], in_=psum[:, 0, 0:256].rearrange("p (k n) -> p k n", k=4)),
             reads=[PB(0)], writes=["cw"])

        it = [0]
        wu_i = [0]
        wd_i = [0]
        def ffn_h3T(th):
            if th == 1:
                S.op("pool", lambda e: e.tensor_copy(out=h3T[:, :, 0:2], in_=h3h[:, :, :]),
                     reads=["h3h"], writes=[("h3T", -1)])
            items = []
            for lb in range(-1 if th == 0 else 0, 8):
                qb = 1 + 8 * th + lb
                i = (lb + 1) % 2
                if qb == 0:
                    pre, src, srck = None, xhalo[:], [("xres", 0)]
                elif th == 1:
                    pre = None
                    src, srck = xpre[lb][0], [xpre[lb][1]]
                else:
                    pre = (lambda qb=qb, i=i: S.dma(lambda e: e.dma_start(out=xl[i][:], in_=out[(qb - 1) * 128:qb * 128, :]),
                                                    reads=[("outd", qb - 1)], writes=[("xl", i)]))
                    src, srck = xl[i][:], [("xl", i)]
                if lb == -1:
                    items.append((pre, src, srck, 2, (lambda: h3T[:, :, 0:2]), ("h3T", -1), (126, 128)))
                else:
                    items.append((pre, src, srck, 2, (lambda lb=lb: h3T[:, :, 2 + lb * 128:2 + (lb + 1) * 128]),
                                  ("h3T", lb), None))
            norm_seq(items)
            if th == 1:
                S.alias([("ylo", i) for i in range(8)] + [("yhb", i) for i in range(4)], [("xpre", i) for i in range(6)])
            if th == 0:
                S.op("pool", lambda e: e.tensor_copy(out=h3h[:, :, :], in_=h3T[:, :, 1024:1026]),
                     reads=[("h3T", 7)], writes=["h3h"])
        def load_up(fc):
            wb = fc % 2
            load_w(w_up, 0, 8, fc * 128, 128, wupb[wb][:, 0, :, :], ("wup", wb, 0))
            load_w(w_up, 0, 8, 4096 + fc * 128, 128, wupb[wb][:, 1, :, :], ("wup", wb, 1))

        def ffn_up(th, preloaded=0, deferred=(), xprefetch=False):
            pending = []
            if preloaded < 1:
                load_up(0)
            for fc in range(32):
                wb = fc % 2
                if fc == 3:
                    for d_ in deferred:
                        d_()
                if xprefetch and 4 <= fc < 12:
                    lb_ = fc - 4
                    buf_, bkey_ = xpre[lb_]
                    S.dma(lambda e, lb_=lb_, buf_=buf_: e.dma_start(out=buf_, in_=out[(8 + lb_) * 128:(9 + lb_) * 128, :]),
                          reads=[("outd", 8 + lb_), ("actT", fc - 2, 1)], writes=[bkey_])
                if fc + 1 < 32 and fc + 1 >= preloaded:
                    load_up(fc + 1)
                hbufs = (hgb[wb], hub[wb])
                for g in range(2):
                    for kc in range(8):
                        mm(psum[:, 4, 2 * g:2 * g + 2], wupb[wb][:, g, kc, :], h3T[:, kc, 0:2],
                           kc == 0, kc == 7, [("wup", wb, g), ("h3T", -1)], PB(4))
                fcol = flagt[:, 0:1] if th == 0 else flagt[:, 1:2]
                for g in range(2):
                    S.op("dve", lambda e, g=g, hbuf=hbufs[g], fcol=fcol: e.tensor_scalar(
                        out=hbuf[:, 0:2], in0=psum[:, 4, 2 * g:2 * g + 2], scalar1=fcol, scalar2=None,
                        op0=ALU.mult), reads=[PB(4), "flagt"], writes=[("hbuf", wb, g, "h")])
                for tc in range(2):
                    ii = it[0] % 2; it[0] += 1
                    bg, bu = 2 * ii, 2 * ii + 1
                    hk = [("h3T", lb) for lb in range(4 * tc, 4 * tc + 4)]
                    for g, bb in ((0, bg), (1, bu)):
                        for kc in range(8):
                            mm(psum[:, bb, :], wupb[wb][:, g, kc, :], h3T[:, kc, 2 + tc * 512:2 + (tc + 1) * 512],
                               kc == 0, kc == 7, [("wup", wb, g)] + hk, PB(bb))
                    cvs = (cvg[ii], cvu[ii])
                    c0 = tc * 512
                    for g, bb in ((0, bg), (1, bu)):
                        fcg = fc + 32 * g
                        S.op("act", lambda e, bb=bb, hbuf=hbufs[g], c0=c0: e.copy(out=hbuf[:, 2 + c0:514 + c0],
                                                                                 in_=psum[:, bb, :]),
                             reads=[PB(bb)], writes=[("hbuf", wb, g, tc)])
                        S.op("act", lambda e, bb=bb, cv=cvs[g], fcg=fcg: e.activation(
                            out=cv, in_=psum[:, bb, :], func=AF.Identity, scale=cw[:, 2, fcg:fcg + 1],
                            bias=cw[:, 3, fcg:fcg + 1]), reads=[PB(bb), "cw"], writes=[("cv", ii, g)])
                    prevk = "h" if tc == 0 else 0
                    for tap, sh in ((1, 1), (0, 0)):
                        for g in range(2):
                            fcg = fc + 32 * g
                            S.op("dve", lambda e, cv=cvs[g], hbuf=hbufs[g], fcg=fcg, tap=tap, sh=sh, c0=c0:
                                 e.scalar_tensor_tensor(out=cv, in0=hbuf[:, c0 + sh:c0 + sh + 512],
                                                        scalar=cw[:, tap, fcg:fcg + 1], in1=cv, op0=ALU.mult, op1=ALU.add),
                                 reads=[("hbuf", wb, g, tc), ("hbuf", wb, g, prevk), ("cv", ii, g), "cw"],
                                 writes=[("cv", ii, g)])

                    def tail(ii=ii, fc=fc, tc=tc):
                        S.op("act", lambda e: e.activation(out=glb[ii], in_=cvg[ii], func=AF.Gelu_apprx_tanh),
                             reads=[("cv", ii, 0)], writes=[("gl", ii)])
                        S.op("dve", lambda e: e.tensor_tensor(out=actT[:, fc, tc * 512:(tc + 1) * 512], in0=glb[ii],
                                                              in1=cvu[ii], op=ALU.mult),
                             reads=[("gl", ii), ("cv", ii, 1)], writes=[("actT", fc, tc)])

                    for t_ in pending:
                        t_()
                    pending = [tail]
            for t_ in pending:
                t_()
            pending = []
        obufs = [(xl[0][:], ("xl", 0)), (xl[1][:], ("xl", 1)), (tmpf, "tmpf"), (gt[1][:], ("gt", 1)), (gt[2][:], ("gt", 2))]
        ob_rot = [0]
        fin_th = [0]

        def fin(lb, cols, ybs):
            col, key = cols[lb]
            yb_ = ybs[lb]
            finish_rstd(col, key)
            ob = 8 * fin_th[0] + lb
            buf, bkey = obufs[ob_rot[0] % len(obufs)]; ob_rot[0] += 1
            S.op("dve", lambda e: e.scalar_tensor_tensor(
                out=buf[:, 0:512], in0=ylo[:, lb, :], scalar=col, in1=gt[0][:, 0:512],
                op0=ALU.mult, op1=ALU.mult), reads=[("ylo", lb), key, ("gt", 0)], writes=[bkey])
            S.op("dve", lambda e: e.scalar_tensor_tensor(
                out=buf[:, 512:1024], in0=yhb[yb_], scalar=col, in1=gt[0][:, 512:1024],
                op0=ALU.mult, op1=ALU.mult), reads=[("yhb", yb_), key, ("gt", 0), bkey], writes=[bkey])
            S.dma(lambda e: e.dma_start(out=out[ob * 128:(ob + 1) * 128, :], in_=buf, accum_op=ALU.add),
                  reads=[bkey], writes=[("outd", ob)], queue="pool")

        def ffn_down(th):
            fin_th[0] = th
            ssA = {}
            cols = {}
            ybs = {}

            def mm_pass(half, lbs):
                for kc in range(32):
                    db = wd_i[0] % 8; wd_i[0] += 1
                    S.dma(lambda e, kc=kc, db=db: e.dma_start(
                        out=wdnb[db], in_=w_down[kc * 128:(kc + 1) * 128, half * 512:(half + 1) * 512]),
                          writes=[("wdn", db)], queue="pool")
                    for lb in lbs:
                        mm(psum[:, lb, :], actT[:, kc, lb * 128:(lb + 1) * 128], wdnb[db], kc == 0, kc == 31,
                           [("actT", kc, lb // 4), ("wdn", db)], PB(lb))

            def epi(half, lbs):
                for lb in lbs:
                    col, key = rs_col()
                    cols[lb] = (col, key)
                    S.op("act", lambda e, lb=lb, col=col: e.activation(out=junk[:, 0:512], in_=psum[:, lb, :],
                                                                       func=AF.Square, accum_out=col),
                         reads=[PB(lb)], writes=[key])
                    if half == 0:
                        ssA[lb] = (col, key)
                        S.op("act", lambda e, lb=lb: e.copy(out=ylo[:, lb, :], in_=psum[:, lb, :]),
                             reads=[PB(lb)], writes=[("ylo", lb)])
                    else:
                        colA, keyA = ssA[lb]
                        yb_ = yh_i[0] % 4; yh_i[0] += 1
                        ybs[lb] = yb_
                        S.op("act", lambda e, lb=lb, yb_=yb_: e.copy(out=yhb[yb_], in_=psum[:, lb, :]),
                             reads=[PB(lb)], writes=[("yhb", yb_)])
                        S.op("dve", lambda e, col=col, colA=colA: e.tensor_tensor(out=col, in0=col, in1=colA, op=ALU.add),
                             reads=[key, keyA], writes=[key])
                if half == 1:
                    for lb in lbs:
                        fin(lb, cols, ybs)

            mm_pass(0, range(8))
            epi(0, range(8))
            mm_pass(1, range(0, 4))
            mm_pass(1, range(4, 8))
            epi(1, range(0, 4))
            epi(1, range(4, 8))
        def xpre_loads():
            pass

        load_up(0)
        ffn_up(0, preloaded=1, xprefetch=(stage >= 4))
        load_up(0)
        load_up(1)
        ffn_h3T(1)
        ffn_down(0)
        ffn_up(1, preloaded=2)
        ffn_down(1)
        S.barrier()
        S.emit()
    return nc


_CACHE = {}


def _prep_inputs(inp):
    f = lambda a: np.ascontiguousarray(np.asarray(a, dtype=np.float32))
    x = f(inp["x"]); mem = f(inp["mem"])
    gains = np.stack([f(inp[k])[0] for k in ("norm_mix_pre", "norm_mix_post", "norm_mem", "norm_xa_pre",
                                             "norm_xa_post", "norm_ffn_pre", "norm_ffn_post")], 0)
    shared = {
        "gains": gains, "w_in": f(inp["w_in"])[0], "b_forget": f(inp["b_forget"])[0].reshape(12, 1),
        "w_pool": f(inp["w_pool"])[0], "pool_scale": f(inp["pool_scale"])[0].reshape(256, 1),
        "w_mix_out": f(inp["w_mix_out"])[0], "w_xq": f(inp["w_xq"])[0], "w_xkv": f(inp["w_xkv"])[0],
        "w_xo": f(inp["w_xo"])[0], "w_up": f(inp["w_up"])[0], "conv_w": f(inp["conv_w"])[0].reshape(3, 8192),
        "conv_b": f(inp["conv_b"])[0].reshape(1, 8192), "w_down": f(inp["w_down"])[0],
    }
    bf = ml_dtypes.bfloat16
    in_maps = []
    wins = np.array([2, 4, 8, 16], np.float32)
    for c in range(8):
        b, half = c // 2, c % 2
        xk = np.zeros((NT, D), np.float32)
        if half == 1:
            xk[:2048] = x[b, :2048]
        xk[2048:] = x[b, half * 2048:(half + 1) * 2048]
        kconst = np.zeros((4, NT), np.float32)
        kconst[0:3] = -1.0
        if half == 0:
            kconst[3, :2048] = MASKV
        qconst = np.ones((4, NQ), np.float32)
        qconst[3, :128] = 0.0
        invc = np.zeros((2, 128, 16), np.float32)
        t1 = np.arange(1, 17, dtype=np.float32)
        for g in range(4):
            cc, hf = g // 2, g % 2
            if half == 0:
                invc[cc, hf * 64:(hf + 1) * 64, :] = 1.0 / np.minimum(t1, wins[g])[None, :]
            else:
                invc[cc, hf * 64:(hf + 1) * 64, :] = 1.0 / wins[g]
        flags = np.ones((128, 2), np.float32)
        flags[:, 0] = float(half)
        m = dict(shared)
        m.update({"xk": xk, "mem": mem[b], "kconst": kconst.astype(bf), "qconst": qconst.astype(bf),
                  "invc": invc, "flags": flags})
        in_maps.append(m)
    return in_maps


def kernel(**inputs):
    if "nc" not in _CACHE:
        _CACHE["nc"] = build()
    nc = _CACHE["nc"]
    in_maps = _prep_inputs(inputs)
    res = run_bass_kernel_spmd(nc, in_maps, core_ids=list(range(8)))
    outp = np.zeros((4, 4096, D), np.float32)
    for c in range(8):
        b, half = c // 2, c % 2
        outp[b, half * 2048:(half + 1) * 2048] = res.results[c]["out"]
    return outp
```

```python
import contextlib
import numpy as np
import ml_dtypes
import concourse.bass as bass
import concourse.mybir as mybir
from concourse.bass_utils import run_bass_kernel_spmd

F32 = mybir.dt.float32
BF16 = mybir.dt.bfloat16
AF = mybir.ActivationFunctionType
ALU = mybir.AluOpType

COMPUTE = ("pe", "act", "dve", "pool")
D = 1024
DIN = 2572
NT = 4096
NQ = 2176
QOFF = 1920
NQB = 17
MASKV = -30000.0


class Sched:
    def __init__(self, nc, es, n_dma_sems=48, same_engine_sync=True):
        self.nc = nc
        self.sem = {e: es.enter_context(nc.semaphore("s_" + e)) for e in COMPUTE}
        self.cnt = {e: 0 for e in COMPUTE}
        self.dsems = [es.enter_context(nc.semaphore("d%d" % i)) for i in range(n_dma_sems)]
        self.dcnt = [0] * n_dma_sems
        self.dnext = 0
        self.streams = {e: [] for e in COMPUTE + ("sp",)}
        self.waited = {e: {} for e in COMPUTE + ("sp",)}
        self.lastw = {}
        self.readers = {}
        self.same = same_engine_sync

    def _need(self, eng, token, waits, allow_same=True):
        if token is None:
            return
        key, val = token
        if key == eng and not (self.same and allow_same):
            return
        if self.waited[eng].get(key, 0) >= val:
            return
        self.waited[eng][key] = val
        waits.append(token)

    def _deps(self, eng, reads, writes, waits, accum=False):
        for r in reads:
            self._need(eng, self.lastw.get(r), waits)
        for w in writes:
            self._need(eng, self.lastw.get(w), waits, allow_same=not accum)
            for k, v in self.readers.get(w, {}).items():
                if k != eng:
                    self._need(eng, (k, v), waits)

    def _commit(self, token, reads, writes):
        k, v = token
        for r in reads:
            d = self.readers.setdefault(r, {})
            if d.get(k, 0) < v:
                d[k] = v
        for w in writes:
            self.lastw[w] = token
            self.readers[w] = {}

    def op(self, eng, fn, reads=(), writes=(), accum=False):
        waits = []
        self._deps(eng, reads, writes, waits, accum)
        self.cnt[eng] += 1
        token = (eng, self.cnt[eng])
        self.streams[eng].append((fn, waits, None))
        self._commit(token, reads, writes)
        return token

    def dma(self, fn, reads=(), writes=(), queue="sp"):
        waits = []
        self._deps(queue, reads, writes, waits)
        i = self.dnext
        self.dnext = (self.dnext + 1) % len(self.dsems)
        key = ("d", i)
        if self.dcnt[i] > 0:
            self._need(queue, (key, 16 * self.dcnt[i]), waits)
        self.dcnt[i] += 1
        token = (key, 16 * self.dcnt[i])
        self.streams[queue].append((fn, waits, i))
        self._commit(token, reads, writes)
        return token

    def alias(self, new_keys, old_keys):
        toks = {}
        for ok in old_keys:
            lw = self.lastw.get(ok)
            if lw is not None:
                toks[lw[0]] = max(toks.get(lw[0], 0), lw[1])
            for k, v in self.readers.get(ok, {}).items():
                toks[k] = max(toks.get(k, 0), v)
        for nk in new_keys:
            d = self.readers.setdefault(nk, {})
            for k, v in toks.items():
                d[k] = max(d.get(k, 0), v)

    def barrier(self):
        toks = [(e, self.cnt[e]) for e in COMPUTE if self.cnt[e] > 0]
        toks += [(("d", i), 16 * c) for i, c in enumerate(self.dcnt) if c > 0]
        for eng in COMPUTE + ("sp",):
            waits = []
            for t in toks:
                self._need(eng, t, waits)
            if waits:
                self.streams[eng].append((None, waits, None))

    def semh(self, key):
        if isinstance(key, tuple):
            return self.dsems[key[1]]
        return self.sem[key]

    def replay(self, eng, e):
        for fn, waits, di in self.streams[eng]:
            for key, val in waits:
                e.wait_ge(self.semh(key), val)
            if fn is None:
                continue
            inst = fn(e)
            if di is not None:
                inst.then_inc(self.dsems[di], 16)
            else:
                inst.then_inc(self.sem[eng], 1)

    def emit(self):
        with self.nc.Block() as block:
            @block.tensor
            def _(e):
                self.replay("pe", e)

            @block.scalar
            def _(e):
                self.replay("act", e)

            @block.vector
            def _(e):
                self.replay("dve", e)

            @block.gpsimd
            def _(e):
                self.replay("pool", e)

            @block.sync
            def _(e):
                self.replay("sp", e)


def build(stage=4):
    nc = bass.Bass("TRN2", target_bir_lowering=False)

    def dt(name, shape, dty=F32, kind="ExternalInput"):
        return nc.dram_tensor(name, shape, dty, kind=kind).ap()

    xk = dt("xk", [NT, D])
    memd = dt("mem", [256, D])
    gains = dt("gains", [7, D])
    w_in = dt("w_in", [D, DIN])
    b_f = dt("b_forget", [12, 1])
    w_pool = dt("w_pool", [4, 64, 64])
    pscale = dt("pool_scale", [256, 1])
    w_mix = dt("w_mix_out", [D, D])
    w_xq = dt("w_xq", [D, D])
    w_xkv = dt("w_xkv", [D, 2 * D])
    w_xo = dt("w_xo", [D, D])
    w_up = dt("w_up", [D, 8192])
    conv_w = dt("conv_w", [3, 8192])
    conv_b = dt("conv_b", [1, 8192])
    w_down = dt("w_down", [4096, D])
    kconst = dt("kconst", [4, NT], BF16)
    qconst = dt("qconst", [4, NQ], BF16)
    invc = dt("invc", [2, 128, 16])
    flags = dt("flags", [128, 2])
    out = dt("out", [2048, D], kind="ExternalOutput")
    gsc = dt("gsc", [12, 3, NT], BF16, kind="Internal")

    with contextlib.ExitStack() as es:
        S = Sched(nc, es)

        def sb(name, shape, dty):
            return es.enter_context(nc.sbuf_tensor(name, shape, dty))

        A1 = sb("A1", [128, 32768], BF16)
        A2 = sb("A2", [128, 34816], BF16)
        A3 = sb("A3", [128, 15360], BF16)
        xl = [sb("xl%d" % i, [128, D], F32) for i in range(2)]
        hb = [sb("hb%d" % i, [128, D], BF16) for i in range(3)]
        junk = sb("junk", [128, D], BF16)
        gt = [sb("gt%d" % i, [128, D], F32) for i in range(3)]
        xhalo = sb("xhalo", [128, D], F32)
        ident = sb("ident", [128, 128], BF16)
        identf = sb("identf", [128, 128], F32)
        cbias = sb("cbias", [128, 128], BF16)
        cbf = sb("cbf", [128, 128], F32)
        rs = sb("rs", [128, 64], F32)
        rl = sb("rl", [128, 16], F32)
        flagt = sb("flagt", [128, 2], F32)
        pscol = sb("pscol", [128, 2], F32)
        negb = sb("negb", [12, 1], F32)
        epst = sb("epst", [128, 1], F32)
        pmk = sb("pmk", [128, 4], F32)
        cw = sb("cw", [128, 4, 64], F32)
        memkv = sb("memkv", [128, 4104], BF16)
        invt = sb("invt", [128, 2, 16], F32)
        h3h = sb("h3h", [128, 8, 2], BF16)
        psum = es.enter_context(nc.psum_tensor("psum", [128, 8, 512], F32))
        pst = psum[:, 7, :].bitcast(BF16)
        PB = lambda b: ("ps", b)

        hT = A1[:].rearrange("p (c t) -> p c t", c=8)
        xres = A1[:].bitcast(F32).rearrange("p (b d) -> p b d", b=16)
        actT = A1[:].rearrange("p (c t) -> p c t", c=32)

        KP = A2[:, 0:8192].rearrange("p (s t) -> p s t", s=2)
        QP = A2[:, 8192:12544].rearrange("p (s t) -> p s t", s=2)
        VP = A2[:, 12544:16704].rearrange("p (b s d) -> p b s d", b=32, s=2)
        ymix = A2[:, 16704:16704 + 17 * 768].rearrange("p (b d) -> p b d", b=17)

        def xr(qb):
            return xhalo[:] if qb == 0 else xres[:, qb - 1, :]

        rs_i = [0]

        def rs_col():
            rs_i[0] = (rs_i[0] + 1) % 64
            return rs[:, rs_i[0]:rs_i[0] + 1], ("rs", rs_i[0])

        rl_i = [0]

        def rl_col():
            rl_i[0] = (rl_i[0] + 1) % 16
            return rl[:, rl_i[0]:rl_i[0] + 1], ("rl", rl_i[0])

        def finish_rstd(col, key):
            S.op("act", lambda e: e.activation(out=col, in_=col, func=AF.Sqrt, scale=1.0 / D, bias=epst[:, 0:1]),
                 reads=[key, "epst"], writes=[key])
            S.op("dve", lambda e: e.reciprocal(out=col, in_=col), reads=[key], writes=[key])

        def rstd_of(src, src_reads):
            col, key = rs_col()
            S.op("act", lambda e: e.activation(out=junk[:], in_=src, func=AF.Square, accum_out=col),
                 reads=src_reads, writes=[key])
            finish_rstd(col, key)
            return col, key

        def load_gain(i, gi):
            S.dma(lambda e: e.dma_start(out=gt[i][:], in_=gains[gi:gi + 1, :].partition_broadcast(128)),
                  writes=[("gt", i)])

        hb_i = [0]
        sb_i = [0]

        ev_i = [0]

        def norm_p1(src, src_reads, gi):
            col, key = rstd_of(src, src_reads)
            hb_i[0] = (hb_i[0] + 1) % 3
            i = hb_i[0]
            S.op("dve", lambda e: e.scalar_tensor_tensor(out=hb[i][:], in0=src, scalar=col, in1=gt[gi][:],
                                                         op0=ALU.mult, op1=ALU.mult),
                 reads=list(src_reads) + [key, ("gt", gi)], writes=[("hb", i)])
            return i

        def norm_p2(i, dst_fn, dst_key, sel=None):
            for c in range(8):
                S.op("pe", lambda e, c=c: e.transpose(out=pst[:, c * 128:(c + 1) * 128],
                                                      in_=hb[i][:, c * 128:(c + 1) * 128], identity=ident[:]),
                     reads=[("hb", i), "ident"], writes=[PB(7)], accum=True)
            src_v = pst.rearrange("p (c t) -> p c t", c=8)
            if sel is not None:
                src_v = src_v[:, :, sel[0]:sel[1]]
            ev_i[0] ^= 1
            if ev_i[0]:
                S.op("act", lambda e: e.copy(out=dst_fn(), in_=src_v), reads=[PB(7)], writes=[dst_key])
            else:
                S.op("dve", lambda e: e.tensor_copy(out=dst_fn(), in_=src_v), reads=[PB(7)], writes=[dst_key])

        def norm_seq(items):
            pend = []
            for (pre, src, src_reads, gi, dst_fn, dst_key, sel) in items:
                if pre is not None:
                    pre()
                i = norm_p1(src, src_reads, gi)
                pend.append((i, dst_fn, dst_key, sel))
                if len(pend) > 2:
                    norm_p2(*pend.pop(0))
            for p_ in pend:
                norm_p2(*p_)

        def load_w(dram, r0, nk, c0, ncols, dst, dst_key, eng=None, after=()):
            S.dma(lambda e: e.dma_start(out=dst, in_=dram[r0:r0 + nk * 128, c0:c0 + ncols]
                                        .rearrange("(k p) n -> p k n", p=128)), reads=list(after), writes=[dst_key],
                  queue="pool")

        def mm(out_ap, lhsT, rhs, start, stop, reads, wkey):
            S.op("pe", lambda e: e.matmul(out=out_ap, lhsT=lhsT, rhs=rhs, start=start, stop=stop),
                 reads=reads, writes=[wkey], accum=True)

        S.op("pool", lambda e: e.memset(epst[:], 1e-6), writes=["epst"])
        for (pr_, vals_) in ((slice(0, 64), (0.0, 0.5, 0.125)), (slice(64, 128), (1.0, 0.25, 0.0625))):
            for c_, v_ in enumerate(vals_):
                S.op("dve", lambda e, pr_=pr_, c_=c_, v_=v_: e.memset(pmk[pr_, c_:c_ + 1], v_), writes=["pmk"])
        S.op("pool", lambda e: e.memset(identf[:], 0.0), writes=["identf"])
        S.op("pool", lambda e: e.affine_select(out=identf[:], in_=identf[:], pattern=[[-1, 128]],
                                               compare_op=ALU.not_equal, fill=1.0, base=0, channel_multiplier=1),
             reads=["identf"], writes=["identf"])
        S.op("pool", lambda e: e.tensor_copy(out=ident[:], in_=identf[:]), reads=["identf"], writes=["ident"])
        S.op("pool", lambda e: e.memset(cbf[:], 0.0), writes=["cbf"])
        S.op("pool", lambda e: e.affine_select(out=cbf[:], in_=cbf[:], pattern=[[1, 128]],
                                               compare_op=ALU.is_ge, fill=MASKV, base=0, channel_multiplier=-1),
             reads=["cbf"], writes=["cbf"])
        S.op("pool", lambda e: e.tensor_copy(out=cbias[:], in_=cbf[:]), reads=["cbf"], writes=["cbias"])
        S.dma(lambda e: e.dma_start(out=flagt[:], in_=flags[:, :]), writes=["flagt"])
        S.dma(lambda e: e.dma_start(out=negb[:], in_=b_f[:, :]), writes=["negb"])
        S.op("act", lambda e: e.mul(out=negb[:], in_=negb[:], mul=-1.0), reads=["negb"], writes=["negb"])
        for cc in range(2):
            S.dma(lambda e, cc=cc: e.dma_start(out=pscol[:, cc:cc + 1], in_=pscale[cc * 128:(cc + 1) * 128, :]),
                  writes=["pscol"])

        o = 0
        wf = A3[:, o:o + 96].rearrange("p (k n) -> p k n", k=8); o += 96
        wpair = A3[:, o:o + 3072].rearrange("p (k w n) -> p k w n", k=8, w=3); o += 3072
        wu = A3[:, o:o + 2048].rearrange("p (k n) -> p k n", k=8); o += 2048
        wpbd = A3[:, o:o + 256].rearrange("p (c n) -> p c n", c=2); o += 256
        PT = [A3[:, o + i * 512:o + (i + 1) * 512] for i in range(3)]; o += 1536
        vtmp = A3[:, o:o + 512]; o += 512
        ypT = A3[:, o:o + 4352].rearrange("p (c t) -> p c t", c=2); o += 4352
        go = 16704
        g_et = A2[0:12, go:go + 1024].bitcast(F32); go += 1024
        g_sps = [A2[0:12, go + i * 1024:go + (i + 1) * 1024].bitcast(F32) for i in range(8)]; go += 8192
        g_G = [A2[0:12, go + i * 1024:go + (i + 1) * 1024].bitcast(F32) for i in range(2)]; go += 2048
        g_r1 = A2[0:12, go:go + 1024].bitcast(F32); go += 1024
        g_r2 = A2[0:12, go:go + 1024].bitcast(F32); go += 1024
        g_ones = A2[0:12, go:go + 512]; go += 512
        g_parts = [A2[0:12, go + i * 1536:go + (i + 1) * 1536].rearrange("p (a t) -> p a t", a=3) for i in range(2)]
        go += 3072
        wpst = A3[:, o:o + 512].bitcast(F32).rearrange("p (c n) -> p c n", c=2); o += 512
        yT = A3[:, o:o + 768].rearrange("p (c t) -> p c t", c=6); o += 768
        tmpf = A3[:, o:o + 2048].bitcast(F32); o += 2048
        assert o <= 15360, o

        G_KEYS = ["g_et"] + [("g_sp", i) for i in range(8)] + [("gG", 0), ("gG", 1), "g_r1", "g_r2", "g_ones", ("gp", 0), ("gp", 1)]
        hTk = lambda t0, n: [("hT", b) for b in range(t0 // 128, (t0 + n + 127) // 128)]
        load_gain(0, 0)
        load_gain(2, 2)
        wxk = A2[:, 16704:24896].rearrange("p (k n) -> p k n", k=8)
        wxv = A2[:, 24896:33088].rearrange("p (k n) -> p k n", k=8)
        memT = xhalo[:].bitcast(BF16).rearrange("p (k t) -> p k t", k=8)
        kTm = memkv[:, 0:2048].rearrange("p (k t) -> p k t", k=8)
        Vm = memkv[:, 2048:4104].rearrange("p (m h d) -> p m h d", m=2, h=4)

        def pair_weights(pair):
            for which in range(3):
                load_w(w_in, 0, 8, 256 + which * 768 + pair * 128, 128, wpair[:, :, which, :], ("wpair", which))

        S.op("dve", lambda e: e.memset(Vm[:, :, :, 256:257], 1.0), writes=["Vmones"])
        items = []
        for mb in range(2):
            i = mb % 2
            pre = (lambda mb=mb, i=i: S.dma(lambda e: e.dma_start(out=xl[i][:], in_=memd[mb * 128:(mb + 1) * 128, :]),
                                            writes=[("xl", i)]))
            items.append((pre, xl[i][:], [("xl", i)], 2, (lambda mb=mb: memT[:, :, mb * 128:(mb + 1) * 128]),
                          ("memT", mb), None))
        norm_seq(items)
        load_w(w_in, 0, 8, 2560, 12, wf, "wf", after=[("memT", 1)])
        load_w(w_in, 0, 8, 0, 128, wu[:, :, 0:128], "wu")
        load_w(w_in, 0, 8, 128, 128, wu[:, :, 128:256], "wu")
        S.op("dve", lambda e: e.memset(wpst[:], 0.0), writes=["wpst"])
        for g in range(4):
            cc, hf = g // 2, g % 2
            S.dma(lambda e, g=g, cc=cc, hf=hf: e.dma_start(out=wpst[hf * 64:(hf + 1) * 64, cc, hf * 64:(hf + 1) * 64],
                                                           in_=w_pool[g, :, :]), reads=["wpst"], writes=["wpst"])
        for cc in range(2):
            S.dma(lambda e, cc=cc: e.dma_start(out=invt[:, cc, :], in_=invc[cc, :, :]), writes=["invt"])
        pair_weights(0)
        for k2 in range(8):
            load_w(w_xkv, k2 * 128, 1, 0, 1024, wxk[:, k2:k2 + 1, :], ("wxk", k2 // 2))
            load_w(w_xkv, k2 * 128, 1, 1024, 1024, wxv[:, k2:k2 + 1, :], ("wxv", k2 // 2))
        S.op("pool", lambda e: e.tensor_copy(out=wpbd, in_=wpst), reads=["wpst"], writes=["wpbd"])
        items = []
        xs = A2[:, 0:16384].bitcast(F32).rearrange("p (r d) -> p r d", r=8)
        for kb in range(32):
            i = kb % 8
            pre = (lambda kb=kb, i=i: S.dma(lambda e: e.dma_start(out=xs[:, i, :], in_=xk[kb * 128:(kb + 1) * 128, :]),
                                            reads=([("xs", (kb - 3) % 8)] if kb >= 3 else []), writes=[("xs", i)]))
            items.append((pre, xs[:, i, :], [("xs", i)], 0, (lambda kb=kb: hT[:, :, kb * 128:(kb + 1) * 128]),
                          ("hT", kb), None))
        norm_seq(items[:16])
        mTk = [("memT", 0), ("memT", 1)]
        for fc in range(8):
            b = sb_i[0] % 3; sb_i[0] += 1
            for kc in range(8):
                mm(psum[:, b, 0:256], wxk[:, kc, fc * 128:(fc + 1) * 128], memT[:, kc, :], kc == 0, kc == 7,
                   mTk + [("wxk", kc // 2)], PB(b))
            S.op("act", lambda e, b=b, fc=fc: e.copy(out=kTm[:, fc, :], in_=psum[:, b, 0:256]),
                 reads=[PB(b)], writes=["kTm"])
        for mb in range(2):
            for n in range(2):
                b = sb_i[0] % 3; sb_i[0] += 1
                for kc in range(8):
                    mm(psum[:, b, :], memT[:, kc, mb * 128:(mb + 1) * 128], wxv[:, kc, n * 512:(n + 1) * 512],
                       kc == 0, kc == 7, mTk + [("wxv", kc // 2)], PB(b))
                S.op("act", lambda e, b=b, mb=mb, n=n: e.copy(out=Vm[:, mb, 2 * n:2 * n + 2, 0:256],
                                                              in_=psum[:, b, :].rearrange("p (h d) -> p h d", h=2)),
                     reads=[PB(b)], writes=["Vm"])
        norm_seq(items[16:])
        S.barrier()
        S.op("dve", lambda e: e.memset(KP[0:64, 1, :], 0.0), writes=[("KPaug", 1)])
        S.op("dve", lambda e: e.memset(QP[0:64, 1, :], 0.0), writes=[("QPaug", 1)])
        S.op("pool", lambda e: e.memset(KP[64:128, 0, :], 0.0), writes=[("KPaug", 0)])
        S.op("pool", lambda e: e.memset(QP[64:128, 0, :], 0.0), writes=[("QPaug", 0)])
        S.op("dve", lambda e: e.memset(VP[:, :, :, 64:65], 1.0), writes=["VPones"])
        for s in range(2):
            base = 64 if s == 0 else 0
            S.dma(lambda e, s=s, base=base: e.dma_start(out=KP[base + 3:base + 7, s, :], in_=kconst[:, :]),
                  writes=[("KPaug", s)])
            S.dma(lambda e, s=s, base=base: e.dma_start(out=QP[base:base + 3, s, :], in_=qconst[0:3, :]),
                  writes=[("QPaug", s)])
            S.dma(lambda e, s=s, base=base: e.dma_start(out=QP[base + 6:base + 7, s, :], in_=qconst[3:4, :]),
                  writes=[("QPaug", s)])

        qchunks = [(QOFF, 128, 0)] + [(2048 + 512 * i, 512, 128 + 512 * i) for i in range(4)]
        pt_i = [0]
        npairs = 6 if stage >= 1 else 0
        def pair_aug(pair):
            for s in range(2):
                h = 2 * pair + s
                base = 64 if s == 0 else 0
                S.dma(lambda e, s=s, h=h, base=base: e.dma_start(out=KP[base:base + 3, s, :], in_=gsc[h, :, :]),
                      reads=["gsc"], writes=[("KPaug", s)])
                S.dma(lambda e, s=s, h=h, base=base: e.dma_start(out=QP[base + 3:base + 6, s, :],
                                                                 in_=gsc[h, :, QOFF:NT]),
                      reads=["gsc"], writes=[("QPaug", s)])

        def g_chain():
            S.alias(G_KEYS, [("wxk", k) for k in range(4)] + [("wxv", k) for k in range(4)] + [("memT", 0), ("memT", 1)])
            S.op("dve", lambda e: e.memset(g_ones, 1.0), writes=["g_ones"])
            for tc in range(8):
                b = tc % 3
                for kc in range(8):
                    mm(psum[0:12, b, :], wf[:, kc, :], hT[:, kc, tc * 512:(tc + 1) * 512], kc == 0, kc == 7,
                       ["wf"] + hTk(tc * 512, 512), PB(b))
                S.op("act", lambda e, b=b: e.activation(out=g_et, in_=psum[0:12, b, :], func=AF.Exp,
                                                        bias=negb[:, 0:1], scale=-1.0),
                     reads=[PB(b), "negb"], writes=["g_et"])
                S.op("act", lambda e, tc=tc: e.activation(out=g_sps[tc], in_=g_et, func=AF.Ln, bias=1.0, scale=1.0),
                     reads=["g_et"], writes=[("g_sp", tc)])
            for tc in range(8):
                gi = tc % 2
                init = 0.0 if tc == 0 else g_G[1 - gi][:, 511:512]
                S.op("dve", lambda e, gi=gi, init=init, tc=tc: e.tensor_tensor_scan(out=g_G[gi], data0=g_ones, data1=g_sps[tc],
                                                                                    initial=init, op0=ALU.mult, op1=ALU.add),
                     reads=[("g_sp", tc), "g_ones", ("gG", 1 - gi)], writes=[("gG", gi)])
                pp = g_parts[gi]
                S.op("dve", lambda e, gi=gi, pp=pp: e.tensor_copy(out=pp[:, 0, :], in_=g_G[gi]),
                     reads=[("gG", gi)], writes=[("gp", gi)])
                S.op("dve", lambda e, gi=gi, pp=pp: e.tensor_tensor(out=g_r1, in0=g_G[gi], in1=pp[:, 0, :], op=ALU.subtract),
                     reads=[("gG", gi), ("gp", gi)], writes=["g_r1"])
                S.op("dve", lambda e, pp=pp: e.tensor_copy(out=pp[:, 1, :], in_=g_r1), reads=["g_r1"], writes=[("gp", gi)])
                S.op("dve", lambda e, pp=pp: e.tensor_tensor(out=g_r2, in0=g_r1, in1=pp[:, 1, :], op=ALU.subtract),
                     reads=["g_r1", ("gp", gi)], writes=["g_r2"])
                S.op("dve", lambda e, pp=pp: e.tensor_copy(out=pp[:, 2, :], in_=g_r2), reads=["g_r2"], writes=[("gp", gi)])
                S.dma(lambda e, tc=tc, pp=pp: e.dma_start(out=gsc[:, :, tc * 512:(tc + 1) * 512], in_=pp),
                      reads=[("gp", gi)], writes=["gsc"])
            S.alias([("ymix", qb) for qb in range(NQB)], G_KEYS)

        def pair_proj(pair):
            for ci, (t0, n, q0) in enumerate(qchunks):
                b = sb_i[0] % 3; sb_i[0] += 1
                for kc in range(8):
                    mm(psum[:, b, 0:n], wpair[:, kc, 0, :], hT[:, kc, t0:t0 + n], kc == 0, kc == 7,
                       [("wpair", 0)] + hTk(t0, n), PB(b))
                S.op("act", lambda e, b=b, n=n, q0=q0: e.activation(out=QP[0:64, 0, q0:q0 + n], in_=psum[0:64, b, 0:n],
                                                                    func=AF.Copy, scale=0.125),
                     reads=[PB(b)], writes=[("QP", 0, ci)])
                if pair == 0:
                    S.op("act", lambda e, b=b, n=n, q0=q0: e.activation(out=QP[64:128, 1, q0:q0 + n],
                                                                        in_=psum[64:128, b, 0:n], func=AF.Copy, scale=0.125),
                         reads=[PB(b)], writes=[("QP", 1, ci)])
                else:
                    S.op("dve", lambda e, b=b, n=n, q0=q0: e.tensor_scalar(out=QP[64:128, 1, q0:q0 + n],
                                                                           in0=psum[64:128, b, 0:n], scalar1=0.125,
                                                                           scalar2=None, op0=ALU.mult),
                         reads=[PB(b)], writes=[("QP", 1, ci)])
            for tc in range(8):
                b = sb_i[0] % 3; sb_i[0] += 1
                for kc in range(8):
                    mm(psum[:, b, :], wpair[:, kc, 1, :], hT[:, kc, tc * 512:(tc + 1) * 512], kc == 0, kc == 7,
                       [("wpair", 1)] + hTk(tc * 512, 512), PB(b))
                S.op("act", lambda e, b=b, tc=tc: e.copy(out=KP[0:64, 0, tc * 512:(tc + 1) * 512], in_=psum[0:64, b, :]),
                     reads=[PB(b)], writes=[("KP", 0, tc)])
                if pair == 0:
                    S.op("act", lambda e, b=b, tc=tc: e.copy(out=KP[64:128, 1, tc * 512:(tc + 1) * 512],
                                                             in_=psum[64:128, b, :]),
                         reads=[PB(b)], writes=[("KP", 1, tc)])
                else:
                    S.op("dve", lambda e, b=b, tc=tc: e.tensor_copy(out=KP[64:128, 1, tc * 512:(tc + 1) * 512],
                                                                    in_=psum[64:128, b, :]),
                         reads=[PB(b)], writes=[("KP", 1, tc)])
            for tc in range(8):
                b = sb_i[0] % 3; sb_i[0] += 1
                for kc in range(8):
                    mm(psum[:, b, :], wpair[:, kc, 2, :], hT[:, kc, tc * 512:(tc + 1) * 512], kc == 0, kc == 7,
                       [("wpair", 2)] + hTk(tc * 512, 512), PB(b))
                S.op("act", lambda e, b=b: e.copy(out=vtmp, in_=psum[:, b, :]), reads=[PB(b)], writes=["vtmp"])
                for i in range(4):
                    S.op("pe", lambda e, i=i: e.transpose(out=pst[:, i * 128:(i + 1) * 128],
                                                          in_=vtmp[:, i * 128:(i + 1) * 128], identity=ident[:]),
                         reads=["vtmp", "ident"], writes=[PB(7)], accum=True)
                if pair == 0:
                    S.op("act", lambda e, tc=tc: e.copy(
                        out=VP[:, 4 * tc:4 * tc + 4, :, 0:64],
                        in_=pst[:, 0:512].rearrange("p (b s d) -> p b s d", b=4, s=2)),
                         reads=[PB(7)], writes=[("VP", tc)])
                else:
                    S.op("dve", lambda e, tc=tc: e.tensor_copy(
                        out=VP[:, 4 * tc:4 * tc + 4, :, 0:64],
                        in_=pst[:, 0:512].rearrange("p (b s d) -> p b s d", b=4, s=2)),
                         reads=[PB(7)], writes=[("VP", tc)])

        def pair_attn(pair):
            steps = []
            for s in range(2):
                h = 2 * pair + s
                rows = slice(0, 128)
                for ci, (qb0, nqb) in enumerate([(0, 1), (1, 4), (5, 4), (9, 4), (13, 4)]):
                    q0 = qb0 * 128
                    klast = 15 + qb0 + nqb - 1
                    if ci == 0:
                        for g in range(4):
                            b = sb_i[0] % 3; sb_i[0] += 1
                            pi = pt_i[0] % 3; pt_i[0] += 1

                            def fS0(s=s, g=g, b=b):
                                for t in range(4):
                                    j = 4 * g + t
                                    dg = (j == 15)
                                    mm(psum[:, b, t * 128:(t + 1) * 128], KP[:, s, j * 128:(j + 1) * 128], QP[:, s, 0:128],
                                       True, not dg, [("KP", s, g), ("KPaug", s), ("QP", s, 0), ("QPaug", s)], PB(b))
                                    if dg:
                                        mm(psum[:, b, t * 128:(t + 1) * 128], ident[:], cbias[:], False, True,
                                           ["ident", "cbias"], PB(b))

                            def fE0(b=b, pi=pi):
                                S.op("act", lambda e: e.activation(out=PT[pi][:, 0:512], in_=psum[:, b, 0:512], func=AF.Exp),
                                     reads=[PB(b)], writes=[("PT", pi)])

                            def fPV0(s=s, g=g, pi=pi, h=h):
                                for t in range(4):
                                    j = 4 * g + t
                                    mm(psum[:, 3, 0:65], PT[pi][:, t * 128:(t + 1) * 128], VP[:, j, s, :],
                                       j == 0, j == 15, [("PT", pi), ("VP", g), "VPones"], PB(3))
                                if g == 3:
                                    col, key = rl_col()
                                    S.op("dve", lambda e: e.reciprocal(out=col, in_=psum[:, 3, 64:65]),
                                         reads=[PB(3)], writes=[key])
                                    S.op("dve", lambda e: e.tensor_scalar(
                                        out=ymix[:, 0, h * 64:(h + 1) * 64], in0=psum[:, 3, 0:64], scalar1=col,
                                        scalar2=None, op0=ALU.mult), reads=[PB(3), key], writes=[("ymix", 0)])

                            steps.append((fS0, fE0, fPV0))
                        continue
                    for j in range(klast + 1):
                        m = max(0, j - (15 + qb0))
                        c0, c1 = m * 128, nqb * 128
                        b = sb_i[0] % 3; sb_i[0] += 1
                        pi = pt_i[0] % 3; pt_i[0] += 1
                        diag = j >= 15 + qb0

                        def fS(s=s, rows=rows, ci=ci, q0=q0, j=j, c0=c0, c1=c1, b=b, diag=diag):
                            mm(psum[:, b, c0:c1], KP[rows, s, j * 128:(j + 1) * 128], QP[rows, s, q0 + c0:q0 + c1],
                               True, not diag, [("KP", s, j // 4), ("KPaug", s), ("QP", s, ci), ("QPaug", s)], PB(b))
                            if diag:
                                mm(psum[:, b, c0:c0 + 128], ident[:], cbias[:], False, True, ["ident", "cbias"], PB(b))

                        def fE(b=b, c0=c0, c1=c1, pi=pi):
                            S.op("act", lambda e: e.activation(out=PT[pi][:, c0:c1], in_=psum[:, b, c0:c1], func=AF.Exp),
                                 reads=[PB(b)], writes=[("PT", pi)])

                        def fPV(s=s, j=j, m=m, nqb=nqb, qb0=qb0, pi=pi, h=h, last=(j == klast)):
                            for lq in range(m, nqb):
                                mm(psum[:, 3 + lq, 0:65], PT[pi][:, lq * 128:(lq + 1) * 128], VP[:, j, s, :],
                                   j == 0, j == 15 + qb0 + lq, [("PT", pi), ("VP", j // 4), "VPones"], PB(3 + lq))
                            if last:
                                for lq in range(nqb):
                                    col, key = rl_col()
                                    S.op("dve", lambda e, lq=lq, col=col: e.reciprocal(out=col, in_=psum[:, 3 + lq, 64:65]),
                                         reads=[PB(3 + lq)], writes=[key])
                                    S.op("dve", lambda e, lq=lq, col=col, qb=qb0 + lq: e.tensor_scalar(
                                        out=ymix[:, qb, h * 64:(h + 1) * 64], in0=psum[:, 3 + lq, 0:64], scalar1=col,
                                        scalar2=None, op0=ALU.mult),
                                         reads=[PB(3 + lq), key], writes=[("ymix", qb0 + lq)])

                        steps.append((fS, fE, fPV))
            LA = 2
            for idx in range(len(steps) + LA):
                if idx < len(steps):
                    steps[idx][0]()
                    steps[idx][1]()
                if idx - LA >= 0:
                    steps[idx - LA][2]()

        wmixA = A2[:, 29760:33856].rearrange("p (k n) -> p k n", k=4)
        wmixC = A3[:, 96:3168].rearrange("p (k n) -> p k n", k=3)
        wmixD = A3[:, 5472:6496]
        for pair in range(npairs):
            if pair == 2:
                S.alias([("wmix", 0), ("wmix", 1)], G_KEYS)
                for k2 in range(4):
                    load_w(w_mix, k2 * 128, 1, 0, 1024, wmixA[:, k2:k2 + 1, :], ("wmix", k2 // 2))
            if pair == 0:
                g_chain()
            pair_proj(pair)
            if pair + 1 < npairs:
                pair_weights(pair + 1)
            if pair == npairs - 1:
                S.alias([("wmix", 2), ("wmix", 3)], [("wpair", w_) for w_ in range(3)] + ["wf"])
                for k2 in range(4, 7):
                    load_w(w_mix, k2 * 128, 1, 0, 1024, wmixC[:, k2 - 4:k2 - 3, :], ("wmix", k2 // 2))
            pair_aug(pair)
            pair_attn(pair)
            if pair == npairs - 1:
                S.alias([("wmix", 3)], [("PT", i_) for i_ in range(3)] + ["vtmp"])
                load_w(w_mix, 7 * 128, 1, 0, 1024, wmixD.rearrange("p (k n) -> p k n", k=1), ("wmix", 3))

        S.barrier()
        NP_ = 2304
        P0 = 1792
        uT = A2[:, 0:4608].bitcast(F32)
        sA = A2[:, 4608:9216].bitcast(F32)
        sB = A2[:, 9216:13824].bitcast(F32)
        dT = A2[:, 13824:13824 + 2304]
        t16 = A2[:, 16128:16160].bitcast(F32)
        for cc in range(2):
            for (a, n) in [(0, 512), (512, 512), (1024, 512), (1536, 512), (2048, 256)]:
                b = sb_i[0] % 3; sb_i[0] += 1
                for kc in range(8):
                    mm(psum[:, b, 0:n], wu[:, kc, cc * 128:(cc + 1) * 128], hT[:, kc, P0 + a:P0 + a + n],
                       kc == 0, kc == 7, ["wu"], PB(b))
                S.op("act", lambda e, b=b, a=a, n=n: e.copy(out=uT[:, a:a + n], in_=psum[:, b, 0:n]),
                     reads=[PB(b)], writes=["uT"])
            N = NP_

            def tt(out_ap, a_ap, b_ap, rd, wr):
                S.op("dve", lambda e: e.tensor_tensor(out=out_ap, in0=a_ap, in1=b_ap, op=ALU.add), reads=rd, writes=wr)

            def msk_add(sh, lo_, cc=cc, N=NP_):
                S.op("dve", lambda e: e.scalar_tensor_tensor(out=sB[:, lo_:N], in0=sA[:, lo_ - sh:N - sh], scalar=pmk[:, 0:1],
                                                             in1=sA[:, lo_:N], op0=ALU.mult, op1=ALU.add),
                     reads=["sA", "pmk"], writes=["sB"])

            def dif_all(cc=cc, N=NP_):
                S.op("dve", lambda e: e.scalar_tensor_tensor(out=dT[:, 128:N], in0=sB[:, 128:N], scalar=pmk[:, 1 + cc:2 + cc],
                                                             in1=uT[:, 128:N], op0=ALU.mult, op1=ALU.subtract),
                     reads=["sB", "uT", "pmk"], writes=["dT"])
                S.op("dve", lambda e: e.tensor_tensor(out=t16[:, :], in0=sB[:, 256:272], in1=invt[:, cc, :],
                                                      op=ALU.mult), reads=["sB", "invt"], writes=["t16"])
                S.op("dve", lambda e: e.tensor_tensor(out=dT[:, 256:272], in0=t16[:, :], in1=uT[:, 256:272],
                                                      op=ALU.subtract), reads=["t16", "uT", "dT"], writes=["dT"])

            tt(sA[:, 1:N], uT[:, 1:N], uT[:, 0:N - 1], ["uT"], ["sA"])
            if cc == 0:
                msk_add(2, 3)
            else:
                tt(sB[:, 3:N], sA[:, 3:N], sA[:, 1:N - 2], ["sA"], ["sB"])
                tt(sA[:, 7:N], sB[:, 7:N], sB[:, 3:N - 4], ["sB"], ["sA"])
                msk_add(8, 15)
            dif_all()
            for (a, n) in [(0, 512), (512, 512), (1024, 512), (1536, 512), (2048, 128)]:
                b = sb_i[0] % 3; sb_i[0] += 1
                mm(psum[:, b, 0:n], wpbd[:, cc, :], dT[:, 128 + a:128 + a + n], True, True, ["wpbd", "dT"], PB(b))
                S.op("act", lambda e, b=b, a=a, n=n, cc=cc: e.activation(out=ypT[:, cc, a:a + n], in_=psum[:, b, 0:n],
                                                                         func=AF.Identity, scale=pscol[:, cc:cc + 1]),
                     reads=[PB(b), "pscol"], writes=["ypT"])

        S.barrier()
        wxq = A2[:, 0:8192].rearrange("p (k n) -> p k n", k=8)
        wxo = A2[:, 8208:16400].rearrange("p (k n) -> p k n", k=8)
        for k2 in range(8):
            pass
        load_gain(1, 1)
        for k2 in range(8):
            load_w(w_xq, k2 * 128, 1, 0, 1024, wxq[:, k2:k2 + 1, :], ("wxq", k2 // 2))
        wmk = [("wmix", k2) for k2 in range(4)]

        def post_norm_residual(ybanks, gi, xsrc, xsrc_reads, dst, dst_key):
            yv = psum[:, ybanks[0]:ybanks[0] + 2, :].rearrange("p a b -> p (a b)")
            yk = [PB(ybanks[0]), PB(ybanks[1])]
            col, key = rstd_of(yv, yk)
            S.op("dve", lambda e: e.scalar_tensor_tensor(out=tmpf, in0=yv, scalar=col, in1=gt[gi][:],
                                                         op0=ALU.mult, op1=ALU.mult),
                 reads=yk + [key, ("gt", gi)], writes=["tmpf"])
            S.op("dve", lambda e: e.tensor_tensor(out=dst, in0=tmpf, in1=xsrc, op=ALU.add),
                 reads=["tmpf"] + list(xsrc_reads), writes=[dst_key])

        nB = NQB if stage >= 1 else 0
        yT2 = A2[:, 33856:34624].rearrange("p (c t) -> p c t", c=6)
        yTb = [yT, yT2]

        def b_T(qb):
            for c in range(6):
                S.op("pe", lambda e, c=c: e.transpose(out=pst[:, c * 128:(c + 1) * 128],
                                                      in_=ymix[:, qb, c * 128:(c + 1) * 128], identity=ident[:]),
                     reads=[("ymix", qb), "ident"], writes=[PB(7)], accum=True)
            S.op("act", lambda e: e.copy(out=yTb[qb % 2], in_=pst[:, 0:768].rearrange("p (c t) -> p c t", c=6)),
                 reads=[PB(7)], writes=[("yT", qb % 2)])

        def b_mm(qb):
            yb = (0, 1) if qb % 2 == 0 else (2, 3)
            for n in range(2):
                for kc in range(8):
                    lhsT = ypT[:, kc, qb * 128:(qb + 1) * 128] if kc < 2 else yTb[qb % 2][:, kc - 2, :]
                    wsl = (wmixA[:, kc, n * 512:(n + 1) * 512] if kc < 4 else
                           wmixC[:, kc - 4, n * 512:(n + 1) * 512] if kc < 7 else wmixD[:, n * 512:(n + 1) * 512])
                    mm(psum[:, yb[n], :], lhsT, wsl, kc == 0, kc == 7,
                       ["ypT", ("yT", qb % 2), ("wmix", kc // 2)], PB(yb[n]))
            i = qb % 2
            S.dma(lambda e: e.dma_start(out=xl[i][:], in_=xk[QOFF + qb * 128:QOFF + (qb + 1) * 128, :]),
                  writes=[("xl", i)])
            post_norm_residual(yb, 1, xl[i][:], [("xl", i)], xr(qb), ("xres", qb))

        if nB:
            b_T(0)
        for qb in range(nB):
            if qb + 1 < nB:
                b_T(qb + 1)
            b_mm(qb)

        if stage == 1:
            import os
            if os.environ.get("BAR"):
                S.barrier()
            toks = []
            for qb in range(1, NQB):
                toks.append(S.dma(lambda e, qb=qb: e.dma_start(out=out[(qb - 1) * 128:qb * 128, :], in_=xr(qb)),
                                  reads=[("xres", qb)], writes=[("outd", qb)]))
            S.barrier()
            S.emit()
            return nc

        S.alias([("wxo", k) for k in range(4)], [("wmix", k) for k in range(4)])
        S.alias([("h2T", i) for i in range(4)] + [("q2T", i) for i in range(8)] + [("oc", i) for i in range(4)],
                [("ymix", qb) for qb in range(NQB)])
        S.alias([("PTx", i) for i in range(4)] + [("ocT", 0), ("ocT", 1)], ["wf", "wu", ("wmix", 2), ("wmix", 3)] + [("wpair", i) for i in range(3)])
        o = 0
        PTx = [A3[:, o + i * 512:o + (i + 1) * 512] for i in range(4)]; o += 2048
        ocT = A3[:, o:o + 1024].rearrange("p (c t) -> p c t", c=8); o += 1024
        ocT2 = A3[:, o:o + 1024].rearrange("p (c t) -> p c t", c=8); o += 1024
        for k2 in range(8):
            load_w(w_xo, k2 * 128, 1, 0, 1024, wxo[:, k2:k2 + 1, :], ("wxo", k2 // 2))
        load_gain(0, 3)
        load_gain(1, 4)
        h2T = A2[:, 16400:20496].rearrange("p (k t) -> p k t", k=8)
        q2T = A2[:, 20496:24592].rearrange("p (k t) -> p k t", k=8)
        oc = A2[:, 24592:28688].rearrange("p (b d) -> p b d", b=4)
        wq_k = [("wxq", k2) for k2 in range(4)]
        wo_k = [("wxo", k2) for k2 in range(4)]
        px_i = [0]
        ob_i = [0]
        out_tokens = []
        chunks = [(0, 1), (1, 4), (5, 4), (9, 4), (13, 4)]
        ocTb = [ocT, ocT2]
        oct_i = [0]
        yb_i = [0]

        def c_norms(ci):
            qb0, nqb = chunks[ci]
            norm_seq([(None, xr(qb0 + lb), [("xres", qb0 + lb)], 0,
                       (lambda lb=lb: h2T[:, :, lb * 128:(lb + 1) * 128]), ("h2T", lb), None) for lb in range(nqb)])

        def c_qproj(ci, fc):
            qb0, nqb = chunks[ci]
            N = nqb * 128
            h2k = [("h2T", lb) for lb in range(nqb)]
            b = sb_i[0] % 3; sb_i[0] += 1
            for kc in range(8):
                mm(psum[:, b, 0:N], wxq[:, kc, fc * 128:(fc + 1) * 128], h2T[:, kc, 0:N], kc == 0, kc == 7,
                   h2k + [("wxq", kc // 2)], PB(b))
            S.op("act", lambda e: e.activation(out=q2T[:, fc, 0:N], in_=psum[:, b, 0:N], func=AF.Copy, scale=0.0625),
                 reads=[PB(b)], writes=[("q2T", fc)])

        def c_attn(ci):
            qb0, nqb = chunks[ci]
            N = nqb * 128

            def s_stage(hh):
                pis = []
                for mb in range(2):
                    b = sb_i[0] % 3; sb_i[0] += 1
                    for d in range(2):
                        mm(psum[:, b, 0:N], kTm[:, 2 * hh + d, mb * 128:(mb + 1) * 128], q2T[:, 2 * hh + d, 0:N],
                           d == 0, d == 1, ["kTm", ("q2T", 2 * hh + d)], PB(b))
                    pi = px_i[0] % 4; px_i[0] += 1
                    pis.append(pi)
                    S.op("act", lambda e, b=b, pi=pi: e.activation(out=PTx[pi][:, 0:N], in_=psum[:, b, 0:N], func=AF.Exp),
                         reads=[PB(b)], writes=[("PTx", pi)])
                return pis

            def pv_stage(hh, pis):
                for lb in range(nqb):
                    ob = 3 + (ob_i[0] % 4); ob_i[0] += 1
                    for mb in range(2):
                        mm(psum[:, ob, 0:257], PTx[pis[mb]][:, lb * 128:(lb + 1) * 128], Vm[:, mb, hh, :],
                           mb == 0, mb == 1, [("PTx", pis[mb]), "Vm", "Vmones"], PB(ob))
                    col, key = rl_col()
                    S.op("dve", lambda e, ob=ob, col=col: e.reciprocal(out=col, in_=psum[:, ob, 256:257]),
                         reads=[PB(ob)], writes=[key])
                    S.op("dve", lambda e, ob=ob, col=col, lb=lb: e.tensor_scalar(
                        out=oc[:, lb, hh * 256:(hh + 1) * 256], in0=psum[:, ob, 0:256], scalar1=col, scalar2=None,
                        op0=ALU.mult), reads=[PB(ob), key], writes=[("oc", lb)])

            prev = None
            for hh in range(4):
                pis = s_stage(hh)
                if prev is not None:
                    pv_stage(*prev)
                prev = (hh, pis)
            pv_stage(*prev)

        def c_out_T(ci, lb):
            k = oct_i[0] % 2; oct_i[0] += 1
            for c in range(8):
                S.op("pe", lambda e, c=c: e.transpose(out=pst[:, c * 128:(c + 1) * 128],
                                                      in_=oc[:, lb, c * 128:(c + 1) * 128], identity=ident[:]),
                     reads=[("oc", lb), "ident"], writes=[PB(7)], accum=True)
            S.op("act", lambda e: e.copy(out=ocTb[k], in_=pst.rearrange("p (c t) -> p c t", c=8)),
                 reads=[PB(7)], writes=[("ocT", k)])
            return k

        def c_out_mm(ci, lb, k):
            qb0, nqb = chunks[ci]
            qb = qb0 + lb
            yb = (5, 6) if yb_i[0] % 2 == 0 else (3, 4)
            yb_i[0] += 1
            for n in range(2):
                for kc in range(8):
                    mm(psum[:, yb[n], :], ocTb[k][:, kc, :], wxo[:, kc, n * 512:(n + 1) * 512], kc == 0, kc == 7,
                       [("ocT", k), ("wxo", kc // 2)], PB(yb[n]))
            post_norm_residual(yb, 1, xr(qb), [("xres", qb)], xr(qb), ("xres", qb))
            if qb >= 1:
                out_tokens.append(S.dma(lambda e: e.dma_start(out=out[(qb - 1) * 128:qb * 128, :], in_=xr(qb)),
                                        reads=[("xres", qb)], writes=[("outd", qb - 1)]))

        c_norms(0)
        for fc in range(8):
            c_qproj(0, fc)
        c_attn(0)
        h3T = A2[:, 0:8208].rearrange("p (k t) -> p k t", k=8)
        load_gain(2, 5)
        h3_items = [(None, xhalo[:], [("xres", 0)], 2, (lambda: h3T[:, :, 0:2]), ("h3T", -1), (126, 128))]
        for lb in range(8):
            h3_items.append((None, xr(1 + lb), [("xres", 1 + lb)], 2,
                             (lambda lb=lb: h3T[:, :, 2 + lb * 128:2 + (lb + 1) * 128]), ("h3T", lb), None))
        for ci in range(len(chunks)):
            nqb = chunks[ci][1]
            nxt = ci + 1 < len(chunks)
            if nxt:
                c_norms(ci + 1)
            else:
                S.alias([("h3T", lb) for lb in range(-1, 8)], [("wxq", k_) for k_ in range(4)])
            fcs = list(range(8))
            per = (8 + nqb - 1) // nqb
            for lb in range(nqb):
                k = c_out_T(ci, lb)
                if nxt:
                    for fc in fcs[lb * per:(lb + 1) * per]:
                        c_qproj(ci + 1, fc)
                elif stage >= 4:
                    norm_seq(h3_items[lb * 3:(lb + 1) * 3])
                c_out_mm(ci, lb, k)
            if nxt:
                c_attn(ci + 1)
        if stage >= 4:
            S.op("pool", lambda e: e.tensor_copy(out=h3h[:, :, :], in_=h3T[:, :, 1024:1026]),
                 reads=[("h3T", 7)], writes=["h3h"])
        if stage == 2:
            S.barrier()
            S.emit()
            return nc

        S.barrier()
        o = 0
        o += 8208
        wupb = [A2[:, o + i * 2048:o + (i + 1) * 2048].rearrange("p (g k n) -> p g k n", g=2, k=8) for i in range(2)]
        o += 4096
        wdnb = [A2[:, o + i * 512:o + (i + 1) * 512] for i in range(8)]
        o += 4096
        ylo = A3[:, 0:8192].bitcast(F32).rearrange("p (b n) -> p b n", b=8)
        yhb = [A3[:, 8192 + i * 1024:8192 + (i + 1) * 1024].bitcast(F32) for i in range(4)]
        yh_i = [0]
        hgb = [A2[:, o + i * 2052:o + (i + 1) * 2052].bitcast(F32) for i in range(2)]; o += 4104
        hub = [A2[:, o + i * 2052:o + (i + 1) * 2052].bitcast(F32) for i in range(2)]; o += 4104
        cvg = [A2[:, o + i * 1024:o + (i + 1) * 1024].bitcast(F32) for i in range(2)]; o += 2048
        cvu = [A2[:, o + i * 1024:o + (i + 1) * 1024].bitcast(F32) for i in range(2)]; o += 2048
        glb = [A2[:, o + i * 1024:o + (i + 1) * 1024].bitcast(F32) for i in range(2)]; o += 2048
        cwr = A2[0:64, o:o + 1024].bitcast(F32).rearrange("p (k n) -> p k n", k=4); o += 1024
        assert o <= 34816
        load_gain(0, 6)
        xpre = [(A3[:, i * 2048:(i + 1) * 2048].bitcast(F32), ("xpre", i)) for i in range(6)] + \
               [(xl[0][:], ("xl", 0)), (xl[1][:], ("xl", 1))]
        S.dma(lambda e: e.dma_start(out=cwr[:, 0:3, :], in_=conv_w.rearrange("k (c p) -> c k p", p=128)), writes=["cwr"])
        S.dma(lambda e: e.dma_start(out=cwr[:, 3, :], in_=conv_b.rearrange("k (c p) -> (k c) p", p=128)),
              writes=["cwr"])
        for k in range(4):
            S.op("pe", lambda e, k=k: e.transpose(out=psum[:, 0, k * 64:(k + 1) * 64], in_=cwr[:, k, :],
                                                  identity=identf[0:64, 0:64]),
                 reads=["cwr", "identf"], writes=[PB(0)], accum=True)
        S.op("act", lambda e: e.copy(out=cw[:], in_=psum[:, 0, 0:256].rearrange("p (k n) -> p k n", k=4)),
             reads=[PB(0)], writes=["cw"])

        it = [0]
        wu_i = [0]
        wd_i = [0]
        def ffn_h3T(th):
            if th == 1:
                S.op("pool", lambda e: e.tensor_copy(out=h3T[:, :, 0:2], in_=h3h[:, :, :]),
                     reads=["h3h"], writes=[("h3T", -1)])
            items = []
            for lb in range(-1 if th == 0 else 0, 8):
                qb = 1 + 8 * th + lb
                i = (lb + 1) % 2
                if qb == 0:
                    pre, src, srck = None, xhalo[:], [("xres", 0)]
                elif th == 1:
                    pre = None
                    src, srck = xpre[lb][0], [xpre[lb][1]]
                else:
                    pre = (lambda qb=qb, i=i: S.dma(lambda e: e.dma_start(out=xl[i][:], in_=out[(qb - 1) * 128:qb * 128, :]),
                                                    reads=[("outd", qb - 1)], writes=[("xl", i)]))
                    src, srck = xl[i][:], [("xl", i)]
                if lb == -1:
                    items.append((pre, src, srck, 2, (lambda: h3T[:, :, 0:2]), ("h3T", -1), (126, 128)))
                else:
                    items.append((pre, src, srck, 2, (lambda lb=lb: h3T[:, :, 2 + lb * 128:2 + (lb + 1) * 128]),
                                  ("h3T", lb), None))
            norm_seq(items)
            if th == 1:
                S.alias([("ylo", i) for i in range(8)] + [("yhb", i) for i in range(4)], [("xpre", i) for i in range(6)])
            if th == 0:
                S.op("pool", lambda e: e.tensor_copy(out=h3h[:, :, :], in_=h3T[:, :, 1024:1026]),
                     reads=[("h3T", 7)], writes=["h3h"])
        def load_up(fc):
            wb = fc % 2
            load_w(w_up, 0, 8, fc * 128, 128, wupb[wb][:, 0, :, :], ("wup", wb, 0))
            load_w(w_up, 0, 8, 4096 + fc * 128, 128, wupb[wb][:, 1, :, :], ("wup", wb, 1))

        def ffn_up(th, preloaded=0, deferred=(), xprefetch=False):
            pending = []
            if preloaded < 1:
                load_up(0)
            for fc in range(32):
                wb = fc % 2
                if fc == 3:
                    for d_ in deferred:
                        d_()
                if xprefetch and 4 <= fc < 12:
                    lb_ = fc - 4
                    buf_, bkey_ = xpre[lb_]
                    S.dma(lambda e, lb_=lb_, buf_=buf_: e.dma_start(out=buf_, in_=out[(8 + lb_) * 128:(9 + lb_) * 128, :]),
                          reads=[("outd", 8 + lb_), ("actT", fc - 2, 1)], writes=[bkey_])
                if fc + 1 < 32 and fc + 1 >= preloaded:
                    load_up(fc + 1)
                hbufs = (hgb[wb], hub[wb])
                for g in range(2):
                    for kc in range(8):
                        mm(psum[:, 4, 2 * g:2 * g + 2], wupb[wb][:, g, kc, :], h3T[:, kc, 0:2],
                           kc == 0, kc == 7, [("wup", wb, g), ("h3T", -1)], PB(4))
                fcol = flagt[:, 0:1] if th == 0 else flagt[:, 1:2]
                for g in range(2):
                    S.op("dve", lambda e, g=g, hbuf=hbufs[g], fcol=fcol: e.tensor_scalar(
                        out=hbuf[:, 0:2], in0=psum[:, 4, 2 * g:2 * g + 2], scalar1=fcol, scalar2=None,
                        op0=ALU.mult), reads=[PB(4), "flagt"], writes=[("hbuf", wb, g, "h")])
                for tc in range(2):
                    ii = it[0] % 2; it[0] += 1
                    bg, bu = 2 * ii, 2 * ii + 1
                    hk = [("h3T", lb) for lb in range(4 * tc, 4 * tc + 4)]
                    for g, bb in ((0, bg), (1, bu)):
                        for kc in range(8):
                            mm(psum[:, bb, :], wupb[wb][:, g, kc, :], h3T[:, kc, 2 + tc * 512:2 + (tc + 1) * 512],
                               kc == 0, kc == 7, [("wup", wb, g)] + hk, PB(bb))
                    cvs = (cvg[ii], cvu[ii])
                    c0 = tc * 512
                    for g, bb in ((0, bg), (1, bu)):
                        fcg = fc + 32 * g
                        S.op("act", lambda e, bb=bb, hbuf=hbufs[g], c0=c0: e.copy(out=hbuf[:, 2 + c0:514 + c0],
                                                                                 in_=psum[:, bb, :]),
                             reads=[PB(bb)], writes=[("hbuf", wb, g, tc)])
                        S.op("act", lambda e, bb=bb, cv=cvs[g], fcg=fcg: e.activation(
                            out=cv, in_=psum[:, bb, :], func=AF.Identity, scale=cw[:, 2, fcg:fcg + 1],
                            bias=cw[:, 3, fcg:fcg + 1]), reads=[PB(bb), "cw"], writes=[("cv", ii, g)])
                    prevk = "h" if tc == 0 else 0
                    for tap, sh in ((1, 1), (0, 0)):
                        for g in range(2):
                            fcg = fc + 32 * g
                            S.op("dve", lambda e, cv=cvs[g], hbuf=hbufs[g], fcg=fcg, tap=tap, sh=sh, c0=c0:
                                 e.scalar_tensor_tensor(out=cv, in0=hbuf[:, c0 + sh:c0 + sh + 512],
                                                        scalar=cw[:, tap, fcg:fcg + 1], in1=cv, op0=ALU.mult, op1=ALU.add),
                                 reads=[("hbuf", wb, g, tc), ("hbuf", wb, g, prevk), ("cv", ii, g), "cw"],
                                 writes=[("cv", ii, g)])

                    def tail(ii=ii, fc=fc, tc=tc):
                        S.op("act", lambda e: e.activation(out=glb[ii], in_=cvg[ii], func=AF.Gelu_apprx_tanh),
                             reads=[("cv", ii, 0)], writes=[("gl", ii)])
                        S.op("dve", lambda e: e.tensor_tensor(out=actT[:, fc, tc * 512:(tc + 1) * 512], in0=glb[ii],
                                                              in1=cvu[ii], op=ALU.mult),
                             reads=[("gl", ii), ("cv", ii, 1)], writes=[("actT", fc, tc)])

                    for t_ in pending:
                        t_()
                    pending = [tail]
            for t_ in pending:
                t_()
            pending = []
        obufs = [(xl[0][:], ("xl", 0)), (xl[1][:], ("xl", 1)), (tmpf, "tmpf"), (gt[1][:], ("gt", 1)), (gt[2][:], ("gt", 2))]
        ob_rot = [0]
        fin_th = [0]

        def fin(lb, cols, ybs):
            col, key = cols[lb]
            yb_ = ybs[lb]
            finish_rstd(col, key)
            ob = 8 * fin_th[0] + lb
            buf, bkey = obufs[ob_rot[0] % len(obufs)]; ob_rot[0] += 1
            S.op("dve", lambda e: e.scalar_tensor_tensor(
                out=buf[:, 0:512], in0=ylo[:, lb, :], scalar=col, in1=gt[0][:, 0:512],
                op0=ALU.mult, op1=ALU.mult), reads=[("ylo", lb), key, ("gt", 0)], writes=[bkey])
            S.op("dve", lambda e: e.scalar_tensor_tensor(
                out=buf[:, 512:1024], in0=yhb[yb_], scalar=col, in1=gt[0][:, 512:1024],
                op0=ALU.mult, op1=ALU.mult), reads=[("yhb", yb_), key, ("gt", 0), bkey], writes=[bkey])
            S.dma(lambda e: e.dma_start(out=out[ob * 128:(ob + 1) * 128, :], in_=buf, accum_op=ALU.add),
                  reads=[bkey], writes=[("outd", ob)], queue="pool")

        def ffn_down(th):
            fin_th[0] = th
            ssA = {}
            cols = {}
            ybs = {}

            def mm_pass(half, lbs):
                for kc in range(32):
                    db = wd_i[0] % 8; wd_i[0] += 1
                    S.dma(lambda e, kc=kc, db=db: e.dma_start(
                        out=wdnb[db], in_=w_down[kc * 128:(kc + 1) * 128, half * 512:(half + 1) * 512]),
                          writes=[("wdn", db)], queue="pool")
                    for lb in lbs:
                        mm(psum[:, lb, :], actT[:, kc, lb * 128:(lb + 1) * 128], wdnb[db], kc == 0, kc == 31,
                           [("actT", kc, lb // 4), ("wdn", db)], PB(lb))

            def epi(half, lbs):
                for lb in lbs:
                    col, key = rs_col()
                    cols[lb] = (col, key)
                    S.op("act", lambda e, lb=lb, col=col: e.activation(out=junk[:, 0:512], in_=psum[:, lb, :],
                                                                       func=AF.Square, accum_out=col),
                         reads=[PB(lb)], writes=[key])
                    if half == 0:
                        ssA[lb] = (col, key)
                        S.op("act", lambda e, lb=lb: e.copy(out=ylo[:, lb, :], in_=psum[:, lb, :]),
                             reads=[PB(lb)], writes=[("ylo", lb)])
                    else:
                        colA, keyA = ssA[lb]
                        yb_ = yh_i[0] % 4; yh_i[0] += 1
                        ybs[lb] = yb_
                        S.op("act", lambda e, lb=lb, yb_=yb_: e.copy(out=yhb[yb_], in_=psum[:, lb, :]),
                             reads=[PB(lb)], writes=[("yhb", yb_)])
                        S.op("dve", lambda e, col=col, colA=colA: e.tensor_tensor(out=col, in0=col, in1=colA, op=ALU.add),
                             reads=[key, keyA], writes=[key])
                if half == 1:
                    for lb in lbs:
                        fin(lb, cols, ybs)

            mm_pass(0, range(8))
            epi(0, range(8))
            mm_pass(1, range(0, 4))
            mm_pass(1, range(4, 8))
            epi(1, range(0, 4))
            epi(1, range(4, 8))
        def xpre_loads():
            pass

        load_up(0)
        ffn_up(0, preloaded=1, xprefetch=(stage >= 4))
        load_up(0)
        load_up(1)
        ffn_h3T(1)
        ffn_down(0)
        ffn_up(1, preloaded=2)
        ffn_down(1)
        S.barrier()
        S.emit()
    return nc


_CACHE = {}


def _prep_inputs(inp):
    f = lambda a: np.ascontiguousarray(np.asarray(a, dtype=np.float32))
    x = f(inp["x"]); mem = f(inp["mem"])
    gains = np.stack([f(inp[k])[0] for k in ("norm_mix_pre", "norm_mix_post", "norm_mem", "norm_xa_pre",
                                             "norm_xa_post", "norm_ffn_pre", "norm_ffn_post")], 0)
    shared = {
        "gains": gains, "w_in": f(inp["w_in"])[0], "b_forget": f(inp["b_forget"])[0].reshape(12, 1),
        "w_pool": f(inp["w_pool"])[0], "pool_scale": f(inp["pool_scale"])[0].reshape(256, 1),
        "w_mix_out": f(inp["w_mix_out"])[0], "w_xq": f(inp["w_xq"])[0], "w_xkv": f(inp["w_xkv"])[0],
        "w_xo": f(inp["w_xo"])[0], "w_up": f(inp["w_up"])[0], "conv_w": f(inp["conv_w"])[0].reshape(3, 8192),
        "conv_b": f(inp["conv_b"])[0].reshape(1, 8192), "w_down": f(inp["w_down"])[0],
    }
    bf = ml_dtypes.bfloat16
    in_maps = []
    wins = np.array([2, 4, 8, 16], np.float32)
    for c in range(8):
        b, half = c // 2, c % 2
        xk = np.zeros((NT, D), np.float32)
        if half == 1:
            xk[:2048] = x[b, :2048]
        xk[2048:] = x[b, half * 2048:(half + 1) * 2048]
        kconst = np.zeros((4, NT), np.float32)
        kconst[0:3] = -1.0
        if half == 0:
            kconst[3, :2048] = MASKV
        qconst = np.ones((4, NQ), np.float32)
        qconst[3, :128] = 0.0
        invc = np.zeros((2, 128, 16), np.float32)
        t1 = np.arange(1, 17, dtype=np.float32)
        for g in range(4):
            cc, hf = g // 2, g % 2
            if half == 0:
                invc[cc, hf * 64:(hf + 1) * 64, :] = 1.0 / np.minimum(t1, wins[g])[None, :]
            else:
                invc[cc, hf * 64:(hf + 1) * 64, :] = 1.0 / wins[g]
        flags = np.ones((128, 2), np.float32)
        flags[:, 0] = float(half)
        m = dict(shared)
        m.update({"xk": xk, "mem": mem[b], "kconst": kconst.astype(bf), "qconst": qconst.astype(bf),
                  "invc": invc, "flags": flags})
        in_maps.append(m)
    return in_maps


def kernel(**inputs):
    if "nc" not in _CACHE:
        _CACHE["nc"] = build()
    nc = _CACHE["nc"]
    in_maps = _prep_inputs(inputs)
    res = run_bass_kernel_spmd(nc, in_maps, core_ids=list(range(8)))
    outp = np.zeros((4, 4096, D), np.float32)
    for c in range(8):
        b, half = c // 2, c % 2
        outp[b, half * 2048:(half + 1) * 2048] = res.results[c]["out"]
    return outp
```

```python
import contextlib
import numpy as np
import ml_dtypes
import concourse.bass as bass
import concourse.mybir as mybir
from concourse.bass_utils import run_bass_kernel_spmd

F32 = mybir.dt.float32
BF16 = mybir.dt.bfloat16
AF = mybir.ActivationFunctionType
ALU = mybir.AluOpType

COMPUTE = ("pe", "act", "dve", "pool")
D = 1024
DIN = 2572
NT = 4096
NQ = 2176
QOFF = 1920
NQB = 17
MASKV = -30000.0


class Sched:
    def __init__(self, nc, es, n_dma_sems=48, same_engine_sync=True):
        self.nc = nc
        self.sem = {e: es.enter_context(nc.semaphore("s_" + e)) for e in COMPUTE}
        self.cnt = {e: 0 for e in COMPUTE}
        self.dsems = [es.enter_context(nc.semaphore("d%d" % i)) for i in range(n_dma_sems)]
        self.dcnt = [0] * n_dma_sems
        self.dnext = 0
        self.streams = {e: [] for e in COMPUTE + ("sp",)}
        self.waited = {e: {} for e in COMPUTE + ("sp",)}
        self.lastw = {}
        self.readers = {}
        self.same = same_engine_sync

    def _need(self, eng, token, waits, allow_same=True):
        if token is None:
            return
        key, val = token
        if key == eng and not (self.same and allow_same):
            return
        if self.waited[eng].get(key, 0) >= val:
            return
        self.waited[eng][key] = val
        waits.append(token)

    def _deps(self, eng, reads, writes, waits, accum=False):
        for r in reads:
            self._need(eng, self.lastw.get(r), waits)
        for w in writes:
            self._need(eng, self.lastw.get(w), waits, allow_same=not accum)
            for k, v in self.readers.get(w, {}).items():
                if k != eng:
                    self._need(eng, (k, v), waits)

    def _commit(self, token, reads, writes):
        k, v = token
        for r in reads:
            d = self.readers.setdefault(r, {})
            if d.get(k, 0) < v:
                d[k] = v
        for w in writes:
            self.lastw[w] = token
            self.readers[w] = {}

    def op(self, eng, fn, reads=(), writes=(), accum=False):
        waits = []
        self._deps(eng, reads, writes, waits, accum)
        self.cnt[eng] += 1
        token = (eng, self.cnt[eng])
        self.streams[eng].append((fn, waits, None))
        self._commit(token, reads, writes)
        return token

    def dma(self, fn, reads=(), writes=(), queue="sp"):
        waits = []
        self._deps(queue, reads, writes, waits)
        i = self.dnext
        self.dnext = (self.dnext + 1) % len(self.dsems)
        key = ("d", i)
        if self.dcnt[i] > 0:
            self._need(queue, (key, 16 * self.dcnt[i]), waits)
        self.dcnt[i] += 1
        token = (key, 16 * self.dcnt[i])
        self.streams[queue].append((fn, waits, i))
        self._commit(token, reads, writes)
        return token

    def alias(self, new_keys, old_keys):
        toks = {}
        for ok in old_keys:
            lw = self.lastw.get(ok)
            if lw is not None:
                toks[lw[0]] = max(toks.get(lw[0], 0), lw[1])
            for k, v in self.readers.get(ok, {}).items():
                toks[k] = max(toks.get(k, 0), v)
        for nk in new_keys:
            d = self.readers.setdefault(nk, {})
            for k, v in toks.items():
                d[k] = max(d.get(k, 0), v)

    def barrier(self):
        toks = [(e, self.cnt[e]) for e in COMPUTE if self.cnt[e] > 0]
        toks += [(("d", i), 16 * c) for i, c in enumerate(self.dcnt) if c > 0]
        for eng in COMPUTE + ("sp",):
            waits = []
            for t in toks:
                self._need(eng, t, waits)
            if waits:
                self.streams[eng].append((None, waits, None))

    def semh(self, key):
        if isinstance(key, tuple):
            return self.dsems[key[1]]
        return self.sem[key]

    def replay(self, eng, e):
        for fn, waits, di in self.streams[eng]:
            for key, val in waits:
                e.wait_ge(self.semh(key), val)
            if fn is None:
                continue
            inst = fn(e)
            if di is not None:
                inst.then_inc(self.dsems[di], 16)
            else:
                inst.then_inc(self.sem[eng], 1)

    def emit(self):
        with self.nc.Block() as block:
            @block.tensor
            def _(e):
                self.replay("pe", e)

            @block.scalar
            def _(e):
                self.replay("act", e)

            @block.vector
            def _(e):
                self.replay("dve", e)

            @block.gpsimd
            def _(e):
                self.replay("pool", e)

            @block.sync
            def _(e):
                self.replay("sp", e)


def build(stage=4):
    nc = bass.Bass("TRN2", target_bir_lowering=False)

    def dt(name, shape, dty=F32, kind="ExternalInput"):
        return nc.dram_tensor(name, shape, dty, kind=kind).ap()

    xk = dt("xk", [NT, D])
    memd = dt("mem", [256, D])
    gains = dt("gains", [7, D])
    w_in = dt("w_in", [D, DIN])
    b_f = dt("b_forget", [12, 1])
    w_pool = dt("w_pool", [4, 64, 64])
    pscale = dt("pool_scale", [256, 1])
    w_mix = dt("w_mix_out", [D, D])
    w_xq = dt("w_xq", [D, D])
    w_xkv = dt("w_xkv", [D, 2 * D])
    w_xo = dt("w_xo", [D, D])
    w_up = dt("w_up", [D, 8192])
    conv_w = dt("conv_w", [3, 8192])
    conv_b = dt("conv_b", [1, 8192])
    w_down = dt("w_down", [4096, D])
    kconst = dt("kconst", [4, NT], BF16)
    qconst = dt("qconst", [4, NQ], BF16)
    invc = dt("invc", [2, 128, 16])
    flags = dt("flags", [128, 2])
    out = dt("out", [2048, D], kind="ExternalOutput")
    gsc = dt("gsc", [12, 3, NT], BF16, kind="Internal")

    with contextlib.ExitStack() as es:
        S = Sched(nc, es)

        def sb(name, shape, dty):
            return es.enter_context(nc.sbuf_tensor(name, shape, dty))

        A1 = sb("A1", [128, 32768], BF16)
        A2 = sb("A2", [128, 34816], BF16)
        A3 = sb("A3", [128, 15360], BF16)
        xl = [sb("xl%d" % i, [128, D], F32) for i in range(2)]
        hb = [sb("hb%d" % i, [128, D], BF16) for i in range(3)]
        junk = sb("junk", [128, D], BF16)
        gt = [sb("gt%d" % i, [128, D], F32) for i in range(3)]
        xhalo = sb("xhalo", [128, D], F32)
        ident = sb("ident", [128, 128], BF16)
        identf = sb("identf", [128, 128], F32)
        cbias = sb("cbias", [128, 128], BF16)
        cbf = sb("cbf", [128, 128], F32)
        rs = sb("rs", [128, 64], F32)
        rl = sb("rl", [128, 16], F32)
        flagt = sb("flagt", [128, 2], F32)
        pscol = sb("pscol", [128, 2], F32)
        negb = sb("negb", [12, 1], F32)
        epst = sb("epst", [128, 1], F32)
        pmk = sb("pmk", [128, 4], F32)
        cw = sb("cw", [128, 4, 64], F32)
        memkv = sb("memkv", [128, 4104], BF16)
        invt = sb("invt", [128, 2, 16], F32)
        h3h = sb("h3h", [128, 8, 2], BF16)
        psum = es.enter_context(nc.psum_tensor("psum", [128, 8, 512], F32))
        pst = psum[:, 7, :].bitcast(BF16)
        PB = lambda b: ("ps", b)

        hT = A1[:].rearrange("p (c t) -> p c t", c=8)
        xres = A1[:].bitcast(F32).rearrange("p (b d) -> p b d", b=16)
        actT = A1[:].rearrange("p (c t) -> p c t", c=32)

        KP = A2[:, 0:8192].rearrange("p (s t) -> p s t", s=2)
        QP = A2[:, 8192:12544].rearrange("p (s t) -> p s t", s=2)
        VP = A2[:, 12544:16704].rearrange("p (b s d) -> p b s d", b=32, s=2)
        ymix = A2[:, 16704:16704 + 17 * 768].rearrange("p (b d) -> p b d", b=17)

        def xr(qb):
            return xhalo[:] if qb == 0 else xres[:, qb - 1, :]

        rs_i = [0]

        def rs_col():
            rs_i[0] = (rs_i[0] + 1) % 64
            return rs[:, rs_i[0]:rs_i[0] + 1], ("rs", rs_i[0])

        rl_i = [0]

        def rl_col():
            rl_i[0] = (rl_i[0] + 1) % 16
            return rl[:, rl_i[0]:rl_i[0] + 1], ("rl", rl_i[0])

        def finish_rstd(col, key):
            S.op("act", lambda e: e.activation(out=col, in_=col, func=AF.Sqrt, scale=1.0 / D, bias=epst[:, 0:1]),
                 reads=[key, "epst"], writes=[key])
            S.op("dve", lambda e: e.reciprocal(out=col, in_=col), reads=[key], writes=[key])

        def rstd_of(src, src_reads):
            col, key = rs_col()
            S.op("act", lambda e: e.activation(out=junk[:], in_=src, func=AF.Square, accum_out=col),
                 reads=src_reads, writes=[key])
            finish_rstd(col, key)
            return col, key

        def load_gain(i, gi):
            S.dma(lambda e: e.dma_start(out=gt[i][:], in_=gains[gi:gi + 1, :].partition_broadcast(128)),
                  writes=[("gt", i)])

        hb_i = [0]
        sb_i = [0]

        ev_i = [0]

        def norm_p1(src, src_reads, gi):
            col, key = rstd_of(src, src_reads)
            hb_i[0] = (hb_i[0] + 1) % 3
            i = hb_i[0]
            S.op("dve", lambda e: e.scalar_tensor_tensor(out=hb[i][:], in0=src, scalar=col, in1=gt[gi][:],
                                                         op0=ALU.mult, op1=ALU.mult),
                 reads=list(src_reads) + [key, ("gt", gi)], writes=[("hb", i)])
            return i

        def norm_p2(i, dst_fn, dst_key, sel=None):
            for c in range(8):
                S.op("pe", lambda e, c=c: e.transpose(out=pst[:, c * 128:(c + 1) * 128],
                                                      in_=hb[i][:, c * 128:(c + 1) * 128], identity=ident[:]),
                     reads=[("hb", i), "ident"], writes=[PB(7)], accum=True)
            src_v = pst.rearrange("p (c t) -> p c t", c=8)
            if sel is not None:
                src_v = src_v[:, :, sel[0]:sel[1]]
            ev_i[0] ^= 1
            if ev_i[0]:
                S.op("act", lambda e: e.copy(out=dst_fn(), in_=src_v), reads=[PB(7)], writes=[dst_key])
            else:
                S.op("dve", lambda e: e.tensor_copy(out=dst_fn(), in_=src_v), reads=[PB(7)], writes=[dst_key])

        def norm_seq(items):
            pend = []
            for (pre, src, src_reads, gi, dst_fn, dst_key, sel) in items:
                if pre is not None:
                    pre()
                i = norm_p1(src, src_reads, gi)
                pend.append((i, dst_fn, dst_key, sel))
                if len(pend) > 2:
                    norm_p2(*pend.pop(0))
            for p_ in pend:
                norm_p2(*p_)

        def load_w(dram, r0, nk, c0, ncols, dst, dst_key, eng=None, after=()):
            S.dma(lambda e: e.dma_start(out=dst, in_=dram[r0:r0 + nk * 128, c0:c0 + ncols]
                                        .rearrange("(k p) n -> p k n", p=128)), reads=list(after), writes=[dst_key],
                  queue="pool")

        def mm(out_ap, lhsT, rhs, start, stop, reads, wkey):
            S.op("pe", lambda e: e.matmul(out=out_ap, lhsT=lhsT, rhs=rhs, start=start, stop=stop),
                 reads=reads, writes=[wkey], accum=True)

        S.op("pool", lambda e: e.memset(epst[:], 1e-6), writes=["epst"])
        for (pr_, vals_) in ((slice(0, 64), (0.0, 0.5, 0.125)), (slice(64, 128), (1.0, 0.25, 0.0625))):
            for c_, v_ in enumerate(vals_):
                S.op("dve", lambda e, pr_=pr_, c_=c_, v_=v_: e.memset(pmk[pr_, c_:c_ + 1], v_), writes=["pmk"])
        S.op("pool", lambda e: e.memset(identf[:], 0.0), writes=["identf"])
        S.op("pool", lambda e: e.affine_select(out=identf[:], in_=identf[:], pattern=[[-1, 128]],
                                               compare_op=ALU.not_equal, fill=1.0, base=0, channel_multiplier=1),
             reads=["identf"], writes=["identf"])
        S.op("pool", lambda e: e.tensor_copy(out=ident[:], in_=identf[:]), reads=["identf"], writes=["ident"])
        S.op("pool", lambda e: e.memset(cbf[:], 0.0), writes=["cbf"])
        S.op("pool", lambda e: e.affine_select(out=cbf[:], in_=cbf[:], pattern=[[1, 128]],
                                               compare_op=ALU.is_ge, fill=MASKV, base=0, channel_multiplier=-1),
             reads=["cbf"], writes=["cbf"])
        S.op("pool", lambda e: e.tensor_copy(out=cbias[:], in_=cbf[:]), reads=["cbf"], writes=["cbias"])
        S.dma(lambda e: e.dma_start(out=flagt[:], in_=flags[:, :]), writes=["flagt"])
        S.dma(lambda e: e.dma_start(out=negb[:], in_=b_f[:, :]), writes=["negb"])
        S.op("act", lambda e: e.mul(out=negb[:], in_=negb[:], mul=-1.0), reads=["negb"], writes=["negb"])
        for cc in range(2):
            S.dma(lambda e, cc=cc: e.dma_start(out=pscol[:, cc:cc + 1], in_=pscale[cc * 128:(cc + 1) * 128, :]),
                  writes=["pscol"])

        o = 0
        wf = A3[:, o:o + 96].rearrange("p (k n) -> p k n", k=8); o += 96
        wpair = A3[:, o:o + 3072].rearrange("p (k w n) -> p k w n", k=8, w=3); o += 3072
        wu = A3[:, o:o + 2048].rearrange("p (k n) -> p k n", k=8); o += 2048
        wpbd = A3[:, o:o + 256].rearrange("p (c n) -> p c n", c=2); o += 256
        PT = [A3[:, o + i * 512:o + (i + 1) * 512] for i in range(3)]; o += 1536
        vtmp = A3[:, o:o + 512]; o += 512
        ypT = A3[:, o:o + 4352].rearrange("p (c t) -> p c t", c=2); o += 4352
        go = 16704
        g_et = A2[0:12, go:go + 1024].bitcast(F32); go += 1024
        g_sps = [A2[0:12, go + i * 1024:go + (i + 1) * 1024].bitcast(F32) for i in range(8)]; go += 8192
        g_G = [A2[0:12, go + i * 1024:go + (i + 1) * 1024].bitcast(F32) for i in range(2)]; go += 2048
        g_r1 = A2[0:12, go:go + 1024].bitcast(F32); go += 1024
        g_r2 = A2[0:12, go:go + 1024].bitcast(F32); go += 1024
        g_ones = A2[0:12, go:go + 512]; go += 512
        g_parts = [A2[0:12, go + i * 1536:go + (i + 1) * 1536].rearrange("p (a t) -> p a t", a=3) for i in range(2)]
        go += 3072
        wpst = A3[:, o:o + 512].bitcast(F32).rearrange("p (c n) -> p c n", c=2); o += 512
        yT = A3[:, o:o + 768].rearrange("p (c t) -> p c t", c=6); o += 768
        tmpf = A3[:, o:o + 2048].bitcast(F32); o += 2048
        assert o <= 15360, o

        G_KEYS = ["g_et"] + [("g_sp", i) for i in range(8)] + [("gG", 0), ("gG", 1), "g_r1", "g_r2", "g_ones", ("gp", 0), ("gp", 1)]
        hTk = lambda t0, n: [("hT", b) for b in range(t0 // 128, (t0 + n + 127) // 128)]
        load_gain(0, 0)
        load_gain(2, 2)
        wxk = A2[:, 16704:24896].rearrange("p (k n) -> p k n", k=8)
        wxv = A2[:, 24896:33088].rearrange("p (k n) -> p k n", k=8)
        memT = xhalo[:].bitcast(BF16).rearrange("p (k t) -> p k t", k=8)
        kTm = memkv[:, 0:2048].rearrange("p (k t) -> p k t", k=8)
        Vm = memkv[:, 2048:4104].rearrange("p (m h d) -> p m h d", m=2, h=4)

        def pair_weights(pair):
            for which in range(3):
                load_w(w_in, 0, 8, 256 + which * 768 + pair * 128, 128, wpair[:, :, which, :], ("wpair", which))

        S.op("dve", lambda e: e.memset(Vm[:, :, :, 256:257], 1.0), writes=["Vmones"])
        items = []
        for mb in range(2):
            i = mb % 2
            pre = (lambda mb=mb, i=i: S.dma(lambda e: e.dma_start(out=xl[i][:], in_=memd[mb * 128:(mb + 1) * 128, :]),
                                            writes=[("xl", i)]))
            items.append((pre, xl[i][:], [("xl", i)], 2, (lambda mb=mb: memT[:, :, mb * 128:(mb + 1) * 128]),
                          ("memT", mb), None))
        norm_seq(items)
        load_w(w_in, 0, 8, 2560, 12, wf, "wf", after=[("memT", 1)])
        load_w(w_in, 0, 8, 0, 128, wu[:, :, 0:128], "wu")
        load_w(w_in, 0, 8, 128, 128, wu[:, :, 128:256], "wu")
        S.op("dve", lambda e: e.memset(wpst[:], 0.0), writes=["wpst"])
        for g in range(4):
            cc, hf = g // 2, g % 2
            S.dma(lambda e, g=g, cc=cc, hf=hf: e.dma_start(out=wpst[hf * 64:(hf + 1) * 64, cc, hf * 64:(hf + 1) * 64],
                                                           in_=w_pool[g, :, :]), reads=["wpst"], writes=["wpst"])
        for cc in range(2):
            S.dma(lambda e, cc=cc: e.dma_start(out=invt[:, cc, :], in_=invc[cc, :, :]), writes=["invt"])
        pair_weights(0)
        for k2 in range(8):
            load_w(w_xkv, k2 * 128, 1, 0, 1024, wxk[:, k2:k2 + 1, :], ("wxk", k2 // 2))
            load_w(w_xkv, k2 * 128, 1, 1024, 1024, wxv[:, k2:k2 + 1, :], ("wxv", k2 // 2))
        S.op("pool", lambda e: e.tensor_copy(out=wpbd, in_=wpst), reads=["wpst"], writes=["wpbd"])
        items = []
        xs = A2[:, 0:16384].bitcast(F32).rearrange("p (r d) -> p r d", r=8)
        for kb in range(32):
            i = kb % 8
            pre = (lambda kb=kb, i=i: S.dma(lambda e: e.dma_start(out=xs[:, i, :], in_=xk[kb * 128:(kb + 1) * 128, :]),
                                            reads=([("xs", (kb - 3) % 8)] if kb >= 3 else []), writes=[("xs", i)]))
            items.append((pre, xs[:, i, :], [("xs", i)], 0, (lambda kb=kb: hT[:, :, kb * 128:(kb + 1) * 128]),
                          ("hT", kb), None))
        norm_seq(items[:16])
        mTk = [("memT", 0), ("memT", 1)]
        for fc in range(8):
            b = sb_i[0] % 3; sb_i[0] += 1
            for kc in range(8):
                mm(psum[:, b, 0:256], wxk[:, kc, fc * 128:(fc + 1) * 128], memT[:, kc, :], kc == 0, kc == 7,
                   mTk + [("wxk", kc // 2)], PB(b))
            S.op("act", lambda e, b=b, fc=fc: e.copy(out=kTm[:, fc, :], in_=psum[:, b, 0:256]),
                 reads=[PB(b)], writes=["kTm"])
        for mb in range(2):
            for n in range(2):
                b = sb_i[0] % 3; sb_i[0] += 1
                for kc in range(8):
                    mm(psum[:, b, :], memT[:, kc, mb * 128:(mb + 1) * 128], wxv[:, kc, n * 512:(n + 1) * 512],
                       kc == 0, kc == 7, mTk + [("wxv", kc // 2)], PB(b))
                S.op("act", lambda e, b=b, mb=mb, n=n: e.copy(out=Vm[:, mb, 2 * n:2 * n + 2, 0:256],
                                                              in_=psum[:, b, :].rearrange("p (h d) -> p h d", h=2)),
                     reads=[PB(b)], writes=["Vm"])
        norm_seq(items[16:])
        S.barrier()
        S.op("dve", lambda e: e.memset(KP[0:64, 1, :], 0.0), writes=[("KPaug", 1)])
        S.op("dve", lambda e: e.memset(QP[0:64, 1, :], 0.0), writes=[("QPaug", 1)])
        S.op("pool", lambda e: e.memset(KP[64:128, 0, :], 0.0), writes=[("KPaug", 0)])
        S.op("pool", lambda e: e.memset(QP[64:128, 0, :], 0.0), writes=[("QPaug", 0)])
        S.op("dve", lambda e: e.memset(VP[:, :, :, 64:65], 1.0), writes=["VPones"])
        for s in range(2):
            base = 64 if s == 0 else 0
            S.dma(lambda e, s=s, base=base: e.dma_start(out=KP[base + 3:base + 7, s, :], in_=kconst[:, :]),
                  writes=[("KPaug", s)])
            S.dma(lambda e, s=s, base=base: e.dma_start(out=QP[base:base + 3, s, :], in_=qconst[0:3, :]),
                  writes=[("QPaug", s)])
            S.dma(lambda e, s=s, base=base: e.dma_start(out=QP[base + 6:base + 7, s, :], in_=qconst[3:4, :]),
                  writes=[("QPaug", s)])

        qchunks = [(QOFF, 128, 0)] + [(2048 + 512 * i, 512, 128 + 512 * i) for i in range(4)]
        pt_i = [0]
        npairs = 6 if stage >= 1 else 0
        def pair_aug(pair):
            for s in range(2):
                h = 2 * pair + s
                base = 64 if s == 0 else 0
                S.dma(lambda e, s=s, h=h, base=base: e.dma_start(out=KP[base:base + 3, s, :], in_=gsc[h, :, :]),
                      reads=["gsc"], writes=[("KPaug", s)])
                S.dma(lambda e, s=s, h=h, base=base: e.dma_start(out=QP[base + 3:base + 6, s, :],
                                                                 in_=gsc[h, :, QOFF:NT]),
                      reads=["gsc"], writes=[("QPaug", s)])

        def g_chain():
            S.alias(G_KEYS, [("wxk", k) for k in range(4)] + [("wxv", k) for k in range(4)] + [("memT", 0), ("memT", 1)])
            S.op("dve", lambda e: e.memset(g_ones, 1.0), writes=["g_ones"])
            for tc in range(8):
                b = tc % 3
                for kc in range(8):
                    mm(psum[0:12, b, :], wf[:, kc, :], hT[:, kc, tc * 512:(tc + 1) * 512], kc == 0, kc == 7,
                       ["wf"] + hTk(tc * 512, 512), PB(b))
                S.op("act", lambda e, b=b: e.activation(out=g_et, in_=psum[0:12, b, :], func=AF.Exp,
                                                        bias=negb[:, 0:1], scale=-1.0),
                     reads=[PB(b), "negb"], writes=["g_et"])
                S.op("act", lambda e, tc=tc: e.activation(out=g_sps[tc], in_=g_et, func=AF.Ln, bias=1.0, scale=1.0),
                     reads=["g_et"], writes=[("g_sp", tc)])
            for tc in range(8):
                gi = tc % 2
                init = 0.0 if tc == 0 else g_G[1 - gi][:, 511:512]
                S.op("dve", lambda e, gi=gi, init=init, tc=tc: e.tensor_tensor_scan(out=g_G[gi], data0=g_ones, data1=g_sps[tc],
                                                                                    initial=init, op0=ALU.mult, op1=ALU.add),
                     reads=[("g_sp", tc), "g_ones", ("gG", 1 - gi)], writes=[("gG", gi)])
                pp = g_parts[gi]
                S.op("dve", lambda e, gi=gi, pp=pp: e.tensor_copy(out=pp[:, 0, :], in_=g_G[gi]),
                     reads=[("gG", gi)], writes=[("gp", gi)])
                S.op("dve", lambda e, gi=gi, pp=pp: e.tensor_tensor(out=g_r1, in0=g_G[gi], in1=pp[:, 0, :], op=ALU.subtract),
                     reads=[("gG", gi), ("gp", gi)], writes=["g_r1"])
                S.op("dve", lambda e, pp=pp: e.tensor_copy(out=pp[:, 1, :], in_=g_r1), reads=["g_r1"], writes=[("gp", gi)])
                S.op("dve", lambda e, pp=pp: e.tensor_tensor(out=g_r2, in0=g_r1, in1=pp[:, 1, :], op=ALU.subtract),
                     reads=["g_r1", ("gp", gi)], writes=["g_r2"])
                S.op("dve", lambda e, pp=pp: e.tensor_copy(out=pp[:, 2, :], in_=g_r2), reads=["g_r2"], writes=[("gp", gi)])
                S.dma(lambda e, tc=tc, pp=pp: e.dma_start(out=gsc[:, :, tc * 512:(tc + 1) * 512], in_=pp),
                      reads=[("gp", gi)], writes=["gsc"])
            S.alias([("ymix", qb) for qb in range(NQB)], G_KEYS)

        def pair_proj(pair):
            for ci, (t0, n, q0) in enumerate(qchunks):
                b = sb_i[0] % 3; sb_i[0] += 1
                for kc in range(8):
                    mm(psum[:, b, 0:n], wpair[:, kc, 0, :], hT[:, kc, t0:t0 + n], kc == 0, kc == 7,
                       [("wpair", 0)] + hTk(t0, n), PB(b))
                S.op("act", lambda e, b=b, n=n, q0=q0: e.activation(out=QP[0:64, 0, q0:q0 + n], in_=psum[0:64, b, 0:n],
                                                                    func=AF.Copy, scale=0.125),
                     reads=[PB(b)], writes=[("QP", 0, ci)])
                if pair == 0:
                    S.op("act", lambda e, b=b, n=n, q0=q0: e.activation(out=QP[64:128, 1, q0:q0 + n],
                                                                        in_=psum[64:128, b, 0:n], func=AF.Copy, scale=0.125),
                         reads=[PB(b)], writes=[("QP", 1, ci)])
                else:
                    S.op("dve", lambda e, b=b, n=n, q0=q0: e.tensor_scalar(out=QP[64:128, 1, q0:q0 + n],
                                                                           in0=psum[64:128, b, 0:n], scalar1=0.125,
                                                                           scalar2=None, op0=ALU.mult),
                         reads=[PB(b)], writes=[("QP", 1, ci)])
            for tc in range(8):
                b = sb_i[0] % 3; sb_i[0] += 1
                for kc in range(8):
                    mm(psum[:, b, :], wpair[:, kc, 1, :], hT[:, kc, tc * 512:(tc + 1) * 512], kc == 0, kc == 7,
                       [("wpair", 1)] + hTk(tc * 512, 512), PB(b))
                S.op("act", lambda e, b=b, tc=tc: e.copy(out=KP[0:64, 0, tc * 512:(tc + 1) * 512], in_=psum[0:64, b, :]),
                     reads=[PB(b)], writes=[("KP", 0, tc)])
                if pair == 0:
                    S.op("act", lambda e, b=b, tc=tc: e.copy(out=KP[64:128, 1, tc * 512:(tc + 1) * 512],
                                                             in_=psum[64:128, b, :]),
                         reads=[PB(b)], writes=[("KP", 1, tc)])
                else:
                    S.op("dve", lambda e, b=b, tc=tc: e.tensor_copy(out=KP[64:128, 1, tc * 512:(tc + 1) * 512],
                                                                    in_=psum[64:128, b, :]),
                         reads=[PB(b)], writes=[("KP", 1, tc)])
            for tc in range(8):
                b = sb_i[0] % 3; sb_i[0] += 1
                for kc in range(8):
                    mm(psum[:, b, :], wpair[:, kc, 2, :], hT[:, kc, tc * 512:(tc + 1) * 512], kc == 0, kc == 7,
                       [("wpair", 2)] + hTk(tc * 512, 512), PB(b))
                S.op("act", lambda e, b=b: e.copy(out=vtmp, in_=psum[:, b, :]), reads=[PB(b)], writes=["vtmp"])
                for i in range(4):
                    S.op("pe", lambda e, i=i: e.transpose(out=pst[:, i * 128:(i + 1) * 128],
                                                          in_=vtmp[:, i * 128:(i + 1) * 128], identity=ident[:]),
                         reads=["vtmp", "ident"], writes=[PB(7)], accum=True)
                if pair == 0:
                    S.op("act", lambda e, tc=tc: e.copy(
                        out=VP[:, 4 * tc:4 * tc + 4, :, 0:64],
                        in_=pst[:, 0:512].rearrange("p (b s d) -> p b s d", b=4, s=2)),
                         reads=[PB(7)], writes=[("VP", tc)])
                else:
                    S.op("dve", lambda e, tc=tc: e.tensor_copy(
                        out=VP[:, 4 * tc:4 * tc + 4, :, 0:64],
                        in_=pst[:, 0:512].rearrange("p (b s d) -> p b s d", b=4, s=2)),
                         reads=[PB(7)], writes=[("VP", tc)])

        def pair_attn(pair):
            steps = []
            for s in range(2):
                h = 2 * pair + s
                rows = slice(0, 128)
                for ci, (qb0, nqb) in enumerate([(0, 1), (1, 4), (5, 4), (9, 4), (13, 4)]):
                    q0 = qb0 * 128
                    klast = 15 + qb0 + nqb - 1
                    if ci == 0:
                        for g in range(4):
                            b = sb_i[0] % 3; sb_i[0] += 1
                            pi = pt_i[0] % 3; pt_i[0] += 1

                            def fS0(s=s, g=g, b=b):
                                for t in range(4):
                                    j = 4 * g + t
                                    dg = (j == 15)
                                    mm(psum[:, b, t * 128:(t + 1) * 128], KP[:, s, j * 128:(j + 1) * 128], QP[:, s, 0:128],
                                       True, not dg, [("KP", s, g), ("KPaug", s), ("QP", s, 0), ("QPaug", s)], PB(b))
                                    if dg:
                                        mm(psum[:, b, t * 128:(t + 1) * 128], ident[:], cbias[:], False, True,
                                           ["ident", "cbias"], PB(b))

                            def fE0(b=b, pi=pi):
                                S.op("act", lambda e: e.activation(out=PT[pi][:, 0:512], in_=psum[:, b, 0:512], func=AF.Exp),
                                     reads=[PB(b)], writes=[("PT", pi)])

                            def fPV0(s=s, g=g, pi=pi, h=h):
                                for t in range(4):
                                    j = 4 * g + t
                                    mm(psum[:, 3, 0:65], PT[pi][:, t * 128:(t + 1) * 128], VP[:, j, s, :],
                                       j == 0, j == 15, [("PT", pi), ("VP", g), "VPones"], PB(3))
                                if g == 3:
                                    col, key = rl_col()
                                    S.op("dve", lambda e: e.reciprocal(out=col, in_=psum[:, 3, 64:65]),
                                         reads=[PB(3)], writes=[key])
                                    S.op("dve", lambda e: e.tensor_scalar(
                                        out=ymix[:, 0, h * 64:(h + 1) * 64], in0=psum[:, 3, 0:64], scalar1=col,
                                        scalar2=None, op0=ALU.mult), reads=[PB(3), key], writes=[("ymix", 0)])

                            steps.append((fS0, fE0, fPV0))
                        continue
                    for j in range(klast + 1):
                        m = max(0, j - (15 + qb0))
                        if m == 3:
                            continue
                        c0, c1 = m * 128, nqb * 128
                        b = sb_i[0] % 3; sb_i[0] += 1
                        pi = pt_i[0] % 3; pt_i[0] += 1
                        diag = j >= 15 + qb0
                        if m == 2:
                            def fS2(s=s, ci=ci, q0=q0, j=j, b=b):
                                kk = [("KP", s, j // 4), ("KP", s, (j + 1) // 4), ("KPaug", s), ("QP", s, ci), ("QPaug", s)]
                                mm(psum[:, b, 0:256], KP[:, s, j * 128:(j + 1) * 128], QP[:, s, q0 + 256:q0 + 512],
                                   True, False, kk, PB(b))
                                mm(psum[:, b, 0:128], ident[:], cbias[:], False, True, ["ident", "cbias"], PB(b))
                                mm(psum[:, b, 256:384], KP[:, s, (j + 1) * 128:(j + 2) * 128], QP[:, s, q0 + 384:q0 + 512],
                                   True, False, kk, PB(b))
                                mm(psum[:, b, 256:384], ident[:], cbias[:], False, True, ["ident", "cbias"], PB(b))

                            def fE2(b=b, pi=pi):
                                S.op("act", lambda e: e.activation(out=PT[pi][:, 0:384], in_=psum[:, b, 0:384], func=AF.Exp),
                                     reads=[PB(b)], writes=[("PT", pi)])

                            def fPV2(s=s, j=j, nqb=nqb, qb0=qb0, pi=pi, h=h):
                                vk = [("PT", pi), ("VP", j // 4), ("VP", (j + 1) // 4), "VPones"]
                                mm(psum[:, 3 + 2, 0:65], PT[pi][:, 0:128], VP[:, j, s, :], False, True, vk, PB(3 + 2))
                                mm(psum[:, 3 + 3, 0:65], PT[pi][:, 128:256], VP[:, j, s, :], False, False, vk, PB(3 + 3))
                                mm(psum[:, 3 + 3, 0:65], PT[pi][:, 256:384], VP[:, j + 1, s, :], False, True, vk, PB(3 + 3))
                                for lq in range(nqb):
                                    col, key = rl_col()
                                    S.op("dve", lambda e, lq=lq, col=col: e.reciprocal(out=col, in_=psum[:, 3 + lq, 64:65]),
                                         reads=[PB(3 + lq)], writes=[key])
                                    S.op("dve", lambda e, lq=lq, col=col, qb=qb0 + lq: e.tensor_scalar(
                                        out=ymix[:, qb, h * 64:(h + 1) * 64], in0=psum[:, 3 + lq, 0:64], scalar1=col,
                                        scalar2=None, op0=ALU.mult),
                                         reads=[PB(3 + lq), key], writes=[("ymix", qb0 + lq)])

                            steps.append((fS2, fE2, fPV2))
                            continue

                        def fS(s=s, rows=rows, ci=ci, q0=q0, j=j, c0=c0, c1=c1, b=b, diag=diag):
                            mm(psum[:, b, c0:c1], KP[rows, s, j * 128:(j + 1) * 128], QP[rows, s, q0 + c0:q0 + c1],
                               True, not diag, [("KP", s, j // 4), ("KPaug", s), ("QP", s, ci), ("QPaug", s)], PB(b))
                            if diag:
                                mm(psum[:, b, c0:c0 + 128], ident[:], cbias[:], False, True, ["ident", "cbias"], PB(b))

                        def fE(b=b, c0=c0, c1=c1, pi=pi):
                            S.op("act", lambda e: e.activation(out=PT[pi][:, c0:c1], in_=psum[:, b, c0:c1], func=AF.Exp),
                                 reads=[PB(b)], writes=[("PT", pi)])

                        def fPV(s=s, j=j, m=m, nqb=nqb, qb0=qb0, pi=pi, h=h, last=(j == klast)):
                            for lq in range(m, nqb):
                                mm(psum[:, 3 + lq, 0:65], PT[pi][:, lq * 128:(lq + 1) * 128], VP[:, j, s, :],
                                   j == 0, j == 15 + qb0 + lq, [("PT", pi), ("VP", j // 4), "VPones"], PB(3 + lq))
                            if last:
                                for lq in range(nqb):
                                    col, key = rl_col()
                                    S.op("dve", lambda e, lq=lq, col=col: e.reciprocal(out=col, in_=psum[:, 3 + lq, 64:65]),
                                         reads=[PB(3 + lq)], writes=[key])
                                    S.op("dve", lambda e, lq=lq, col=col, qb=qb0 + lq: e.tensor_scalar(
                                        out=ymix[:, qb, h * 64:(h + 1) * 64], in0=psum[:, 3 + lq, 0:64], scalar1=col,
                                        scalar2=None, op0=ALU.mult),
                                         reads=[PB(3 + lq), key], writes=[("ymix", qb0 + lq)])

                        steps.append((fS, fE, fPV))
            LA = 2
            for idx in range(len(steps) + LA):
                if idx < len(steps):
                    steps[idx][0]()
                    steps[idx][1]()
                if idx - LA >= 0:
                    steps[idx - LA][2]()

        wmixA = A2[:, 29760:33856].rearrange("p (k n) -> p k n", k=4)
        wmixC = A3[:, 96:3168].rearrange("p (k n) -> p k n", k=3)
        wmixD = A3[:, 5472:6496]
        for pair in range(npairs):
            if pair == 2:
                S.alias([("wmix", 0), ("wmix", 1)], G_KEYS)
                for k2 in range(4):
                    load_w(w_mix, k2 * 128, 1, 0, 1024, wmixA[:, k2:k2 + 1, :], ("wmix", k2 // 2))
            if pair == 0:
                g_chain()
            pair_proj(pair)
            if pair + 1 < npairs:
                pair_weights(pair + 1)
            if pair == npairs - 1:
                S.alias([("wmix", 2), ("wmix", 3)], [("wpair", w_) for w_ in range(3)] + ["wf"])
                for k2 in range(4, 7):
                    load_w(w_mix, k2 * 128, 1, 0, 1024, wmixC[:, k2 - 4:k2 - 3, :], ("wmix", k2 // 2))
            pair_aug(pair)
            pair_attn(pair)
            if pair == npairs - 1:
                S.alias([("wmix", 3)], [("PT", i_) for i_ in range(3)] + ["vtmp"])
                load_w(w_mix, 7 * 128, 1, 0, 1024, wmixD.rearrange("p (k n) -> p k n", k=1), ("wmix", 3))

        S.barrier()
        NP_ = 2304
        P0 = 1792
        uT = A2[:, 0:4608].bitcast(F32)
        sA = A2[:, 4608:9216].bitcast(F32)
        sB = A2[:, 9216:13824].bitcast(F32)
        dT = A2[:, 13824:13824 + 2304]
        t16 = A2[:, 16128:16160].bitcast(F32)
        for cc in range(2):
            for (a, n) in [(0, 512), (512, 512), (1024, 512), (1536, 512), (2048, 256)]:
                b = sb_i[0] % 3; sb_i[0] += 1
                for kc in range(8):
                    mm(psum[:, b, 0:n], wu[:, kc, cc * 128:(cc + 1) * 128], hT[:, kc, P0 + a:P0 + a + n],
                       kc == 0, kc == 7, ["wu"], PB(b))
                S.op("act", lambda e, b=b, a=a, n=n: e.copy(out=uT[:, a:a + n], in_=psum[:, b, 0:n]),
                     reads=[PB(b)], writes=["uT"])
            N = NP_

            def tt(out_ap, a_ap, b_ap, rd, wr):
                S.op("dve", lambda e: e.tensor_tensor(out=out_ap, in0=a_ap, in1=b_ap, op=ALU.add), reads=rd, writes=wr)

            def msk_add(sh, lo_, cc=cc, N=NP_):
                S.op("dve", lambda e: e.scalar_tensor_tensor(out=sB[:, lo_:N], in0=sA[:, lo_ - sh:N - sh], scalar=pmk[:, 0:1],
                                                             in1=sA[:, lo_:N], op0=ALU.mult, op1=ALU.add),
                     reads=["sA", "pmk"], writes=["sB"])

            def dif_all(cc=cc, N=NP_):
                S.op("dve", lambda e: e.scalar_tensor_tensor(out=dT[:, 128:N], in0=sB[:, 128:N], scalar=pmk[:, 1 + cc:2 + cc],
                                                             in1=uT[:, 128:N], op0=ALU.mult, op1=ALU.subtract),
                     reads=["sB", "uT", "pmk"], writes=["dT"])
                S.op("dve", lambda e: e.tensor_tensor(out=t16[:, :], in0=sB[:, 256:272], in1=invt[:, cc, :],
                                                      op=ALU.mult), reads=["sB", "invt"], writes=["t16"])
                S.op("dve", lambda e: e.tensor_tensor(out=dT[:, 256:272], in0=t16[:, :], in1=uT[:, 256:272],
                                                      op=ALU.subtract), reads=["t16", "uT", "dT"], writes=["dT"])

            tt(sA[:, 1:N], uT[:, 1:N], uT[:, 0:N - 1], ["uT"], ["sA"])
            if cc == 0:
                msk_add(2, 3)
            else:
                tt(sB[:, 3:N], sA[:, 3:N], sA[:, 1:N - 2], ["sA"], ["sB"])
                tt(sA[:, 7:N], sB[:, 7:N], sB[:, 3:N - 4], ["sB"], ["sA"])
                msk_add(8, 15)
            dif_all()
            for (a, n) in [(0, 512), (512, 512), (1024, 512), (1536, 512), (2048, 128)]:
                b = sb_i[0] % 3; sb_i[0] += 1
                mm(psum[:, b, 0:n], wpbd[:, cc, :], dT[:, 128 + a:128 + a + n], True, True, ["wpbd", "dT"], PB(b))
                S.op("act", lambda e, b=b, a=a, n=n, cc=cc: e.activation(out=ypT[:, cc, a:a + n], in_=psum[:, b, 0:n],
                                                                         func=AF.Identity, scale=pscol[:, cc:cc + 1]),
                     reads=[PB(b), "pscol"], writes=["ypT"])

        S.barrier()
        wxq = A2[:, 0:8192].rearrange("p (k n) -> p k n", k=8)
        wxo = A2[:, 8208:16400].rearrange("p (k n) -> p k n", k=8)
        for k2 in range(8):
            pass
        load_gain(1, 1)
        for k2 in range(8):
            load_w(w_xq, k2 * 128, 1, 0, 1024, wxq[:, k2:k2 + 1, :], ("wxq", k2 // 2))
        wmk = [("wmix", k2) for k2 in range(4)]

        def post_norm_residual(ybanks, gi, xsrc, xsrc_reads, dst, dst_key):
            yv = psum[:, ybanks[0]:ybanks[0] + 2, :].rearrange("p a b -> p (a b)")
            yk = [PB(ybanks[0]), PB(ybanks[1])]
            col, key = rstd_of(yv, yk)
            S.op("dve", lambda e: e.scalar_tensor_tensor(out=tmpf, in0=yv, scalar=col, in1=gt[gi][:],
                                                         op0=ALU.mult, op1=ALU.mult),
                 reads=yk + [key, ("gt", gi)], writes=["tmpf"])
            S.op("dve", lambda e: e.tensor_tensor(out=dst, in0=tmpf, in1=xsrc, op=ALU.add),
                 reads=["tmpf"] + list(xsrc_reads), writes=[dst_key])

        nB = NQB if stage >= 1 else 0
        yT2 = A2[:, 33856:34624].rearrange("p (c t) -> p c t", c=6)
        yTb = [yT, yT2]

        def b_T(qb):
            for c in range(6):
                S.op("pe", lambda e, c=c: e.transpose(out=pst[:, c * 128:(c + 1) * 128],
                                                      in_=ymix[:, qb, c * 128:(c + 1) * 128], identity=ident[:]),
                     reads=[("ymix", qb), "ident"], writes=[PB(7)], accum=True)
            S.op("act", lambda e: e.copy(out=yTb[qb % 2], in_=pst[:, 0:768].rearrange("p (c t) -> p c t", c=6)),
                 reads=[PB(7)], writes=[("yT", qb % 2)])

        def b_mm(qb):
            yb = (0, 1) if qb % 2 == 0 else (2, 3)
            for n in range(2):
                for kc in range(8):
                    lhsT = ypT[:, kc, qb * 128:(qb + 1) * 128] if kc < 2 else yTb[qb % 2][:, kc - 2, :]
                    wsl = (wmixA[:, kc, n * 512:(n + 1) * 512] if kc < 4 else
                           wmixC[:, kc - 4, n * 512:(n + 1) * 512] if kc < 7 else wmixD[:, n * 512:(n + 1) * 512])
                    mm(psum[:, yb[n], :], lhsT, wsl, kc == 0, kc == 7,
                       ["ypT", ("yT", qb % 2), ("wmix", kc // 2)], PB(yb[n]))
            i = qb % 2
            S.dma(lambda e: e.dma_start(out=xl[i][:], in_=xk[QOFF + qb * 128:QOFF + (qb + 1) * 128, :]),
                  writes=[("xl", i)])
            post_norm_residual(yb, 1, xl[i][:], [("xl", i)], xr(qb), ("xres", qb))

        if nB:
            b_T(0)
        for qb in range(nB):
            if qb + 1 < nB:
                b_T(qb + 1)
            b_mm(qb)

        if stage == 1:
            import os
            if os.environ.get("BAR"):
                S.barrier()
            toks = []
            for qb in range(1, NQB):
                toks.append(S.dma(lambda e, qb=qb: e.dma_start(out=out[(qb - 1) * 128:qb * 128, :], in_=xr(qb)),
                                  reads=[("xres", qb)], writes=[("outd", qb)]))
            S.barrier()
            S.emit()
            return nc

        S.alias([("wxo", k) for k in range(4)], [("wmix", k) for k in range(4)])
        S.alias([("h2T", i) for i in range(4)] + [("q2T", i) for i in range(8)] + [("oc", i) for i in range(4)],
                [("ymix", qb) for qb in range(NQB)])
        S.alias([("PTx", i) for i in range(4)] + [("ocT", 0), ("ocT", 1)], ["wf", "wu", ("wmix", 2), ("wmix", 3)] + [("wpair", i) for i in range(3)])
        o = 0
        PTx = [A3[:, o + i * 512:o + (i + 1) * 512] for i in range(4)]; o += 2048
        ocT = A3[:, o:o + 1024].rearrange("p (c t) -> p c t", c=8); o += 1024
        ocT2 = A3[:, o:o + 1024].rearrange("p (c t) -> p c t", c=8); o += 1024
        for k2 in range(8):
            load_w(w_xo, k2 * 128, 1, 0, 1024, wxo[:, k2:k2 + 1, :], ("wxo", k2 // 2))
        load_gain(0, 3)
        load_gain(1, 4)
        h2T = A2[:, 16400:20496].rearrange("p (k t) -> p k t", k=8)
        q2T = A2[:, 20496:24592].rearrange("p (k t) -> p k t", k=8)
        oc = A2[:, 24592:28688].rearrange("p (b d) -> p b d", b=4)
        wq_k = [("wxq", k2) for k2 in range(4)]
        wo_k = [("wxo", k2) for k2 in range(4)]
        px_i = [0]
        ob_i = [0]
        out_tokens = []
        chunks = [(0, 1), (1, 4), (5, 4), (9, 4), (13, 4)]
        ocTb = [ocT, ocT2]
        oct_i = [0]
        yb_i = [0]

        def c_norms(ci):
            qb0, nqb = chunks[ci]
            norm_seq([(None, xr(qb0 + lb), [("xres", qb0 + lb)], 0,
                       (lambda lb=lb: h2T[:, :, lb * 128:(lb + 1) * 128]), ("h2T", lb), None) for lb in range(nqb)])

        def c_qproj(ci, fc):
            qb0, nqb = chunks[ci]
            N = nqb * 128
            h2k = [("h2T", lb) for lb in range(nqb)]
            b = sb_i[0] % 3; sb_i[0] += 1
            for kc in range(8):
                mm(psum[:, b, 0:N], wxq[:, kc, fc * 128:(fc + 1) * 128], h2T[:, kc, 0:N], kc == 0, kc == 7,
                   h2k + [("wxq", kc // 2)], PB(b))
            S.op("act", lambda e: e.activation(out=q2T[:, fc, 0:N], in_=psum[:, b, 0:N], func=AF.Copy, scale=0.0625),
                 reads=[PB(b)], writes=[("q2T", fc)])

        def c_attn(ci):
            qb0, nqb = chunks[ci]
            N = nqb * 128

            def s_stage(hh):
                pis = []
                for mb in range(2):
                    b = sb_i[0] % 3; sb_i[0] += 1
                    for d in range(2):
                        mm(psum[:, b, 0:N], kTm[:, 2 * hh + d, mb * 128:(mb + 1) * 128], q2T[:, 2 * hh + d, 0:N],
                           d == 0, d == 1, ["kTm", ("q2T", 2 * hh + d)], PB(b))
                    pi = px_i[0] % 4; px_i[0] += 1
                    pis.append(pi)
                    S.op("act", lambda e, b=b, pi=pi: e.activation(out=PTx[pi][:, 0:N], in_=psum[:, b, 0:N], func=AF.Exp),
                         reads=[PB(b)], writes=[("PTx", pi)])
                return pis

            def pv_stage(hh, pis):
                for lb in range(nqb):
                    ob = 3 + (ob_i[0] % 4); ob_i[0] += 1
                    for mb in range(2):
                        mm(psum[:, ob, 0:257], PTx[pis[mb]][:, lb * 128:(lb + 1) * 128], Vm[:, mb, hh, :],
                           mb == 0, mb == 1, [("PTx", pis[mb]), "Vm", "Vmones"], PB(ob))
                    col, key = rl_col()
                    S.op("dve", lambda e, ob=ob, col=col: e.reciprocal(out=col, in_=psum[:, ob, 256:257]),
                         reads=[PB(ob)], writes=[key])
                    S.op("dve", lambda e, ob=ob, col=col, lb=lb: e.tensor_scalar(
                        out=oc[:, lb, hh * 256:(hh + 1) * 256], in0=psum[:, ob, 0:256], scalar1=col, scalar2=None,
                        op0=ALU.mult), reads=[PB(ob), key], writes=[("oc", lb)])

            prev = None
            for hh in range(4):
                pis = s_stage(hh)
                if prev is not None:
                    pv_stage(*prev)
                prev = (hh, pis)
            pv_stage(*prev)

        def c_out_T(ci, lb):
            k = oct_i[0] % 2; oct_i[0] += 1
            for c in range(8):
                S.op("pe", lambda e, c=c: e.transpose(out=pst[:, c * 128:(c + 1) * 128],
                                                      in_=oc[:, lb, c * 128:(c + 1) * 128], identity=ident[:]),
                     reads=[("oc", lb), "ident"], writes=[PB(7)], accum=True)
            S.op("act", lambda e: e.copy(out=ocTb[k], in_=pst.rearrange("p (c t) -> p c t", c=8)),
                 reads=[PB(7)], writes=[("ocT", k)])
            return k

        def c_out_mm(ci, lb, k):
            qb0, nqb = chunks[ci]
            qb = qb0 + lb
            yb = (5, 6) if yb_i[0] % 2 == 0 else (3, 4)
            yb_i[0] += 1
            for n in range(2):
                for kc in range(8):
                    mm(psum[:, yb[n], :], ocTb[k][:, kc, :], wxo[:, kc, n * 512:(n + 1) * 512], kc == 0, kc == 7,
                       [("ocT", k), ("wxo", kc // 2)], PB(yb[n]))
            post_norm_residual(yb, 1, xr(qb), [("xres", qb)], xr(qb), ("xres", qb))
            if qb >= 1:
                out_tokens.append(S.dma(lambda e: e.dma_start(out=out[(qb - 1) * 128:qb * 128, :], in_=xr(qb)),
                                        reads=[("xres", qb)], writes=[("outd", qb - 1)]))

        c_norms(0)
        for fc in range(8):
            c_qproj(0, fc)
        c_attn(0)
        h3T = A2[:, 0:8208].rearrange("p (k t) -> p k t", k=8)
        load_gain(2, 5)
        h3_items = [(None, xhalo[:], [("xres", 0)], 2, (lambda: h3T[:, :, 0:2]), ("h3T", -1), (126, 128))]
        for lb in range(8):
            h3_items.append((None, xr(1 + lb), [("xres", 1 + lb)], 2,
                             (lambda lb=lb: h3T[:, :, 2 + lb * 128:2 + (lb + 1) * 128]), ("h3T", lb), None))
        for ci in range(len(chunks)):
            nqb = chunks[ci][1]
            nxt = ci + 1 < len(chunks)
            if nxt:
                c_norms(ci + 1)
            else:
                S.alias([("h3T", lb) for lb in range(-1, 8)], [("wxq", k_) for k_ in range(4)])
            fcs = list(range(8))
            per = (8 + nqb - 1) // nqb
            for lb in range(nqb):
                k = c_out_T(ci, lb)
                if nxt:
                    for fc in fcs[lb * per:(lb + 1) * per]:
                        c_qproj(ci + 1, fc)
                elif stage >= 4:
                    norm_seq(h3_items[lb * 3:(lb + 1) * 3])
                c_out_mm(ci, lb, k)
            if nxt:
                c_attn(ci + 1)
        if stage >= 4:
            S.op("pool", lambda e: e.tensor_copy(out=h3h[:, :, :], in_=h3T[:, :, 1024:1026]),
                 reads=[("h3T", 7)], writes=["h3h"])
        if stage == 2:
            S.barrier()
            S.emit()
            return nc

        S.barrier()
        o = 0
        o += 8208
        wupb = [A2[:, o + i * 2048:o + (i + 1) * 2048].rearrange("p (g k n) -> p g k n", g=2, k=8) for i in range(2)]
        o += 4096
        wdnb = [A2[:, o + i * 512:o + (i + 1) * 512] for i in range(8)]
        o += 4096
        ylo = A3[:, 0:8192].bitcast(F32).rearrange("p (b n) -> p b n", b=8)
        yhb = [A3[:, 8192 + i * 1024:8192 + (i + 1) * 1024].bitcast(F32) for i in range(4)]
        yh_i = [0]
        hgb = [A2[:, o + i * 2052:o + (i + 1) * 2052].bitcast(F32) for i in range(2)]; o += 4104
        hub = [A2[:, o + i * 2052:o + (i + 1) * 2052].bitcast(F32) for i in range(2)]; o += 4104
        cvg = [A2[:, o + i * 1024:o + (i + 1) * 1024].bitcast(F32) for i in range(2)]; o += 2048
        cvu = [A2[:, o + i * 1024:o + (i + 1) * 1024].bitcast(F32) for i in range(2)]; o += 2048
        glb = [A2[:, o + i * 1024:o + (i + 1) * 1024].bitcast(F32) for i in range(2)]; o += 2048
        cwr = A2[0:64, o:o + 1024].bitcast(F32).rearrange("p (k n) -> p k n", k=4); o += 1024
        assert o <= 34816
        load_gain(0, 6)
        xpre = [(A3[:, i * 2048:(i + 1) * 2048].bitcast(F32), ("xpre", i)) for i in range(6)] + \
               [(xl[0][:], ("xl", 0)), (xl[1][:], ("xl", 1))]
        S.dma(lambda e: e.dma_start(out=cwr[:, 0:3, :], in_=conv_w.rearrange("k (c p) -> c k p", p=128)), writes=["cwr"])
        S.dma(lambda e: e.dma_start(out=cwr[:, 3, :], in_=conv_b.rearrange("k (c p) -> (k c) p", p=128)),
              writes=["cwr"])
        for k in range(4):
            S.op("pe", lambda e, k=k: e.transpose(out=psum[:, 0, k * 64:(k + 1) * 64], in_=cwr[:, k, :],
                                                  identity=identf[0:64, 0:64]),
                 reads=["cwr", "identf"], writes=[PB(0)], accum=True)
        S.op("act", lambda e: e.copy(out=cw[:], in_=psum[:, 0, 0:256].rearrange("p (k n) -> p k n", k=4)),
             reads=[PB(0)], writes=["cw"])

        it = [0]
        wu_i = [0]
        wd_i = [0]
        def ffn_h3T(th):
            if th == 1:
                S.op("pool", lambda e: e.tensor_copy(out=h3T[:, :, 0:2], in_=h3h[:, :, :]),
                     reads=["h3h"], writes=[("h3T", -1)])
            items = []
            for lb in range(-1 if th == 0 else 0, 8):
                qb = 1 + 8 * th + lb
                i = (lb + 1) % 2
                if qb == 0:
                    pre, src, srck = None, xhalo[:], [("xres", 0)]
                elif th == 1:
                    pre = None
                    src, srck = xpre[lb][0], [xpre[lb][1]]
                else:
                    pre = (lambda qb=qb, i=i: S.dma(lambda e: e.dma_start(out=xl[i][:], in_=out[(qb - 1) * 128:qb * 128, :]),
                                                    reads=[("outd", qb - 1)], writes=[("xl", i)]))
                    src, srck = xl[i][:], [("xl", i)]
                if lb == -1:
                    items.append((pre, src, srck, 2, (lambda: h3T[:, :, 0:2]), ("h3T", -1), (126, 128)))
                else:
                    items.append((pre, src, srck, 2, (lambda lb=lb: h3T[:, :, 2 + lb * 128:2 + (lb + 1) * 128]),
                                  ("h3T", lb), None))
            norm_seq(items)
            if th == 1:
                S.alias([("ylo", i) for i in range(8)] + [("yhb", i) for i in range(4)], [("xpre", i) for i in range(6)])
            if th == 0:
                S.op("pool", lambda e: e.tensor_copy(out=h3h[:, :, :], in_=h3T[:, :, 1024:1026]),
                     reads=[("h3T", 7)], writes=["h3h"])
        def load_up(fc):
            wb = fc % 2
            load_w(w_up, 0, 8, fc * 128, 128, wupb[wb][:, 0, :, :], ("wup", wb, 0))
            load_w(w_up, 0, 8, 4096 + fc * 128, 128, wupb[wb][:, 1, :, :], ("wup", wb, 1))

        def ffn_up(th, preloaded=0, deferred=(), xprefetch=False):
            pending = []
            if preloaded < 1:
                load_up(0)
            for fc in range(32):
                wb = fc % 2
                if fc == 3:
                    for d_ in deferred:
                        d_()
                if xprefetch and 4 <= fc < 12:
                    lb_ = fc - 4
                    buf_, bkey_ = xpre[lb_]
                    S.dma(lambda e, lb_=lb_, buf_=buf_: e.dma_start(out=buf_, in_=out[(8 + lb_) * 128:(9 + lb_) * 128, :]),
                          reads=[("outd", 8 + lb_), ("actT", fc - 2, 1)], writes=[bkey_])
                if fc + 1 < 32 and fc + 1 >= preloaded:
                    load_up(fc + 1)
                hbufs = (hgb[wb], hub[wb])
                for g in range(2):
                    for kc in range(8):
                        mm(psum[:, 4, 2 * g:2 * g + 2], wupb[wb][:, g, kc, :], h3T[:, kc, 0:2],
                           kc == 0, kc == 7, [("wup", wb, g), ("h3T", -1)], PB(4))
                fcol = flagt[:, 0:1] if th == 0 else flagt[:, 1:2]
                for g in range(2):
                    S.op("dve", lambda e, g=g, hbuf=hbufs[g], fcol=fcol: e.tensor_scalar(
                        out=hbuf[:, 0:2], in0=psum[:, 4, 2 * g:2 * g + 2], scalar1=fcol, scalar2=None,
                        op0=ALU.mult), reads=[PB(4), "flagt"], writes=[("hbuf", wb, g, "h")])
                for tc in range(2):
                    ii = it[0] % 2; it[0] += 1
                    bg, bu = 2 * ii, 2 * ii + 1
                    hk = [("h3T", lb) for lb in range(4 * tc, 4 * tc + 4)]
                    for g, bb in ((0, bg), (1, bu)):
                        for kc in range(8):
                            mm(psum[:, bb, :], wupb[wb][:, g, kc, :], h3T[:, kc, 2 + tc * 512:2 + (tc + 1) * 512],
                               kc == 0, kc == 7, [("wup", wb, g)] + hk, PB(bb))
                    cvs = (cvg[ii], cvu[ii])
                    c0 = tc * 512
                    for g, bb in ((0, bg), (1, bu)):
                        fcg = fc + 32 * g
                        S.op("act", lambda e, bb=bb, hbuf=hbufs[g], c0=c0: e.copy(out=hbuf[:, 2 + c0:514 + c0],
                                                                                 in_=psum[:, bb, :]),
                             reads=[PB(bb)], writes=[("hbuf", wb, g, tc)])
                        S.op("act", lambda e, bb=bb, cv=cvs[g], fcg=fcg: e.activation(
                            out=cv, in_=psum[:, bb, :], func=AF.Identity, scale=cw[:, 2, fcg:fcg + 1],
                            bias=cw[:, 3, fcg:fcg + 1]), reads=[PB(bb), "cw"], writes=[("cv", ii, g)])
                    prevk = "h" if tc == 0 else 0
                    for tap, sh in ((1, 1), (0, 0)):
                        for g in range(2):
                            fcg = fc + 32 * g
                            S.op("dve", lambda e, cv=cvs[g], hbuf=hbufs[g], fcg=fcg, tap=tap, sh=sh, c0=c0:
                                 e.scalar_tensor_tensor(out=cv, in0=hbuf[:, c0 + sh:c0 + sh + 512],
                                                        scalar=cw[:, tap, fcg:fcg + 1], in1=cv, op0=ALU.mult, op1=ALU.add),
                                 reads=[("hbuf", wb, g, tc), ("hbuf", wb, g, prevk), ("cv", ii, g), "cw"],
                                 writes=[("cv", ii, g)])

                    def tail(ii=ii, fc=fc, tc=tc):
                        S.op("act", lambda e: e.activation(out=glb[ii], in_=cvg[ii], func=AF.Gelu_apprx_tanh),
                             reads=[("cv", ii, 0)], writes=[("gl", ii)])
                        S.op("dve", lambda e: e.tensor_tensor(out=actT[:, fc, tc * 512:(tc + 1) * 512], in0=glb[ii],
                                                              in1=cvu[ii], op=ALU.mult),
                             reads=[("gl", ii), ("cv", ii, 1)], writes=[("actT", fc, tc)])

                    for t_ in pending:
                        t_()
                    pending = [tail]
            for t_ in pending:
                t_()
            pending = []
        obufs = [(xl[0][:], ("xl", 0)), (xl[1][:], ("xl", 1)), (tmpf, "tmpf"), (gt[1][:], ("gt", 1)), (gt[2][:], ("gt", 2))]
        ob_rot = [0]
        fin_th = [0]

        def fin(lb, cols, ybs):
            col, key = cols[lb]
            yb_ = ybs[lb]
            finish_rstd(col, key)
            ob = 8 * fin_th[0] + lb
            buf, bkey = obufs[ob_rot[0] % len(obufs)]; ob_rot[0] += 1
            S.op("dve", lambda e: e.scalar_tensor_tensor(
                out=buf[:, 0:512], in0=ylo[:, lb, :], scalar=col, in1=gt[0][:, 0:512],
                op0=ALU.mult, op1=ALU.mult), reads=[("ylo", lb), key, ("gt", 0)], writes=[bkey])
            S.op("dve", lambda e: e.scalar_tensor_tensor(
                out=buf[:, 512:1024], in0=yhb[yb_], scalar=col, in1=gt[0][:, 512:1024],
                op0=ALU.mult, op1=ALU.mult), reads=[("yhb", yb_), key, ("gt", 0), bkey], writes=[bkey])
            S.dma(lambda e: e.dma_start(out=out[ob * 128:(ob + 1) * 128, :], in_=buf, accum_op=ALU.add),
                  reads=[bkey], writes=[("outd", ob)], queue="pool")

        def ffn_down(th):
            fin_th[0] = th
            ssA = {}
            cols = {}
            ybs = {}

            def mm_pass(half, lbs):
                for kc in range(32):
                    db = wd_i[0] % 8; wd_i[0] += 1
                    S.dma(lambda e, kc=kc, db=db: e.dma_start(
                        out=wdnb[db], in_=w_down[kc * 128:(kc + 1) * 128, half * 512:(half + 1) * 512]),
                          writes=[("wdn", db)], queue="pool")
                    for lb in lbs:
                        mm(psum[:, lb, :], actT[:, kc, lb * 128:(lb + 1) * 128], wdnb[db], kc == 0, kc == 31,
                           [("actT", kc, lb // 4), ("wdn", db)], PB(lb))

            def epi(half, lbs):
                for lb in lbs:
                    col, key = rs_col()
                    cols[lb] = (col, key)
                    S.op("act", lambda e, lb=lb, col=col: e.activation(out=junk[:, 0:512], in_=psum[:, lb, :],
                                                                       func=AF.Square, accum_out=col),
                         reads=[PB(lb)], writes=[key])
                    if half == 0:
                        ssA[lb] = (col, key)
                        S.op("act", lambda e, lb=lb: e.copy(out=ylo[:, lb, :], in_=psum[:, lb, :]),
                             reads=[PB(lb)], writes=[("ylo", lb)])
                    else:
                        colA, keyA = ssA[lb]
                        yb_ = yh_i[0] % 4; yh_i[0] += 1
                        ybs[lb] = yb_
                        S.op("act", lambda e, lb=lb, yb_=yb_: e.copy(out=yhb[yb_], in_=psum[:, lb, :]),
                             reads=[PB(lb)], writes=[("yhb", yb_)])
                        S.op("dve", lambda e, col=col, colA=colA: e.tensor_tensor(out=col, in0=col, in1=colA, op=ALU.add),
                             reads=[key, keyA], writes=[key])
                if half == 1:
                    for lb in lbs:
                        fin(lb, cols, ybs)

            mm_pass(0, range(8))
            epi(0, range(8))
            mm_pass(1, range(0, 4))
            mm_pass(1, range(4, 8))
            epi(1, range(0, 4))
            epi(1, range(4, 8))
        def xpre_loads():
            pass

        load_up(0)
        ffn_up(0, preloaded=1, xprefetch=(stage >= 4))
        load_up(0)
        load_up(1)
        ffn_h3T(1)
        ffn_down(0)
        ffn_up(1, preloaded=2)
        ffn_down(1)
        S.barrier()
        S.emit()
    return nc


_CACHE = {}


def _prep_inputs(inp):
    f = lambda a: np.ascontiguousarray(np.asarray(a, dtype=np.float32))
    x = f(inp["x"]); mem = f(inp["mem"])
    gains = np.stack([f(inp[k])[0] for k in ("norm_mix_pre", "norm_mix_post", "norm_mem", "norm_xa_pre",
                                             "norm_xa_post", "norm_ffn_pre", "norm_ffn_post")], 0)
    shared = {
        "gains": gains, "w_in": f(inp["w_in"])[0], "b_forget": f(inp["b_forget"])[0].reshape(12, 1),
        "w_pool": f(inp["w_pool"])[0], "pool_scale": f(inp["pool_scale"])[0].reshape(256, 1),
        "w_mix_out": f(inp["w_mix_out"])[0], "w_xq": f(inp["w_xq"])[0], "w_xkv": f(inp["w_xkv"])[0],
        "w_xo": f(inp["w_xo"])[0], "w_up": f(inp["w_up"])[0], "conv_w": f(inp["conv_w"])[0].reshape(3, 8192),
        "conv_b": f(inp["conv_b"])[0].reshape(1, 8192), "w_down": f(inp["w_down"])[0],
    }
    bf = ml_dtypes.bfloat16
    in_maps = []
    wins = np.array([2, 4, 8, 16], np.float32)
    for c in range(8):
        b, half = c // 2, c % 2
        xk = np.zeros((NT, D), np.float32)
        if half == 1:
            xk[:2048] = x[b, :2048]
        xk[2048:] = x[b, half * 2048:(half + 1) * 2048]
        kconst = np.zeros((4, NT), np.float32)
        kconst[0:3] = -1.0
        if half == 0:
            kconst[3, :2048] = MASKV
        qconst = np.ones((4, NQ), np.float32)
        qconst[3, :128] = 0.0
        invc = np.zeros((2, 128, 16), np.float32)
        t1 = np.arange(1, 17, dtype=np.float32)
        for g in range(4):
            cc, hf = g // 2, g % 2
            if half == 0:
                invc[cc, hf * 64:(hf + 1) * 64, :] = 1.0 / np.minimum(t1, wins[g])[None, :]
            else:
                invc[cc, hf * 64:(hf + 1) * 64, :] = 1.0 / wins[g]
        flags = np.ones((128, 2), np.float32)
        flags[:, 0] = float(half)
        m = dict(shared)
        m.update({"xk": xk, "mem": mem[b], "kconst": kconst.astype(bf), "qconst": qconst.astype(bf),
                  "invc": invc, "flags": flags})
        in_maps.append(m)
    return in_maps


def kernel(**inputs):
    if "nc" not in _CACHE:
        _CACHE["nc"] = build()
    nc = _CACHE["nc"]
    in_maps = _prep_inputs(inputs)
    res = run_bass_kernel_spmd(nc, in_maps, core_ids=list(range(8)))
    outp = np.zeros((4, 4096, D), np.float32)
    for c in range(8):
        b, half = c // 2, c % 2
        outp[b, half * 2048:(half + 1) * 2048] = res.results[c]["out"]
    return outp
```

```python
import contextlib
import numpy as np
import ml_dtypes
import concourse.bass as bass
import concourse.mybir as mybir
from concourse.bass_utils import run_bass_kernel_spmd

F32 = mybir.dt.float32
BF16 = mybir.dt.bfloat16
AF = mybir.ActivationFunctionType
ALU = mybir.AluOpType

COMPUTE = ("pe", "act", "dve", "pool")
D = 1024
DIN = 2572
NT = 4096
NQ = 2176
QOFF = 1920
NQB = 17
MASKV = -30000.0


class Sched:
    def __init__(self, nc, es, n_dma_sems=48, same_engine_sync=True):
        self.nc = nc
        self.sem = {e: es.enter_context(nc.semaphore("s_" + e)) for e in COMPUTE}
        self.cnt = {e: 0 for e in COMPUTE}
        self.dsems = [es.enter_context(nc.semaphore("d%d" % i)) for i in range(n_dma_sems)]
        self.dcnt = [0] * n_dma_sems
        self.dnext = 0
        self.streams = {e: [] for e in COMPUTE + ("sp",)}
        self.waited = {e: {} for e in COMPUTE + ("sp",)}
        self.lastw = {}
        self.readers = {}
        self.same = same_engine_sync

    def _need(self, eng, token, waits, allow_same=True):
        if token is None:
            return
        key, val = token
        if key == eng and not (self.same and allow_same):
            return
        if self.waited[eng].get(key, 0) >= val:
            return
        self.waited[eng][key] = val
        waits.append(token)

    def _deps(self, eng, reads, writes, waits, accum=False):
        for r in reads:
            self._need(eng, self.lastw.get(r), waits)
        for w in writes:
            self._need(eng, self.lastw.get(w), waits, allow_same=not accum)
            for k, v in self.readers.get(w, {}).items():
                if k != eng:
                    self._need(eng, (k, v), waits)

    def _commit(self, token, reads, writes):
        k, v = token
        for r in reads:
            d = self.readers.setdefault(r, {})
            if d.get(k, 0) < v:
                d[k] = v
        for w in writes:
            self.lastw[w] = token
            self.readers[w] = {}

    def op(self, eng, fn, reads=(), writes=(), accum=False):
        waits = []
        self._deps(eng, reads, writes, waits, accum)
        self.cnt[eng] += 1
        token = (eng, self.cnt[eng])
        self.streams[eng].append((fn, waits, None))
        self._commit(token, reads, writes)
        return token

    def dma(self, fn, reads=(), writes=(), queue="sp"):
        waits = []
        self._deps(queue, reads, writes, waits)
        i = self.dnext
        self.dnext = (self.dnext + 1) % len(self.dsems)
        key = ("d", i)
        if self.dcnt[i] > 0:
            self._need(queue, (key, 16 * self.dcnt[i]), waits)
        self.dcnt[i] += 1
        token = (key, 16 * self.dcnt[i])
        self.streams[queue].append((fn, waits, i))
        self._commit(token, reads, writes)
        return token

    def alias(self, new_keys, old_keys):
        toks = {}
        for ok in old_keys:
            lw = self.lastw.get(ok)
            if lw is not None:
                toks[lw[0]] = max(toks.get(lw[0], 0), lw[1])
            for k, v in self.readers.get(ok, {}).items():
                toks[k] = max(toks.get(k, 0), v)
        for nk in new_keys:
            d = self.readers.setdefault(nk, {})
            for k, v in toks.items():
                d[k] = max(d.get(k, 0), v)

    def barrier(self):
        toks = [(e, self.cnt[e]) for e in COMPUTE if self.cnt[e] > 0]
        toks += [(("d", i), 16 * c) for i, c in enumerate(self.dcnt) if c > 0]
        for eng in COMPUTE + ("sp",):
            waits = []
            for t in toks:
                self._need(eng, t, waits)
            if waits:
                self.streams[eng].append((None, waits, None))

    def semh(self, key):
        if isinstance(key, tuple):
            return self.dsems[key[1]]
        return self.sem[key]

    def replay(self, eng, e):
        for fn, waits, di in self.streams[eng]:
            for key, val in waits:
                e.wait_ge(self.semh(key), val)
            if fn is None:
                continue
            inst = fn(e)
            if di is not None:
                inst.then_inc(self.dsems[di], 16)
            else:
                inst.then_inc(self.sem[eng], 1)

    def emit(self):
        with self.nc.Block() as block:
            @block.tensor
            def _(e):
                self.replay("pe", e)

            @block.scalar
            def _(e):
                self.replay("act", e)

            @block.vector
            def _(e):
                self.replay("dve", e)

            @block.gpsimd
            def _(e):
                self.replay("pool", e)

            @block.sync
            def _(e):
                self.replay("sp", e)


def build(stage=4):
    nc = bass.Bass("TRN2", target_bir_lowering=False)

    def dt(name, shape, dty=F32, kind="ExternalInput"):
        return nc.dram_tensor(name, shape, dty, kind=kind).ap()

    xk = dt("xk", [NT, D])
    memd = dt("mem", [256, D])
    gains = dt("gains", [7, D])
    w_in = dt("w_in", [D, DIN])
    b_f = dt("b_forget", [12, 1])
    w_pool = dt("w_pool", [4, 64, 64])
    pscale = dt("pool_scale", [256, 1])
    w_mix = dt("w_mix_out", [D, D])
    w_xq = dt("w_xq", [D, D])
    w_xkv = dt("w_xkv", [D, 2 * D])
    w_xo = dt("w_xo", [D, D])
    w_up = dt("w_up", [D, 8192])
    conv_w = dt("conv_w", [3, 8192])
    conv_b = dt("conv_b", [1, 8192])
    w_down = dt("w_down", [4096, D])
    kconst = dt("kconst", [4, NT], BF16)
    qconst = dt("qconst", [4, NQ], BF16)
    invc = dt("invc", [2, 128, 16])
    flags = dt("flags", [128, 2])
    out = dt("out", [2048, D], kind="ExternalOutput")
    gsc = dt("gsc", [12, 3, NT], BF16, kind="Internal")

    with contextlib.ExitStack() as es:
        S = Sched(nc, es)

        def sb(name, shape, dty):
            return es.enter_context(nc.sbuf_tensor(name, shape, dty))

        A1 = sb("A1", [128, 32768], BF16)
        A2 = sb("A2", [128, 34816], BF16)
        A3 = sb("A3", [128, 15360], BF16)
        xl = [sb("xl%d" % i, [128, D], F32) for i in range(2)]
        hb = [sb("hb%d" % i, [128, D], BF16) for i in range(3)]
        junk = sb("junk", [128, D], BF16)
        gt = [sb("gt%d" % i, [128, D], F32) for i in range(3)]
        xhalo = sb("xhalo", [128, D], F32)
        ident = sb("ident", [128, 128], BF16)
        identf = sb("identf", [128, 128], F32)
        cbias = sb("cbias", [128, 128], BF16)
        cbf = sb("cbf", [128, 128], F32)
        rs = sb("rs", [128, 64], F32)
        rl = sb("rl", [128, 16], F32)
        flagt = sb("flagt", [128, 2], F32)
        pscol = sb("pscol", [128, 2], F32)
        negb = sb("negb", [12, 1], F32)
        epst = sb("epst", [128, 1], F32)
        pmk = sb("pmk", [128, 4], F32)
        cw = sb("cw", [128, 4, 64], F32)
        memkv = sb("memkv", [128, 4104], BF16)
        invt = sb("invt", [128, 2, 16], F32)
        h3h = sb("h3h", [128, 8, 2], BF16)
        psum = es.enter_context(nc.psum_tensor("psum", [128, 8, 512], F32))
        pst = psum[:, 7, :].bitcast(BF16)
        PB = lambda b: ("ps", b)

        hT = A1[:].rearrange("p (c t) -> p c t", c=8)
        xres = A1[:].bitcast(F32).rearrange("p (b d) -> p b d", b=16)
        actT = A1[:].rearrange("p (c t) -> p c t", c=32)

        KP = A2[:, 0:8192].rearrange("p (s t) -> p s t", s=2)
        QP = A2[:, 8192:12544].rearrange("p (s t) -> p s t", s=2)
        VP = A2[:, 12544:16704].rearrange("p (b s d) -> p b s d", b=32, s=2)
        ymix = A2[:, 16704:16704 + 17 * 768].rearrange("p (b d) -> p b d", b=17)

        def xr(qb):
            return xhalo[:] if qb == 0 else xres[:, qb - 1, :]

        rs_i = [0]

        def rs_col():
            rs_i[0] = (rs_i[0] + 1) % 64
            return rs[:, rs_i[0]:rs_i[0] + 1], ("rs", rs_i[0])

        rl_i = [0]

        def rl_col():
            rl_i[0] = (rl_i[0] + 1) % 16
            return rl[:, rl_i[0]:rl_i[0] + 1], ("rl", rl_i[0])

        def finish_rstd(col, key):
            S.op("act", lambda e: e.activation(out=col, in_=col, func=AF.Sqrt, scale=1.0 / D, bias=epst[:, 0:1]),
                 reads=[key, "epst"], writes=[key])
            S.op("dve", lambda e: e.reciprocal(out=col, in_=col), reads=[key], writes=[key])

        def rstd_of(src, src_reads):
            col, key = rs_col()
            S.op("act", lambda e: e.activation(out=junk[:], in_=src, func=AF.Square, accum_out=col),
                 reads=src_reads, writes=[key])
            finish_rstd(col, key)
            return col, key

        def load_gain(i, gi):
            S.dma(lambda e: e.dma_start(out=gt[i][:], in_=gains[gi:gi + 1, :].partition_broadcast(128)),
                  writes=[("gt", i)])

        hb_i = [0]
        sb_i = [0]

        ev_i = [0]

        def norm_p1(src, src_reads, gi):
            col, key = rstd_of(src, src_reads)
            hb_i[0] = (hb_i[0] + 1) % 3
            i = hb_i[0]
            S.op("dve", lambda e: e.scalar_tensor_tensor(out=hb[i][:], in0=src, scalar=col, in1=gt[gi][:],
                                                         op0=ALU.mult, op1=ALU.mult),
                 reads=list(src_reads) + [key, ("gt", gi)], writes=[("hb", i)])
            return i

        def norm_p2(i, dst_fn, dst_key, sel=None):
            for c in range(8):
                S.op("pe", lambda e, c=c: e.transpose(out=pst[:, c * 128:(c + 1) * 128],
                                                      in_=hb[i][:, c * 128:(c + 1) * 128], identity=ident[:]),
                     reads=[("hb", i), "ident"], writes=[PB(7)], accum=True)
            src_v = pst.rearrange("p (c t) -> p c t", c=8)
            if sel is not None:
                src_v = src_v[:, :, sel[0]:sel[1]]
            ev_i[0] ^= 1
            if ev_i[0]:
                S.op("act", lambda e: e.copy(out=dst_fn(), in_=src_v), reads=[PB(7)], writes=[dst_key])
            else:
                S.op("dve", lambda e: e.tensor_copy(out=dst_fn(), in_=src_v), reads=[PB(7)], writes=[dst_key])

        def norm_seq(items):
            pend = []
            for (pre, src, src_reads, gi, dst_fn, dst_key, sel) in items:
                if pre is not None:
                    pre()
                i = norm_p1(src, src_reads, gi)
                pend.append((i, dst_fn, dst_key, sel))
                if len(pend) > 2:
                    norm_p2(*pend.pop(0))
            for p_ in pend:
                norm_p2(*p_)

        def load_w(dram, r0, nk, c0, ncols, dst, dst_key, eng=None, after=()):
            S.dma(lambda e: e.dma_start(out=dst, in_=dram[r0:r0 + nk * 128, c0:c0 + ncols]
                                        .rearrange("(k p) n -> p k n", p=128)), reads=list(after), writes=[dst_key],
                  queue="pool")

        def mm(out_ap, lhsT, rhs, start, stop, reads, wkey):
            S.op("pe", lambda e: e.matmul(out=out_ap, lhsT=lhsT, rhs=rhs, start=start, stop=stop),
                 reads=reads, writes=[wkey], accum=True)

        S.op("pool", lambda e: e.memset(epst[:], 1e-6), writes=["epst"])
        for (pr_, vals_) in ((slice(0, 64), (0.0, 0.5, 0.125)), (slice(64, 128), (1.0, 0.25, 0.0625))):
            for c_, v_ in enumerate(vals_):
                S.op("dve", lambda e, pr_=pr_, c_=c_, v_=v_: e.memset(pmk[pr_, c_:c_ + 1], v_), writes=["pmk"])
        S.op("pool", lambda e: e.memset(identf[:], 0.0), writes=["identf"])
        S.op("pool", lambda e: e.affine_select(out=identf[:], in_=identf[:], pattern=[[-1, 128]],
                                               compare_op=ALU.not_equal, fill=1.0, base=0, channel_multiplier=1),
             reads=["identf"], writes=["identf"])
        S.op("pool", lambda e: e.tensor_copy(out=ident[:], in_=identf[:]), reads=["identf"], writes=["ident"])
        S.op("pool", lambda e: e.memset(cbf[:], 0.0), writes=["cbf"])
        S.op("pool", lambda e: e.affine_select(out=cbf[:], in_=cbf[:], pattern=[[1, 128]],
                                               compare_op=ALU.is_ge, fill=MASKV, base=0, channel_multiplier=-1),
             reads=["cbf"], writes=["cbf"])
        S.op("pool", lambda e: e.tensor_copy(out=cbias[:], in_=cbf[:]), reads=["cbf"], writes=["cbias"])
        S.dma(lambda e: e.dma_start(out=flagt[:], in_=flags[:, :]), writes=["flagt"])
        S.dma(lambda e: e.dma_start(out=negb[:], in_=b_f[:, :]), writes=["negb"])
        S.op("act", lambda e: e.mul(out=negb[:], in_=negb[:], mul=-1.0), reads=["negb"], writes=["negb"])
        for cc in range(2):
            S.dma(lambda e, cc=cc: e.dma_start(out=pscol[:, cc:cc + 1], in_=pscale[cc * 128:(cc + 1) * 128, :]),
                  writes=["pscol"])

        o = 0
        wf = A3[:, o:o + 96].rearrange("p (k n) -> p k n", k=8); o += 96
        wpair = A3[:, o:o + 3072].rearrange("p (k w n) -> p k w n", k=8, w=3); o += 3072
        wu = A3[:, o:o + 2048].rearrange("p (k n) -> p k n", k=8); o += 2048
        wpbd = A3[:, o:o + 256].rearrange("p (c n) -> p c n", c=2); o += 256
        PT = [A3[:, o + i * 512:o + (i + 1) * 512] for i in range(3)]; o += 1536
        vtmp = A3[:, o:o + 512]; o += 512
        ypT = A3[:, o:o + 4352].rearrange("p (c t) -> p c t", c=2); o += 4352
        go = 16704
        g_et = A2[0:12, go:go + 1024].bitcast(F32); go += 1024
        g_sps = [A2[0:12, go + i * 1024:go + (i + 1) * 1024].bitcast(F32) for i in range(8)]; go += 8192
        g_G = [A2[0:12, go + i * 1024:go + (i + 1) * 1024].bitcast(F32) for i in range(2)]; go += 2048
        g_r1 = A2[0:12, go:go + 1024].bitcast(F32); go += 1024
        g_r2 = A2[0:12, go:go + 1024].bitcast(F32); go += 1024
        g_ones = A2[0:12, go:go + 512]; go += 512
        g_parts = [A2[0:12, go + i * 1536:go + (i + 1) * 1536].rearrange("p (a t) -> p a t", a=3) for i in range(2)]
        go += 3072
        wpst = A3[:, o:o + 512].bitcast(F32).rearrange("p (c n) -> p c n", c=2); o += 512
        yT = A3[:, o:o + 768].rearrange("p (c t) -> p c t", c=6); o += 768
        tmpf = A3[:, o:o + 2048].bitcast(F32); o += 2048
        assert o <= 15360, o

        G_KEYS = ["g_et"] + [("g_sp", i) for i in range(8)] + [("gG", 0), ("gG", 1), "g_r1", "g_r2", "g_ones", ("gp", 0), ("gp", 1)]
        hTk = lambda t0, n: [("hT", b) for b in range(t0 // 128, (t0 + n + 127) // 128)]
        load_gain(0, 0)
        load_gain(2, 2)
        wxk = A2[:, 16704:24896].rearrange("p (k n) -> p k n", k=8)
        wxv = A2[:, 24896:33088].rearrange("p (k n) -> p k n", k=8)
        memT = xhalo[:].bitcast(BF16).rearrange("p (k t) -> p k t", k=8)
        kTm = memkv[:, 0:2048].rearrange("p (k t) -> p k t", k=8)
        Vm = memkv[:, 2048:4104].rearrange("p (m h d) -> p m h d", m=2, h=4)

        def pair_weights(pair):
            for which in range(3):
                load_w(w_in, 0, 8, 256 + which * 768 + pair * 128, 128, wpair[:, :, which, :], ("wpair", which))

        S.op("dve", lambda e: e.memset(Vm[:, :, :, 256:257], 1.0), writes=["Vmones"])
        items = []
        for mb in range(2):
            i = mb % 2
            pre = (lambda mb=mb, i=i: S.dma(lambda e: e.dma_start(out=xl[i][:], in_=memd[mb * 128:(mb + 1) * 128, :]),
                                            writes=[("xl", i)]))
            items.append((pre, xl[i][:], [("xl", i)], 2, (lambda mb=mb: memT[:, :, mb * 128:(mb + 1) * 128]),
                          ("memT", mb), None))
        norm_seq(items)
        load_w(w_in, 0, 8, 2560, 12, wf, "wf", after=[("memT", 1)])
        load_w(w_in, 0, 8, 0, 128, wu[:, :, 0:128], "wu")
        load_w(w_in, 0, 8, 128, 128, wu[:, :, 128:256], "wu")
        S.op("dve", lambda e: e.memset(wpst[:], 0.0), writes=["wpst"])
        for g in range(4):
            cc, hf = g // 2, g % 2
            S.dma(lambda e, g=g, cc=cc, hf=hf: e.dma_start(out=wpst[hf * 64:(hf + 1) * 64, cc, hf * 64:(hf + 1) * 64],
                                                           in_=w_pool[g, :, :]), reads=["wpst"], writes=["wpst"])
        for cc in range(2):
            S.dma(lambda e, cc=cc: e.dma_start(out=invt[:, cc, :], in_=invc[cc, :, :]), writes=["invt"])
        pair_weights(0)
        for k2 in range(8):
            load_w(w_xkv, k2 * 128, 1, 0, 1024, wxk[:, k2:k2 + 1, :], ("wxk", k2 // 2))
            load_w(w_xkv, k2 * 128, 1, 1024, 1024, wxv[:, k2:k2 + 1, :], ("wxv", k2 // 2))
        S.op("pool", lambda e: e.tensor_copy(out=wpbd, in_=wpst), reads=["wpst"], writes=["wpbd"])
        items = []
        xs = A2[:, 0:16384].bitcast(F32).rearrange("p (r d) -> p r d", r=8)
        for kb in range(32):
            i = kb % 8
            pre = (lambda kb=kb, i=i: S.dma(lambda e: e.dma_start(out=xs[:, i, :], in_=xk[kb * 128:(kb + 1) * 128, :]),
                                            reads=([("xs", (kb - 3) % 8)] if kb >= 3 else []), writes=[("xs", i)]))
            items.append((pre, xs[:, i, :], [("xs", i)], 0, (lambda kb=kb: hT[:, :, kb * 128:(kb + 1) * 128]),
                          ("hT", kb), None))
        norm_seq(items[:16])
        mTk = [("memT", 0), ("memT", 1)]
        for fc in range(8):
            b = sb_i[0] % 3; sb_i[0] += 1
            for kc in range(8):
                mm(psum[:, b, 0:256], wxk[:, kc, fc * 128:(fc + 1) * 128], memT[:, kc, :], kc == 0, kc == 7,
                   mTk + [("wxk", kc // 2)], PB(b))
            S.op("act", lambda e, b=b, fc=fc: e.copy(out=kTm[:, fc, :], in_=psum[:, b, 0:256]),
                 reads=[PB(b)], writes=["kTm"])
        for mb in range(2):
            for n in range(2):
                b = sb_i[0] % 3; sb_i[0] += 1
                for kc in range(8):
                    mm(psum[:, b, :], memT[:, kc, mb * 128:(mb + 1) * 128], wxv[:, kc, n * 512:(n + 1) * 512],
                       kc == 0, kc == 7, mTk + [("wxv", kc // 2)], PB(b))
                S.op("act", lambda e, b=b, mb=mb, n=n: e.copy(out=Vm[:, mb, 2 * n:2 * n + 2, 0:256],
                                                              in_=psum[:, b, :].rearrange("p (h d) -> p h d", h=2)),
                     reads=[PB(b)], writes=["Vm"])
        norm_seq(items[16:])
        S.barrier()
        S.op("dve", lambda e: e.memset(KP[0:64, 1, :], 0.0), writes=[("KPaug", 1)])
        S.op("dve", lambda e: e.memset(QP[0:64, 1, :], 0.0), writes=[("QPaug", 1)])
        S.op("pool", lambda e: e.memset(KP[64:128, 0, :], 0.0), writes=[("KPaug", 0)])
        S.op("pool", lambda e: e.memset(QP[64:128, 0, :], 0.0), writes=[("QPaug", 0)])
        S.op("dve", lambda e: e.memset(VP[:, :, :, 64:65], 1.0), writes=["VPones"])
        for s in range(2):
            base = 64 if s == 0 else 0
            S.dma(lambda e, s=s, base=base: e.dma_start(out=KP[base + 3:base + 7, s, :], in_=kconst[:, :]),
                  writes=[("KPaug", s)])
            S.dma(lambda e, s=s, base=base: e.dma_start(out=QP[base:base + 3, s, :], in_=qconst[0:3, :]),
                  writes=[("QPaug", s)])
            S.dma(lambda e, s=s, base=base: e.dma_start(out=QP[base + 6:base + 7, s, :], in_=qconst[3:4, :]),
                  writes=[("QPaug", s)])

        qchunks = [(QOFF, 128, 0)] + [(2048 + 512 * i, 512, 128 + 512 * i) for i in range(4)]
        pt_i = [0]
        npairs = 6 if stage >= 1 else 0
        def pair_aug(pair):
            for s in range(2):
                h = 2 * pair + s
                base = 64 if s == 0 else 0
                S.dma(lambda e, s=s, h=h, base=base: e.dma_start(out=KP[base:base + 3, s, :], in_=gsc[h, :, :]),
                      reads=["gsc"], writes=[("KPaug", s)])
                S.dma(lambda e, s=s, h=h, base=base: e.dma_start(out=QP[base + 3:base + 6, s, :],
                                                                 in_=gsc[h, :, QOFF:NT]),
                      reads=["gsc"], writes=[("QPaug", s)])

        def g_chain():
            S.alias(G_KEYS, [("wxk", k) for k in range(4)] + [("wxv", k) for k in range(4)] + [("memT", 0), ("memT", 1)])
            S.op("dve", lambda e: e.memset(g_ones, 1.0), writes=["g_ones"])
            for tc in range(8):
                b = tc % 3
                for kc in range(8):
                    mm(psum[0:12, b, :], wf[:, kc, :], hT[:, kc, tc * 512:(tc + 1) * 512], kc == 0, kc == 7,
                       ["wf"] + hTk(tc * 512, 512), PB(b))
                S.op("act", lambda e, b=b: e.activation(out=g_et, in_=psum[0:12, b, :], func=AF.Exp,
                                                        bias=negb[:, 0:1], scale=-1.0),
                     reads=[PB(b), "negb"], writes=["g_et"])
                S.op("act", lambda e, tc=tc: e.activation(out=g_sps[tc], in_=g_et, func=AF.Ln, bias=1.0, scale=1.0),
                     reads=["g_et"], writes=[("g_sp", tc)])
            for tc in range(8):
                gi = tc % 2
                init = 0.0 if tc == 0 else g_G[1 - gi][:, 511:512]
                S.op("dve", lambda e, gi=gi, init=init, tc=tc: e.tensor_tensor_scan(out=g_G[gi], data0=g_ones, data1=g_sps[tc],
                                                                                    initial=init, op0=ALU.mult, op1=ALU.add),
                     reads=[("g_sp", tc), "g_ones", ("gG", 1 - gi)], writes=[("gG", gi)])
                pp = g_parts[gi]
                S.op("dve", lambda e, gi=gi, pp=pp: e.tensor_copy(out=pp[:, 0, :], in_=g_G[gi]),
                     reads=[("gG", gi)], writes=[("gp", gi)])
                S.op("dve", lambda e, gi=gi, pp=pp: e.tensor_tensor(out=g_r1, in0=g_G[gi], in1=pp[:, 0, :], op=ALU.subtract),
                     reads=[("gG", gi), ("gp", gi)], writes=["g_r1"])
                S.op("dve", lambda e, pp=pp: e.tensor_copy(out=pp[:, 1, :], in_=g_r1), reads=["g_r1"], writes=[("gp", gi)])
                S.op("dve", lambda e, pp=pp: e.tensor_tensor(out=g_r2, in0=g_r1, in1=pp[:, 1, :], op=ALU.subtract),
                     reads=["g_r1", ("gp", gi)], writes=["g_r2"])
                S.op("dve", lambda e, pp=pp: e.tensor_copy(out=pp[:, 2, :], in_=g_r2), reads=["g_r2"], writes=[("gp", gi)])
                S.dma(lambda e, tc=tc, pp=pp: e.dma_start(out=gsc[:, :, tc * 512:(tc + 1) * 512], in_=pp),
                      reads=[("gp", gi)], writes=["gsc"])
            S.alias([("ymix", qb) for qb in range(NQB)], G_KEYS)

        def pair_proj(pair):
            for ci, (t0, n, q0) in enumerate(qchunks):
                b = sb_i[0] % 3; sb_i[0] += 1
                for kc in range(8):
                    mm(psum[:, b, 0:n], wpair[:, kc, 0, :], hT[:, kc, t0:t0 + n], kc == 0, kc == 7,
                       [("wpair", 0)] + hTk(t0, n), PB(b))
                S.op("act", lambda e, b=b, n=n, q0=q0: e.activation(out=QP[0:64, 0, q0:q0 + n], in_=psum[0:64, b, 0:n],
                                                                    func=AF.Copy, scale=0.125),
                     reads=[PB(b)], writes=[("QP", 0, ci)])
                if pair == 0:
                    S.op("act", lambda e, b=b, n=n, q0=q0: e.activation(out=QP[64:128, 1, q0:q0 + n],
                                                                        in_=psum[64:128, b, 0:n], func=AF.Copy, scale=0.125),
                         reads=[PB(b)], writes=[("QP", 1, ci)])
                else:
                    S.op("dve", lambda e, b=b, n=n, q0=q0: e.tensor_scalar(out=QP[64:128, 1, q0:q0 + n],
                                                                           in0=psum[64:128, b, 0:n], scalar1=0.125,
                                                                           scalar2=None, op0=ALU.mult),
                         reads=[PB(b)], writes=[("QP", 1, ci)])
            for tc in range(8):
                b = sb_i[0] % 3; sb_i[0] += 1
                for kc in range(8):
                    mm(psum[:, b, :], wpair[:, kc, 1, :], hT[:, kc, tc * 512:(tc + 1) * 512], kc == 0, kc == 7,
                       [("wpair", 1)] + hTk(tc * 512, 512), PB(b))
                S.op("act", lambda e, b=b, tc=tc: e.copy(out=KP[0:64, 0, tc * 512:(tc + 1) * 512], in_=psum[0:64, b, :]),
                     reads=[PB(b)], writes=[("KP", 0, tc)])
                if pair == 0:
                    S.op("act", lambda e, b=b, tc=tc: e.copy(out=KP[64:128, 1, tc * 512:(tc + 1) * 512],
                                                             in_=psum[64:128, b, :]),
                         reads=[PB(b)], writes=[("KP", 1, tc)])
                else:
                    S.op("dve", lambda e, b=b, tc=tc: e.tensor_copy(out=KP[64:128, 1, tc * 512:(tc + 1) * 512],
                                                                    in_=psum[64:128, b, :]),
                         reads=[PB(b)], writes=[("KP", 1, tc)])
            for tc in range(8):
                b = sb_i[0] % 3; sb_i[0] += 1
                for kc in range(8):
                    mm(psum[:, b, :], wpair[:, kc, 2, :], hT[:, kc, tc * 512:(tc + 1) * 512], kc == 0, kc == 7,
                       [("wpair", 2)] + hTk(tc * 512, 512), PB(b))
                S.op("act", lambda e, b=b: e.copy(out=vtmp, in_=psum[:, b, :]), reads=[PB(b)], writes=["vtmp"])
                for i in range(4):
                    S.op("pe", lambda e, i=i: e.transpose(out=pst[:, i * 128:(i + 1) * 128],
                                                          in_=vtmp[:, i * 128:(i + 1) * 128], identity=ident[:]),
                         reads=["vtmp", "ident"], writes=[PB(7)], accum=True)
                if pair == 0:
                    S.op("act", lambda e, tc=tc: e.copy(
                        out=VP[:, 4 * tc:4 * tc + 4, :, 0:64],
                        in_=pst[:, 0:512].rearrange("p (b s d) -> p b s d", b=4, s=2)),
                         reads=[PB(7)], writes=[("VP", tc)])
                else:
                    S.op("dve", lambda e, tc=tc: e.tensor_copy(
                        out=VP[:, 4 * tc:4 * tc + 4, :, 0:64],
                        in_=pst[:, 0:512].rearrange("p (b s d) -> p b s d", b=4, s=2)),
                         reads=[PB(7)], writes=[("VP", tc)])

        def pair_attn(pair):
            steps = []
            for s in range(2):
                h = 2 * pair + s
                rows = slice(0, 128)
                for ci, (qb0, nqb) in enumerate([(0, 1), (1, 4), (5, 4), (9, 4), (13, 4)]):
                    q0 = qb0 * 128
                    klast = 15 + qb0 + nqb - 1
                    if ci == 0:
                        for g in range(4):
                            b = sb_i[0] % 3; sb_i[0] += 1
                            pi = pt_i[0] % 3; pt_i[0] += 1

                            def fS0(s=s, g=g, b=b):
                                for t in range(4):
                                    j = 4 * g + t
                                    dg = (j == 15)
                                    mm(psum[:, b, t * 128:(t + 1) * 128], KP[:, s, j * 128:(j + 1) * 128], QP[:, s, 0:128],
                                       True, not dg, [("KP", s, g), ("KPaug", s), ("QP", s, 0), ("QPaug", s)], PB(b))
                                    if dg:
                                        mm(psum[:, b, t * 128:(t + 1) * 128], ident[:], cbias[:], False, True,
                                           ["ident", "cbias"], PB(b))

                            def fE0(b=b, pi=pi):
                                S.op("act", lambda e: e.activation(out=PT[pi][:, 0:512], in_=psum[:, b, 0:512], func=AF.Exp),
                                     reads=[PB(b)], writes=[("PT", pi)])

                            def fPV0(s=s, g=g, pi=pi, h=h):
                                for t in range(4):
                                    j = 4 * g + t
                                    mm(psum[:, 3, 0:65], PT[pi][:, t * 128:(t + 1) * 128], VP[:, j, s, :],
                                       j == 0, j == 15, [("PT", pi), ("VP", g), "VPones"], PB(3))
                                if g == 3:
                                    col, key = rl_col()
                                    S.op("dve", lambda e: e.reciprocal(out=col, in_=psum[:, 3, 64:65]),
                                         reads=[PB(3)], writes=[key])
                                    S.op("dve", lambda e: e.tensor_scalar(
                                        out=ymix[:, 0, h * 64:(h + 1) * 64], in0=psum[:, 3, 0:64], scalar1=col,
                                        scalar2=None, op0=ALU.mult), reads=[PB(3), key], writes=[("ymix", 0)])

                            steps.append((fS0, fE0, fPV0))
                        continue
                    for j in range(klast + 1):
                        m = max(0, j - (15 + qb0))
                        if m == 3:
                            continue
                        c0, c1 = m * 128, nqb * 128
                        b = sb_i[0] % 3; sb_i[0] += 1
                        pi = pt_i[0] % 3; pt_i[0] += 1
                        diag = j >= 15 + qb0
                        if m == 2:
                            def fS2(s=s, ci=ci, q0=q0, j=j, b=b):
                                kk = [("KP", s, j // 4), ("KP", s, (j + 1) // 4), ("KPaug", s), ("QP", s, ci), ("QPaug", s)]
                                mm(psum[:, b, 0:256], KP[:, s, j * 128:(j + 1) * 128], QP[:, s, q0 + 256:q0 + 512],
                                   True, False, kk, PB(b))
                                mm(psum[:, b, 0:128], ident[:], cbias[:], False, True, ["ident", "cbias"], PB(b))
                                mm(psum[:, b, 256:384], KP[:, s, (j + 1) * 128:(j + 2) * 128], QP[:, s, q0 + 384:q0 + 512],
                                   True, False, kk, PB(b))
                                mm(psum[:, b, 256:384], ident[:], cbias[:], False, True, ["ident", "cbias"], PB(b))

                            def fE2(b=b, pi=pi):
                                S.op("act", lambda e: e.activation(out=PT[pi][:, 0:384], in_=psum[:, b, 0:384], func=AF.Exp),
                                     reads=[PB(b)], writes=[("PT", pi)])

                            def fPV2(s=s, j=j, nqb=nqb, qb0=qb0, pi=pi, h=h):
                                vk = [("PT", pi), ("VP", j // 4), ("VP", (j + 1) // 4), "VPones"]
                                mm(psum[:, 3 + 2, 0:65], PT[pi][:, 0:128], VP[:, j, s, :], False, True, vk, PB(3 + 2))
                                mm(psum[:, 3 + 3, 0:65], PT[pi][:, 128:256], VP[:, j, s, :], False, False, vk, PB(3 + 3))
                                mm(psum[:, 3 + 3, 0:65], PT[pi][:, 256:384], VP[:, j + 1, s, :], False, True, vk, PB(3 + 3))
                                cks = []
                                for lq in range(nqb):
                                    col, key = rl_col()
                                    cks.append((col, key))
                                    S.op("dve", lambda e, lq=lq, col=col: e.reciprocal(out=col, in_=psum[:, 3 + lq, 64:65]),
                                         reads=[PB(3 + lq)], writes=[key])
                                for lq in range(nqb):
                                    col, key = cks[lq]
                                    S.op("dve", lambda e, lq=lq, col=col, qb=qb0 + lq: e.tensor_scalar(
                                        out=ymix[:, qb, h * 64:(h + 1) * 64], in0=psum[:, 3 + lq, 0:64], scalar1=col,
                                        scalar2=None, op0=ALU.mult),
                                         reads=[PB(3 + lq), key], writes=[("ymix", qb0 + lq)])

                            steps.append((fS2, fE2, fPV2))
                            continue

                        def fS(s=s, rows=rows, ci=ci, q0=q0, j=j, c0=c0, c1=c1, b=b, diag=diag):
                            mm(psum[:, b, c0:c1], KP[rows, s, j * 128:(j + 1) * 128], QP[rows, s, q0 + c0:q0 + c1],
                               True, not diag, [("KP", s, j // 4), ("KPaug", s), ("QP", s, ci), ("QPaug", s)], PB(b))
                            if diag:
                                mm(psum[:, b, c0:c0 + 128], ident[:], cbias[:], False, True, ["ident", "cbias"], PB(b))

                        def fE(b=b, c0=c0, c1=c1, pi=pi):
                            S.op("act", lambda e: e.activation(out=PT[pi][:, c0:c1], in_=psum[:, b, c0:c1], func=AF.Exp),
                                 reads=[PB(b)], writes=[("PT", pi)])

                        def fPV(s=s, j=j, m=m, nqb=nqb, qb0=qb0, pi=pi, h=h, last=(j == klast)):
                            for lq in range(m, nqb):
                                mm(psum[:, 3 + lq, 0:65], PT[pi][:, lq * 128:(lq + 1) * 128], VP[:, j, s, :],
                                   j == 0, j == 15 + qb0 + lq, [("PT", pi), ("VP", j // 4), "VPones"], PB(3 + lq))
                            if last:
                                cks = []
                                for lq in range(nqb):
                                    col, key = rl_col()
                                    cks.append((col, key))
                                    S.op("dve", lambda e, lq=lq, col=col: e.reciprocal(out=col, in_=psum[:, 3 + lq, 64:65]),
                                         reads=[PB(3 + lq)], writes=[key])
                                for lq in range(nqb):
                                    col, key = cks[lq]
                                    S.op("dve", lambda e, lq=lq, col=col, qb=qb0 + lq: e.tensor_scalar(
                                        out=ymix[:, qb, h * 64:(h + 1) * 64], in0=psum[:, 3 + lq, 0:64], scalar1=col,
                                        scalar2=None, op0=ALU.mult),
                                         reads=[PB(3 + lq), key], writes=[("ymix", qb0 + lq)])

                        steps.append((fS, fE, fPV))
            LA = 2
            for idx in range(len(steps) + LA):
                if idx < len(steps):
                    steps[idx][0]()
                    steps[idx][1]()
                if idx - LA >= 0:
                    steps[idx - LA][2]()

        wmixA = A2[:, 29760:33856].rearrange("p (k n) -> p k n", k=4)
        wmixC = A3[:, 96:3168].rearrange("p (k n) -> p k n", k=3)
        wmixD = A3[:, 5472:6496]
        for pair in range(npairs):
            if pair == 2:
                S.alias([("wmix", 0), ("wmix", 1)], G_KEYS)
                for k2 in range(4):
                    load_w(w_mix, k2 * 128, 1, 0, 1024, wmixA[:, k2:k2 + 1, :], ("wmix", k2 // 2))
            if pair == 0:
                g_chain()
            pair_proj(pair)
            if pair + 1 < npairs:
                pair_weights(pair + 1)
            if pair == npairs - 1:
                S.alias([("wmix", 2), ("wmix", 3)], [("wpair", w_) for w_ in range(3)] + ["wf"])
                for k2 in range(4, 7):
                    load_w(w_mix, k2 * 128, 1, 0, 1024, wmixC[:, k2 - 4:k2 - 3, :], ("wmix", k2 // 2))
            pair_aug(pair)
            pair_attn(pair)
            if pair == npairs - 1:
                S.alias([("wmix", 3)], [("PT", i_) for i_ in range(3)] + ["vtmp"])
                load_w(w_mix, 7 * 128, 1, 0, 1024, wmixD.rearrange("p (k n) -> p k n", k=1), ("wmix", 3))

        S.barrier()
        NP_ = 2304
        P0 = 1792
        uT = A2[:, 0:4608].bitcast(F32)
        sA = A2[:, 4608:9216].bitcast(F32)
        sB = A2[:, 9216:13824].bitcast(F32)
        dT = A2[:, 13824:13824 + 2304]
        t16 = A2[:, 16128:16160].bitcast(F32)
        for cc in range(2):
            for (a, n) in [(0, 512), (512, 512), (1024, 512), (1536, 512), (2048, 256)]:
                b = sb_i[0] % 3; sb_i[0] += 1
                for kc in range(8):
                    mm(psum[:, b, 0:n], wu[:, kc, cc * 128:(cc + 1) * 128], hT[:, kc, P0 + a:P0 + a + n],
                       kc == 0, kc == 7, ["wu"], PB(b))
                S.op("act", lambda e, b=b, a=a, n=n: e.copy(out=uT[:, a:a + n], in_=psum[:, b, 0:n]),
                     reads=[PB(b)], writes=["uT"])
            N = NP_

            def tt(out_ap, a_ap, b_ap, rd, wr):
                S.op("dve", lambda e: e.tensor_tensor(out=out_ap, in0=a_ap, in1=b_ap, op=ALU.add), reads=rd, writes=wr)

            def msk_add(sh, lo_, cc=cc, N=NP_):
                S.op("dve", lambda e: e.scalar_tensor_tensor(out=sB[:, lo_:N], in0=sA[:, lo_ - sh:N - sh], scalar=pmk[:, 0:1],
                                                             in1=sA[:, lo_:N], op0=ALU.mult, op1=ALU.add),
                     reads=["sA", "pmk"], writes=["sB"])

            def dif_all(cc=cc, N=NP_):
                S.op("dve", lambda e: e.scalar_tensor_tensor(out=dT[:, 128:N], in0=sB[:, 128:N], scalar=pmk[:, 1 + cc:2 + cc],
                                                             in1=uT[:, 128:N], op0=ALU.mult, op1=ALU.subtract),
                     reads=["sB", "uT", "pmk"], writes=["dT"])
                S.op("dve", lambda e: e.tensor_tensor(out=t16[:, :], in0=sB[:, 256:272], in1=invt[:, cc, :],
                                                      op=ALU.mult), reads=["sB", "invt"], writes=["t16"])
                S.op("dve", lambda e: e.tensor_tensor(out=dT[:, 256:272], in0=t16[:, :], in1=uT[:, 256:272],
                                                      op=ALU.subtract), reads=["t16", "uT", "dT"], writes=["dT"])

            tt(sA[:, 1:N], uT[:, 1:N], uT[:, 0:N - 1], ["uT"], ["sA"])
            if cc == 0:
                msk_add(2, 3)
            else:
                tt(sB[:, 3:N], sA[:, 3:N], sA[:, 1:N - 2], ["sA"], ["sB"])
                tt(sA[:, 7:N], sB[:, 7:N], sB[:, 3:N - 4], ["sB"], ["sA"])
                msk_add(8, 15)
            dif_all()
            for (a, n) in [(0, 512), (512, 512), (1024, 512), (1536, 512), (2048, 128)]:
                b = sb_i[0] % 3; sb_i[0] += 1
                mm(psum[:, b, 0:n], wpbd[:, cc, :], dT[:, 128 + a:128 + a + n], True, True, ["wpbd", "dT"], PB(b))
                S.op("act", lambda e, b=b, a=a, n=n, cc=cc: e.activation(out=ypT[:, cc, a:a + n], in_=psum[:, b, 0:n],
                                                                         func=AF.Identity, scale=pscol[:, cc:cc + 1]),
                     reads=[PB(b), "pscol"], writes=["ypT"])

        S.barrier()
        wxq = A2[:, 0:8192].rearrange("p (k n) -> p k n", k=8)
        wxo = A2[:, 8208:16400].rearrange("p (k n) -> p k n", k=8)
        for k2 in range(8):
            pass
        load_gain(1, 1)
        for k2 in range(8):
            load_w(w_xq, k2 * 128, 1, 0, 1024, wxq[:, k2:k2 + 1, :], ("wxq", k2 // 2))
        wmk = [("wmix", k2) for k2 in range(4)]

        def post_norm_residual(ybanks, gi, xsrc, xsrc_reads, dst, dst_key):
            yv = psum[:, ybanks[0]:ybanks[0] + 2, :].rearrange("p a b -> p (a b)")
            yk = [PB(ybanks[0]), PB(ybanks[1])]
            col, key = rstd_of(yv, yk)
            S.op("dve", lambda e: e.scalar_tensor_tensor(out=tmpf, in0=yv, scalar=col, in1=gt[gi][:],
                                                         op0=ALU.mult, op1=ALU.mult),
                 reads=yk + [key, ("gt", gi)], writes=["tmpf"])
            S.op("dve", lambda e: e.tensor_tensor(out=dst, in0=tmpf, in1=xsrc, op=ALU.add),
                 reads=["tmpf"] + list(xsrc_reads), writes=[dst_key])

        nB = NQB if stage >= 1 else 0
        yT2 = A2[:, 33856:34624].rearrange("p (c t) -> p c t", c=6)
        yTb = [yT, yT2]

        def b_T(qb):
            for c in range(6):
                S.op("pe", lambda e, c=c: e.transpose(out=pst[:, c * 128:(c + 1) * 128],
                                                      in_=ymix[:, qb, c * 128:(c + 1) * 128], identity=ident[:]),
                     reads=[("ymix", qb), "ident"], writes=[PB(7)], accum=True)
            S.op("act", lambda e: e.copy(out=yTb[qb % 2], in_=pst[:, 0:768].rearrange("p (c t) -> p c t", c=6)),
                 reads=[PB(7)], writes=[("yT", qb % 2)])

        def b_mm(qb):
            yb = (0, 1) if qb % 2 == 0 else (2, 3)
            for n in range(2):
                for kc in range(8):
                    lhsT = ypT[:, kc, qb * 128:(qb + 1) * 128] if kc < 2 else yTb[qb % 2][:, kc - 2, :]
                    wsl = (wmixA[:, kc, n * 512:(n + 1) * 512] if kc < 4 else
                           wmixC[:, kc - 4, n * 512:(n + 1) * 512] if kc < 7 else wmixD[:, n * 512:(n + 1) * 512])
                    mm(psum[:, yb[n], :], lhsT, wsl, kc == 0, kc == 7,
                       ["ypT", ("yT", qb % 2), ("wmix", kc // 2)], PB(yb[n]))
            i = qb % 2
            S.dma(lambda e: e.dma_start(out=xl[i][:], in_=xk[QOFF + qb * 128:QOFF + (qb + 1) * 128, :]),
                  writes=[("xl", i)])
            post_norm_residual(yb, 1, xl[i][:], [("xl", i)], xr(qb), ("xres", qb))

        if nB:
            b_T(0)
        for qb in range(nB):
            if qb + 1 < nB:
                b_T(qb + 1)
            b_mm(qb)

        if stage == 1:
            import os
            if os.environ.get("BAR"):
                S.barrier()
            toks = []
            for qb in range(1, NQB):
                toks.append(S.dma(lambda e, qb=qb: e.dma_start(out=out[(qb - 1) * 128:qb * 128, :], in_=xr(qb)),
                                  reads=[("xres", qb)], writes=[("outd", qb)]))
            S.barrier()
            S.emit()
            return nc

        S.alias([("wxo", k) for k in range(4)], [("wmix", k) for k in range(4)])
        S.alias([("h2T", i) for i in range(4)] + [("q2T", i) for i in range(8)] + [("oc", i) for i in range(4)],
                [("ymix", qb) for qb in range(NQB)])
        S.alias([("PTx", i) for i in range(4)] + [("ocT", 0), ("ocT", 1)], ["wf", "wu", ("wmix", 2), ("wmix", 3)] + [("wpair", i) for i in range(3)])
        o = 0
        PTx = [A3[:, o + i * 512:o + (i + 1) * 512] for i in range(4)]; o += 2048
        ocT = A3[:, o:o + 1024].rearrange("p (c t) -> p c t", c=8); o += 1024
        ocT2 = A3[:, o:o + 1024].rearrange("p (c t) -> p c t", c=8); o += 1024
        for k2 in range(8):
            load_w(w_xo, k2 * 128, 1, 0, 1024, wxo[:, k2:k2 + 1, :], ("wxo", k2 // 2))
        load_gain(0, 3)
        load_gain(1, 4)
        h2T = A2[:, 16400:20496].rearrange("p (k t) -> p k t", k=8)
        q2T = A2[:, 20496:24592].rearrange("p (k t) -> p k t", k=8)
        oc = A2[:, 24592:28688].rearrange("p (b d) -> p b d", b=4)
        wq_k = [("wxq", k2) for k2 in range(4)]
        wo_k = [("wxo", k2) for k2 in range(4)]
        px_i = [0]
        ob_i = [0]
        out_tokens = []
        chunks = [(0, 1), (1, 4), (5, 4), (9, 4), (13, 4)]
        ocTb = [ocT, ocT2]
        oct_i = [0]
        yb_i = [0]

        def c_norms(ci):
            qb0, nqb = chunks[ci]
            norm_seq([(None, xr(qb0 + lb), [("xres", qb0 + lb)], 0,
                       (lambda lb=lb: h2T[:, :, lb * 128:(lb + 1) * 128]), ("h2T", lb), None) for lb in range(nqb)])

        def c_qproj(ci, fc):
            qb0, nqb = chunks[ci]
            N = nqb * 128
            h2k = [("h2T", lb) for lb in range(nqb)]
            b = sb_i[0] % 3; sb_i[0] += 1
            for kc in range(8):
                mm(psum[:, b, 0:N], wxq[:, kc, fc * 128:(fc + 1) * 128], h2T[:, kc, 0:N], kc == 0, kc == 7,
                   h2k + [("wxq", kc // 2)], PB(b))
            S.op("act", lambda e: e.activation(out=q2T[:, fc, 0:N], in_=psum[:, b, 0:N], func=AF.Copy, scale=0.0625),
                 reads=[PB(b)], writes=[("q2T", fc)])

        def c_attn(ci):
            qb0, nqb = chunks[ci]
            N = nqb * 128

            def s_stage(hh):
                pis = []
                for mb in range(2):
                    b = sb_i[0] % 3; sb_i[0] += 1
                    for d in range(2):
                        mm(psum[:, b, 0:N], kTm[:, 2 * hh + d, mb * 128:(mb + 1) * 128], q2T[:, 2 * hh + d, 0:N],
                           d == 0, d == 1, ["kTm", ("q2T", 2 * hh + d)], PB(b))
                    pi = px_i[0] % 4; px_i[0] += 1
                    pis.append(pi)
                    S.op("act", lambda e, b=b, pi=pi: e.activation(out=PTx[pi][:, 0:N], in_=psum[:, b, 0:N], func=AF.Exp),
                         reads=[PB(b)], writes=[("PTx", pi)])
                return pis

            def pv_stage(hh, pis):
                for lb in range(nqb):
                    ob = 3 + (ob_i[0] % 4); ob_i[0] += 1
                    for mb in range(2):
                        mm(psum[:, ob, 0:257], PTx[pis[mb]][:, lb * 128:(lb + 1) * 128], Vm[:, mb, hh, :],
                           mb == 0, mb == 1, [("PTx", pis[mb]), "Vm", "Vmones"], PB(ob))
                    col, key = rl_col()
                    S.op("dve", lambda e, ob=ob, col=col: e.reciprocal(out=col, in_=psum[:, ob, 256:257]),
                         reads=[PB(ob)], writes=[key])
                    S.op("dve", lambda e, ob=ob, col=col, lb=lb: e.tensor_scalar(
                        out=oc[:, lb, hh * 256:(hh + 1) * 256], in0=psum[:, ob, 0:256], scalar1=col, scalar2=None,
                        op0=ALU.mult), reads=[PB(ob), key], writes=[("oc", lb)])

            prev = None
            for hh in range(4):
                pis = s_stage(hh)
                if prev is not None:
                    pv_stage(*prev)
                prev = (hh, pis)
            pv_stage(*prev)

        def c_out_T(ci, lb):
            k = oct_i[0] % 2; oct_i[0] += 1
            for c in range(8):
                S.op("pe", lambda e, c=c: e.transpose(out=pst[:, c * 128:(c + 1) * 128],
                                                      in_=oc[:, lb, c * 128:(c + 1) * 128], identity=ident[:]),
                     reads=[("oc", lb), "ident"], writes=[PB(7)], accum=True)
            S.op("act", lambda e: e.copy(out=ocTb[k], in_=pst.rearrange("p (c t) -> p c t", c=8)),
                 reads=[PB(7)], writes=[("ocT", k)])
            return k

        def c_out_mm(ci, lb, k):
            qb0, nqb = chunks[ci]
            qb = qb0 + lb
            yb = (5, 6) if yb_i[0] % 2 == 0 else (3, 4)
            yb_i[0] += 1
            for n in range(2):
                for kc in range(8):
                    mm(psum[:, yb[n], :], ocTb[k][:, kc, :], wxo[:, kc, n * 512:(n + 1) * 512], kc == 0, kc == 7,
                       [("ocT", k), ("wxo", kc // 2)], PB(yb[n]))
            post_norm_residual(yb, 1, xr(qb), [("xres", qb)], xr(qb), ("xres", qb))
            if qb >= 1:
                out_tokens.append(S.dma(lambda e: e.dma_start(out=out[(qb - 1) * 128:qb * 128, :], in_=xr(qb)),
                                        reads=[("xres", qb)], writes=[("outd", qb - 1)]))

        c_norms(0)
        for fc in range(8):
            c_qproj(0, fc)
        c_attn(0)
        h3T = A2[:, 0:8208].rearrange("p (k t) -> p k t", k=8)
        load_gain(2, 5)
        h3_items = [(None, xhalo[:], [("xres", 0)], 2, (lambda: h3T[:, :, 0:2]), ("h3T", -1), (126, 128))]
        for lb in range(8):
            h3_items.append((None, xr(1 + lb), [("xres", 1 + lb)], 2,
                             (lambda lb=lb: h3T[:, :, 2 + lb * 128:2 + (lb + 1) * 128]), ("h3T", lb), None))
        for ci in range(len(chunks)):
            nqb = chunks[ci][1]
            nxt = ci + 1 < len(chunks)
            if nxt:
                c_norms(ci + 1)
            else:
                S.alias([("h3T", lb) for lb in range(-1, 8)], [("wxq", k_) for k_ in range(4)])
            fcs = list(range(8))
            per = (8 + nqb - 1) // nqb
            for lb in range(nqb):
                k = c_out_T(ci, lb)
                if nxt:
                    for fc in fcs[lb * per:(lb + 1) * per]:
                        c_qproj(ci + 1, fc)
                elif stage >= 4:
                    norm_seq(h3_items[lb * 3:(lb + 1) * 3])
                c_out_mm(ci, lb, k)
            if nxt:
                c_attn(ci + 1)
        if stage >= 4:
            S.op("pool", lambda e: e.tensor_copy(out=h3h[:, :, :], in_=h3T[:, :, 1024:1026]),
                 reads=[("h3T", 7)], writes=["h3h"])
        if stage == 2:
            S.barrier()
            S.emit()
            return nc

        S.barrier()
        o = 0
        o += 8208
        wupb = [A2[:, o + i * 2048:o + (i + 1) * 2048].rearrange("p (g k n) -> p g k n", g=2, k=8) for i in range(2)]
        o += 4096
        wdnb = [A2[:, o + i * 512:o + (i + 1) * 512] for i in range(8)]
        o += 4096
        ylo = A3[:, 0:8192].bitcast(F32).rearrange("p (b n) -> p b n", b=8)
        yhb = [A3[:, 8192 + i * 1024:8192 + (i + 1) * 1024].bitcast(F32) for i in range(4)]
        yh_i = [0]
        hgb = [A2[:, o + i * 2052:o + (i + 1) * 2052].bitcast(F32) for i in range(2)]; o += 4104
        hub = [A2[:, o + i * 2052:o + (i + 1) * 2052].bitcast(F32) for i in range(2)]; o += 4104
        cvg = [A2[:, o + i * 1024:o + (i + 1) * 1024].bitcast(F32) for i in range(2)]; o += 2048
        cvu = [A2[:, o + i * 1024:o + (i + 1) * 1024].bitcast(F32) for i in range(2)]; o += 2048
        glb = [A2[:, o + i * 1024:o + (i + 1) * 1024].bitcast(F32) for i in range(2)]; o += 2048
        cwr = A2[0:64, o:o + 1024].bitcast(F32).rearrange("p (k n) -> p k n", k=4); o += 1024
        assert o <= 34816
        load_gain(0, 6)
        xpre = [(A3[:, i * 2048:(i + 1) * 2048].bitcast(F32), ("xpre", i)) for i in range(6)] + \
               [(xl[0][:], ("xl", 0)), (xl[1][:], ("xl", 1))]
        S.dma(lambda e: e.dma_start(out=cwr[:, 0:3, :], in_=conv_w.rearrange("k (c p) -> c k p", p=128)), writes=["cwr"])
        S.dma(lambda e: e.dma_start(out=cwr[:, 3, :], in_=conv_b.rearrange("k (c p) -> (k c) p", p=128)),
              writes=["cwr"])
        for k in range(4):
            S.op("pe", lambda e, k=k: e.transpose(out=psum[:, 0, k * 64:(k + 1) * 64], in_=cwr[:, k, :],
                                                  identity=identf[0:64, 0:64]),
                 reads=["cwr", "identf"], writes=[PB(0)], accum=True)
        S.op("act", lambda e: e.copy(out=cw[:], in_=psum[:, 0, 0:256].rearrange("p (k n) -> p k n", k=4)),
             reads=[PB(0)], writes=["cw"])

        it = [0]
        wu_i = [0]
        wd_i = [0]
        def ffn_h3T(th):
            if th == 1:
                S.op("pool", lambda e: e.tensor_copy(out=h3T[:, :, 0:2], in_=h3h[:, :, :]),
                     reads=["h3h"], writes=[("h3T", -1)])
            items = []
            for lb in range(-1 if th == 0 else 0, 8):
                qb = 1 + 8 * th + lb
                i = (lb + 1) % 2
                if qb == 0:
                    pre, src, srck = None, xhalo[:], [("xres", 0)]
                elif th == 1:
                    pre = None
                    src, srck = xpre[lb][0], [xpre[lb][1]]
                else:
                    pre = (lambda qb=qb, i=i: S.dma(lambda e: e.dma_start(out=xl[i][:], in_=out[(qb - 1) * 128:qb * 128, :]),
                                                    reads=[("outd", qb - 1)], writes=[("xl", i)]))
                    src, srck = xl[i][:], [("xl", i)]
                if lb == -1:
                    items.append((pre, src, srck, 2, (lambda: h3T[:, :, 0:2]), ("h3T", -1), (126, 128)))
                else:
                    items.append((pre, src, srck, 2, (lambda lb=lb: h3T[:, :, 2 + lb * 128:2 + (lb + 1) * 128]),
                                  ("h3T", lb), None))
            norm_seq(items)
            if th == 1:
                S.alias([("ylo", i) for i in range(8)] + [("yhb", i) for i in range(4)], [("xpre", i) for i in range(6)])
            if th == 0:
                S.op("pool", lambda e: e.tensor_copy(out=h3h[:, :, :], in_=h3T[:, :, 1024:1026]),
                     reads=[("h3T", 7)], writes=["h3h"])
        def load_up(fc):
            wb = fc % 2
            load_w(w_up, 0, 8, fc * 128, 128, wupb[wb][:, 0, :, :], ("wup", wb, 0))
            load_w(w_up, 0, 8, 4096 + fc * 128, 128, wupb[wb][:, 1, :, :], ("wup", wb, 1))

        def ffn_up(th, preloaded=0, deferred=(), xprefetch=False):
            pending = []
            if preloaded < 1:
                load_up(0)
            for fc in range(32):
                wb = fc % 2
                if fc == 3:
                    for d_ in deferred:
                        d_()
                if xprefetch and 4 <= fc < 12:
                    lb_ = fc - 4
                    buf_, bkey_ = xpre[lb_]
                    S.dma(lambda e, lb_=lb_, buf_=buf_: e.dma_start(out=buf_, in_=out[(8 + lb_) * 128:(9 + lb_) * 128, :]),
                          reads=[("outd", 8 + lb_), ("actT", fc - 2, 1)], writes=[bkey_])
                if fc + 1 < 32 and fc + 1 >= preloaded:
                    load_up(fc + 1)
                hbufs = (hgb[wb], hub[wb])
                for g in range(2):
                    for kc in range(8):
                        mm(psum[:, 4, 2 * g:2 * g + 2], wupb[wb][:, g, kc, :], h3T[:, kc, 0:2],
                           kc == 0, kc == 7, [("wup", wb, g), ("h3T", -1)], PB(4))
                fcol = flagt[:, 0:1] if th == 0 else flagt[:, 1:2]
                for g in range(2):
                    S.op("dve", lambda e, g=g, hbuf=hbufs[g], fcol=fcol: e.tensor_scalar(
                        out=hbuf[:, 0:2], in0=psum[:, 4, 2 * g:2 * g + 2], scalar1=fcol, scalar2=None,
                        op0=ALU.mult), reads=[PB(4), "flagt"], writes=[("hbuf", wb, g, "h")])
                for tc in range(2):
                    ii = it[0] % 2; it[0] += 1
                    bg, bu = 2 * ii, 2 * ii + 1
                    hk = [("h3T", lb) for lb in range(4 * tc, 4 * tc + 4)]
                    for g, bb in ((0, bg), (1, bu)):
                        for kc in range(8):
                            mm(psum[:, bb, :], wupb[wb][:, g, kc, :], h3T[:, kc, 2 + tc * 512:2 + (tc + 1) * 512],
                               kc == 0, kc == 7, [("wup", wb, g)] + hk, PB(bb))
                    cvs = (cvg[ii], cvu[ii])
                    c0 = tc * 512
                    for g, bb in ((0, bg), (1, bu)):
                        fcg = fc + 32 * g
                        S.op("act", lambda e, bb=bb, hbuf=hbufs[g], c0=c0: e.copy(out=hbuf[:, 2 + c0:514 + c0],
                                                                                 in_=psum[:, bb, :]),
                             reads=[PB(bb)], writes=[("hbuf", wb, g, tc)])
                        S.op("act", lambda e, bb=bb, cv=cvs[g], fcg=fcg: e.activation(
                            out=cv, in_=psum[:, bb, :], func=AF.Identity, scale=cw[:, 2, fcg:fcg + 1],
                            bias=cw[:, 3, fcg:fcg + 1]), reads=[PB(bb), "cw"], writes=[("cv", ii, g)])
                    prevk = "h" if tc == 0 else 0
                    for tap, sh in ((1, 1), (0, 0)):
                        for g in range(2):
                            fcg = fc + 32 * g
                            S.op("dve", lambda e, cv=cvs[g], hbuf=hbufs[g], fcg=fcg, tap=tap, sh=sh, c0=c0:
                                 e.scalar_tensor_tensor(out=cv, in0=hbuf[:, c0 + sh:c0 + sh + 512],
                                                        scalar=cw[:, tap, fcg:fcg + 1], in1=cv, op0=ALU.mult, op1=ALU.add),
                                 reads=[("hbuf", wb, g, tc), ("hbuf", wb, g, prevk), ("cv", ii, g), "cw"],
                                 writes=[("cv", ii, g)])

                    def tail(ii=ii, fc=fc, tc=tc):
                        S.op("act", lambda e: e.activation(out=glb[ii], in_=cvg[ii], func=AF.Gelu_apprx_tanh),
                             reads=[("cv", ii, 0)], writes=[("gl", ii)])
                        S.op("dve", lambda e: e.tensor_tensor(out=actT[:, fc, tc * 512:(tc + 1) * 512], in0=glb[ii],
                                                              in1=cvu[ii], op=ALU.mult),
                             reads=[("gl", ii), ("cv", ii, 1)], writes=[("actT", fc, tc)])

                    for t_ in pending:
                        t_()
                    pending = [tail]
            for t_ in pending:
                t_()
            pending = []
        obufs = [(xl[0][:], ("xl", 0)), (xl[1][:], ("xl", 1)), (tmpf, "tmpf"), (gt[1][:], ("gt", 1)), (gt[2][:], ("gt", 2))]
        ob_rot = [0]
        fin_th = [0]

        def fin(lb, cols, ybs):
            col, key = cols[lb]
            yb_ = ybs[lb]
            finish_rstd(col, key)
            ob = 8 * fin_th[0] + lb
            buf, bkey = obufs[ob_rot[0] % len(obufs)]; ob_rot[0] += 1
            S.op("dve", lambda e: e.scalar_tensor_tensor(
                out=buf[:, 0:512], in0=ylo[:, lb, :], scalar=col, in1=gt[0][:, 0:512],
                op0=ALU.mult, op1=ALU.mult), reads=[("ylo", lb), key, ("gt", 0)], writes=[bkey])
            S.op("dve", lambda e: e.scalar_tensor_tensor(
                out=buf[:, 512:1024], in0=yhb[yb_], scalar=col, in1=gt[0][:, 512:1024],
                op0=ALU.mult, op1=ALU.mult), reads=[("yhb", yb_), key, ("gt", 0), bkey], writes=[bkey])
            S.dma(lambda e: e.dma_start(out=out[ob * 128:(ob + 1) * 128, :], in_=buf, accum_op=ALU.add),
                  reads=[bkey], writes=[("outd", ob)], queue="pool")

        def ffn_down(th):
            fin_th[0] = th
            ssA = {}
            cols = {}
            ybs = {}

            def mm_pass(half, lbs):
                for kc in range(32):
                    db = wd_i[0] % 8; wd_i[0] += 1
                    S.dma(lambda e, kc=kc, db=db: e.dma_start(
                        out=wdnb[db], in_=w_down[kc * 128:(kc + 1) * 128, half * 512:(half + 1) * 512]),
                          writes=[("wdn", db)], queue="pool")
                    for lb in lbs:
                        mm(psum[:, lb, :], actT[:, kc, lb * 128:(lb + 1) * 128], wdnb[db], kc == 0, kc == 31,
                           [("actT", kc, lb // 4), ("wdn", db)], PB(lb))

            def epi(half, lbs):
                for lb in lbs:
                    col, key = rs_col()
                    cols[lb] = (col, key)
                    S.op("act", lambda e, lb=lb, col=col: e.activation(out=junk[:, 0:512], in_=psum[:, lb, :],
                                                                       func=AF.Square, accum_out=col),
                         reads=[PB(lb)], writes=[key])
                    if half == 0:
                        ssA[lb] = (col, key)
                        S.op("act", lambda e, lb=lb: e.copy(out=ylo[:, lb, :], in_=psum[:, lb, :]),
                             reads=[PB(lb)], writes=[("ylo", lb)])
                    else:
                        colA, keyA = ssA[lb]
                        yb_ = yh_i[0] % 4; yh_i[0] += 1
                        ybs[lb] = yb_
                        S.op("act", lambda e, lb=lb, yb_=yb_: e.copy(out=yhb[yb_], in_=psum[:, lb, :]),
                             reads=[PB(lb)], writes=[("yhb", yb_)])
                        S.op("dve", lambda e, col=col, colA=colA: e.tensor_tensor(out=col, in0=col, in1=colA, op=ALU.add),
                             reads=[key, keyA], writes=[key])
                if half == 1:
                    for lb in lbs:
                        fin(lb, cols, ybs)

            mm_pass(0, range(8))
            epi(0, range(8))
            mm_pass(1, range(0, 4))
            mm_pass(1, range(4, 8))
            epi(1, range(0, 4))
            epi(1, range(4, 8))
        def xpre_loads():
            pass

        load_up(0)
        ffn_up(0, preloaded=1, xprefetch=(stage >= 4))
        load_up(0)
        load_up(1)
        ffn_h3T(1)
        ffn_down(0)
        ffn_up(1, preloaded=2)
        ffn_down(1)
        S.barrier()
        S.emit()
    return nc


_CACHE = {}


def _prep_inputs(inp):
    f = lambda a: np.ascontiguousarray(np.asarray(a, dtype=np.float32))
    x = f(inp["x"]); mem = f(inp["mem"])
    gains = np.stack([f(inp[k])[0] for k in ("norm_mix_pre", "norm_mix_post", "norm_mem", "norm_xa_pre",
                                             "norm_xa_post", "norm_ffn_pre", "norm_ffn_post")], 0)
    shared = {
        "gains": gains, "w_in": f(inp["w_in"])[0], "b_forget": f(inp["b_forget"])[0].reshape(12, 1),
        "w_pool": f(inp["w_pool"])[0], "pool_scale": f(inp["pool_scale"])[0].reshape(256, 1),
        "w_mix_out": f(inp["w_mix_out"])[0], "w_xq": f(inp["w_xq"])[0], "w_xkv": f(inp["w_xkv"])[0],
        "w_xo": f(inp["w_xo"])[0], "w_up": f(inp["w_up"])[0], "conv_w": f(inp["conv_w"])[0].reshape(3, 8192),
        "conv_b": f(inp["conv_b"])[0].reshape(1, 8192), "w_down": f(inp["w_down"])[0],
    }
    bf = ml_dtypes.bfloat16
    in_maps = []
    wins = np.array([2, 4, 8, 16], np.float32)
    for c in range(8):
        b, half = c // 2, c % 2
        xk = np.zeros((NT, D), np.float32)
        if half == 1:
            xk[:2048] = x[b, :2048]
        xk[2048:] = x[b, half * 2048:(half + 1) * 2048]
        kconst = np.zeros((4, NT), np.float32)
        kconst[0:3] = -1.0
        if half == 0:
            kconst[3, :2048] = MASKV
        qconst = np.ones((4, NQ), np.float32)
        qconst[3, :128] = 0.0
        invc = np.zeros((2, 128, 16), np.float32)
        t1 = np.arange(1, 17, dtype=np.float32)
        for g in range(4):
            cc, hf = g // 2, g % 2
            if half == 0:
                invc[cc, hf * 64:(hf + 1) * 64, :] = 1.0 / np.minimum(t1, wins[g])[None, :]
            else:
                invc[cc, hf * 64:(hf + 1) * 64, :] = 1.0 / wins[g]
        flags = np.ones((128, 2), np.float32)
        flags[:, 0] = float(half)
        m = dict(shared)
        m.update({"xk": xk, "mem": mem[b], "kconst": kconst.astype(bf), "qconst": qconst.astype(bf),
                  "invc": invc, "flags": flags})
        in_maps.append(m)
    return in_maps


def kernel(**inputs):
    if "nc" not in _CACHE:
        _CACHE["nc"] = build()
    nc = _CACHE["nc"]
    in_maps = _prep_inputs(inputs)
    res = run_bass_kernel_spmd(nc, in_maps, core_ids=list(range(8)))
    outp = np.zeros((4, 4096, D), np.float32)
    for c in range(8):
        b, half = c // 2, c % 2
        outp[b, half * 2048:(half + 1) * 2048] = res.results[c]["out"]
    return outp
```
